# Optimizing a Trainium2 kernel written in Bass

```python
import jax
import jax.numpy as jnp
from jax import lax
import numpy as np


D_MODEL = 2048
BATCH = 2
SEQ = 4096
DEPTH = 4

N_MIXERS = 2
HEAD_DIM = 128
SB_HEADS = D_MODEL // HEAD_DIM
SB_Q_BLOCK = 128
NSA_HEADS = D_MODEL // HEAD_DIM
NSA_KV_GROUPS = 4
NSA_HEADS_PER_GROUP = NSA_HEADS // NSA_KV_GROUPS
NSA_KV_WIDTH = NSA_KV_GROUPS * HEAD_DIM
NSA_N_BRANCHES = 3
NSA_IN_WIDTH = D_MODEL + 6 * NSA_KV_WIDTH + NSA_N_BRANCHES * NSA_HEADS
CMP_LEN = 32
CMP_STRIDE = 16
CMP_HIDDEN = HEAD_DIM
SEL_LEN = 64
SEL_TOP_N = 16
SEL_Q_BLOCK = 64
WIN = 512
WIN_Q_BLOCK = 128
D_FF = 4 * D_MODEL
N_SB_LAYERS = (DEPTH + 1) // 2
N_NSA_LAYERS = DEPTH // 2
NORM_EPS = 1e-6
NEG_INF = -1e30
FORCE_SCORE = 1e4

kernel_name = 'hybrid_stickbreak_nsa_sqrelu'


def rms_norm(x, g):
    xf = x.astype(jnp.float32)
    y = xf * lax.rsqrt(jnp.mean(xf * xf, axis=-1, keepdims=True) + NORM_EPS)
    return (y * g.astype(jnp.float32)).astype(x.dtype)


def alibi_slopes(n_heads):
    return jnp.exp2(-8.0 * jnp.arange(1, n_heads + 1, dtype=jnp.float32) / n_heads)


def sqrelu_mlp(h, w1, w2):
    return jnp.square(jax.nn.relu(h @ w1)) @ w2


def stick_breaking_attention(h, w_qkv, w_out):
    B, T, _ = h.shape
    q, k, v = jnp.split(h @ w_qkv, 3, axis=-1)
    heads = lambda a: a.reshape(B, T, SB_HEADS, HEAD_DIM).transpose(0, 2, 1, 3)
    q, k, v = heads(q), heads(k), heads(v)
    scale = HEAD_DIM ** -0.5
    outs = []
    for i in range(T // SB_Q_BLOCK):
        t0 = i * SB_Q_BLOCK
        end = t0 + SB_Q_BLOCK
        z = jnp.einsum('bhqd,bhsd->bhqs', q[:, :, t0:end], k[:, :, :end]).astype(jnp.float32) * scale
        causal = jnp.arange(end)[None, :] < jnp.arange(t0, end)[:, None]
        log_one_minus = jnp.where(causal, -jax.nn.softplus(z), 0.0)
        after = lax.cumsum(log_one_minus, axis=3, reverse=True) - log_one_minus
        a = jnp.where(causal, jnp.exp(jax.nn.log_sigmoid(z) + after), 0.0)
        outs.append(jnp.einsum('bhqs,bhsd->bhqd', a.astype(v.dtype), v[:, :, :end]))
    o = jnp.concatenate(outs, axis=2)
    return o.transpose(0, 2, 1, 3).reshape(B, T, D_MODEL) @ w_out


def compress_blocks(x, pe, w1, w2):
    B, G, T, Dh = x.shape
    n_cmp = (T - CMP_LEN) // CMP_STRIDE + 1
    idx = jnp.arange(n_cmp)[:, None] * CMP_STRIDE + jnp.arange(CMP_LEN)[None, :]
    blk = x[:, :, idx] + pe.astype(x.dtype)
    flat = blk.reshape(B, G, n_cmp, CMP_LEN * Dh)
    return jax.nn.gelu(flat @ w1) @ w2


def selected_attention(q, k, v, sel_idx, slopes):
    B, G, Hg, T, Dh = q.shape
    n = sel_idx.shape[-1]
    nqb = T // SEL_Q_BLOCK
    k_blk = k.reshape(B, G, T // SEL_LEN, SEL_LEN, Dh)
    v_blk = v.reshape(B, G, T // SEL_LEN, SEL_LEN, Dh)
    b_i = jnp.arange(B)[:, None, None, None]
    g_i = jnp.arange(G)[None, :, None, None]
    sl = slopes[None, :, :, None, None]
    scale = Dh ** -0.5

    def one_block(args):
        qb, ib, t0 = args
        kg = k_blk[b_i, g_i, ib].reshape(B, G, SEL_Q_BLOCK, n * SEL_LEN, Dh)
        vg = v_blk[b_i, g_i, ib].reshape(B, G, SEL_Q_BLOCK, n * SEL_LEN, Dh)
        t = t0 + jnp.arange(SEL_Q_BLOCK)
        pos = (ib[..., None] * SEL_LEN + jnp.arange(SEL_LEN)).reshape(B, G, SEL_Q_BLOCK, n * SEL_LEN)
        dist = (t[None, None, :, None] - pos)[:, :, None]
        s = jnp.einsum('bghqd,bgqkd->bghqk', qb, kg).astype(jnp.float32) * scale - sl * dist.astype(jnp.float32)
        p = jax.nn.softmax(jnp.where(dist >= 0, s, NEG_INF), axis=-1)
        return jnp.einsum('bghqk,bgqkd->bghqd', p.astype(qb.dtype), vg)

    q_blocks = q.reshape(B, G, Hg, nqb, SEL_Q_BLOCK, Dh).transpose(3, 0, 1, 2, 4, 5)
    i_blocks = sel_idx.reshape(B, G, nqb, SEL_Q_BLOCK, n).transpose(2, 0, 1, 3, 4)
    t0s = jnp.arange(nqb) * SEL_Q_BLOCK
    o = lax.map(one_block, (q_blocks, i_blocks, t0s))
    return o.transpose(1, 2, 3, 0, 4, 5).reshape(B, G, Hg, T, Dh)


def window_attention(q, k, v, slopes):
    B, G, Hg, T, Dh = q.shape
    nb = T // WIN_Q_BLOCK
    span = WIN_Q_BLOCK + WIN
    pad = ((0, 0), (0, 0), (WIN, 0), (0, 0))
    k_pad = jnp.pad(k, pad)
    v_pad = jnp.pad(v, pad)
    sl = slopes[None, :, :, None, None]
    scale = Dh ** -0.5

    def one_block(args):
        qb, t0 = args
        kb = lax.dynamic_slice_in_dim(k_pad, t0, span, axis=2)
        vb = lax.dynamic_slice_in_dim(v_pad, t0, span, axis=2)
        t = t0 + jnp.arange(WIN_Q_BLOCK)
        spos = t0 - WIN + jnp.arange(span)
        dist = t[:, None] - spos[None, :]
        mask = (dist >= 0) & (dist < WIN) & (spos[None, :] >= 0)
        s = jnp.einsum('bghqd,bgkd->bghqk', qb, kb).astype(jnp.float32) * scale - sl * dist.astype(jnp.float32)
        p = jax.nn.softmax(jnp.where(mask, s, NEG_INF), axis=-1)
        return jnp.einsum('bghqk,bgkd->bghqd', p.astype(qb.dtype), vb)

    q_blocks = q.reshape(B, G, Hg, nb, WIN_Q_BLOCK, Dh).transpose(3, 0, 1, 2, 4, 5)
    t0s = jnp.arange(nb) * WIN_Q_BLOCK
    o = lax.map(one_block, (q_blocks, t0s))
    return o.transpose(1, 2, 3, 0, 4, 5).reshape(B, G, Hg, T, Dh)


def native_sparse_attention(h, w_in, gate_b, q_norm_g, k_norm_g, cmp_pe, cmp_w1, cmp_w2, w_out):
    B, T, _ = h.shape
    G, Hg, Dh = NSA_KV_GROUPS, NSA_HEADS_PER_GROUP, HEAD_DIM
    offs = [D_MODEL + j * NSA_KV_WIDTH for j in range(7)]
    q, kc, vc, ks, vs, kw, vw, gl = jnp.split(h @ w_in, offs, axis=-1)
    q = rms_norm(q.reshape(B, T, G, Hg, Dh), q_norm_g).transpose(0, 2, 3, 1, 4)
    kv_heads = lambda a: a.reshape(B, T, G, Dh).transpose(0, 2, 1, 3)
    slopes = alibi_slopes(NSA_HEADS).reshape(G, Hg)
    scale = Dh ** -0.5

    kc_blk = rms_norm(compress_blocks(kv_heads(kc), cmp_pe[0], cmp_w1[0], cmp_w2[0]), k_norm_g[0])
    vc_blk = compress_blocks(kv_heads(vc), cmp_pe[1], cmp_w1[1], cmp_w2[1])
    n_cmp = kc_blk.shape[2]
    t_pos = jnp.arange(T)
    c_start = jnp.arange(n_cmp) * CMP_STRIDE
    dist_c = t_pos[:, None] - (c_start + CMP_LEN - 1)[None, :]
    mask_c = dist_c >= 0
    s_c = jnp.einsum('bghtd,bgnd->bghtn', q, kc_blk).astype(jnp.float32) * scale - slopes[None, :, :, None, None] * dist_c.astype(jnp.float32)
    p_cmp = jax.nn.softmax(jnp.where(mask_c, s_c, NEG_INF), axis=-1) * mask_c
    o_cmp = jnp.einsum('bghtn,bgnd->bghtd', p_cmp.astype(vc_blk.dtype), vc_blk)

    n_sel_blk = T // SEL_LEN
    s_start = jnp.arange(n_sel_blk) * SEL_LEN
    overlap = jnp.clip(jnp.minimum(c_start[:, None] + CMP_LEN, s_start[None, :] + SEL_LEN) - jnp.maximum(c_start[:, None], s_start[None, :]), 0, None).astype(jnp.float32) / CMP_LEN
    imp = jnp.einsum('bgtn,nj->bgtj', p_cmp.sum(axis=2), overlap)
    j = jnp.arange(n_sel_blk)[None, :]
    cur = (t_pos // SEL_LEN)[:, None]
    forced = (j == 0) | (j == cur) | (j == cur - 1)
    imp = jnp.where(j > cur, -1.0, jnp.where(forced, FORCE_SCORE, imp))
    _, sel_idx = lax.top_k(imp, min(SEL_TOP_N, n_sel_blk))

    o_slc = selected_attention(q, rms_norm(kv_heads(ks), k_norm_g[1]), kv_heads(vs), sel_idx, slopes)
    o_win = window_attention(q, rms_norm(kv_heads(kw), k_norm_g[2]), kv_heads(vw), slopes)

    gates = jax.nn.sigmoid((gl + gate_b).astype(jnp.float32)).astype(h.dtype)
    gates = gates.reshape(B, T, NSA_N_BRANCHES, G, Hg).transpose(2, 0, 3, 4, 1)[..., None]
    o = gates[0] * o_cmp + gates[1] * o_slc + gates[2] * o_win
    return o.transpose(0, 3, 1, 2, 4).reshape(B, T, D_MODEL) @ w_out


def setup_inputs(seed: int = 0) -> dict:
    key = jax.random.key(seed)
    ks = jax.random.split(key, 16)
    nrm = lambda k, shape: jax.random.normal(k, shape, jnp.float32)
    dense = lambda k, shape, fan_in: nrm(k, shape) * (fan_in ** -0.5)
    gain = lambda k, shape: 1.0 + 0.02 * nrm(k, shape)
    return {
        'x': nrm(ks[0], (BATCH, SEQ, D_MODEL)),
        'sb_norm_g': gain(ks[1], (N_SB_LAYERS, D_MODEL)),
        'sb_w_qkv': dense(ks[2], (N_SB_LAYERS, D_MODEL, 3 * D_MODEL), D_MODEL),
        'sb_w_out': dense(ks[3], (N_SB_LAYERS, D_MODEL, D_MODEL), D_MODEL),
        'nsa_norm_g': gain(ks[4], (N_NSA_LAYERS, D_MODEL)),
        'nsa_w_in': dense(ks[5], (N_NSA_LAYERS, D_MODEL, NSA_IN_WIDTH), D_MODEL),
        'nsa_gate_b': 0.02 * nrm(ks[6], (N_NSA_LAYERS, NSA_N_BRANCHES * NSA_HEADS)),
        'nsa_q_norm_g': gain(ks[7], (N_NSA_LAYERS, HEAD_DIM)),
        'nsa_k_norm_g': gain(ks[8], (N_NSA_LAYERS, NSA_N_BRANCHES, HEAD_DIM)),
        'nsa_cmp_pe': 0.1 * nrm(ks[9], (N_NSA_LAYERS, 2, CMP_LEN, HEAD_DIM)),
        'nsa_cmp_w1': dense(ks[10], (N_NSA_LAYERS, 2, CMP_LEN * HEAD_DIM, CMP_HIDDEN), CMP_LEN * HEAD_DIM),
        'nsa_cmp_w2': dense(ks[11], (N_NSA_LAYERS, 2, CMP_HIDDEN, HEAD_DIM), CMP_HIDDEN),
        'nsa_w_out': dense(ks[12], (N_NSA_LAYERS, D_MODEL, D_MODEL), D_MODEL),
        'mlp_norm_g': gain(ks[13], (DEPTH, D_MODEL)),
        'mlp_w1': dense(ks[14], (DEPTH, D_MODEL, D_FF), D_MODEL),
        'mlp_w2': dense(ks[15], (DEPTH, D_FF, D_MODEL), D_FF),
    }


def reference(x, sb_norm_g, sb_w_qkv, sb_w_out, nsa_norm_g, nsa_w_in, nsa_gate_b, nsa_q_norm_g, nsa_k_norm_g, nsa_cmp_pe, nsa_cmp_w1, nsa_cmp_w2, nsa_w_out, mlp_norm_g, mlp_w1, mlp_w2):
    h = x
    for layer in range(DEPTH):
        slot = layer // N_MIXERS
        if layer % N_MIXERS == 0:
            h = h + stick_breaking_attention(rms_norm(h, sb_norm_g[slot]), sb_w_qkv[slot], sb_w_out[slot])
        else:
            h = h + native_sparse_attention(rms_norm(h, nsa_norm_g[slot]), nsa_w_in[slot], nsa_gate_b[slot], nsa_q_norm_g[slot], nsa_k_norm_g[slot], nsa_cmp_pe[slot], nsa_cmp_w1[slot], nsa_cmp_w2[slot], nsa_w_out[slot])
        h = h + sqrelu_mlp(rms_norm(h, mlp_norm_g[layer]), mlp_w1[layer], mlp_w2[layer])
    return h
```

```python
import numpy as np
from contextlib import ExitStack
import concourse.bass as bass
import concourse.mybir as mybir
from concourse.bass_utils import run_bass_kernel_spmd

F32 = mybir.dt.float32
BF16 = mybir.dt.bfloat16
ALU = mybir.AluOpType
AF = mybir.ActivationFunctionType
AX = mybir.AxisListType

D_MODEL = 2048
SEQ = 4096
BATCH = 2
NCORES = 8
TOK = 1024
NQB = 8
NCH = 16
D_FF = 8192
HEADS = 16
DH = 128
EPS = 1e-6
SCALE = DH ** -0.5
NEG = -30000.0

SEM_CAP = 30000


def core_blocks(j):
    return [j, j + 4, j + 8, j + 12, 19 - j, 23 - j, 27 - j, 31 - j]


class Reg:
    __slots__ = ("w", "rs", "name")

    def __init__(self, name=""):
        self.w = None
        self.rs = {}
        self.name = name


class Eng:
    def __init__(self, prog, name, kind):
        self.prog = prog
        self.name = name
        self.kind = kind
        self.items = []
        self.sem = None
        self.count = 0
        self.waited = {}
        self.slots = []
        self.slot_i = 0

    def new_sem(self):
        self.sem = self.prog.new_sem(self.name)
        self.count = 0


class Prog:
    NSLOT = 6

    def __init__(self, nc):
        self.nc = nc
        self.stack = ExitStack()
        self.nsem = 0
        self.engs = {}
        for name in ("pe", "act", "dve", "pool", "sp"):
            e = Eng(self, name, name)
            self.engs[name] = e
            e.new_sem()
        for name in ("act", "pool", "sp"):
            e = self.engs[name]
            e.slots = [[self.new_sem(f"{name}_dma{i}"), 0] for i in range(self.NSLOT)]
        self.nops = 0

    def new_sem(self, name):
        self.nsem += 1
        return self.stack.enter_context(self.nc.semaphore(f"s{self.nsem}_{name}"))

    def sbuf(self, name, shape, dtype, stack=None):
        st = stack if stack is not None else self.stack
        self.uid = getattr(self, "uid", 0) + 1
        return st.enter_context(self.nc.sbuf_tensor(f"{name}_u{self.uid}", list(shape), dtype))

    def psum(self, name, shape, dtype, stack=None):
        st = stack if stack is not None else self.stack
        return st.enter_context(self.nc.psum_tensor(name, list(shape), dtype))

    def _deps(self, eng, reads, writes):
        toks = []
        for r in reads:
            if r.w is not None:
                toks.append(r.w)
        for r in writes:
            if r.w is not None:
                toks.append(r.w)
            toks.extend(r.rs.values())
        return toks

    def _emit_waits(self, eng, toks):
        for tok in toks:
            sem, val, src = tok
            if src is eng and eng.kind == "pe":
                continue
            key = id(sem)
            if eng.waited.get(key, 0) >= val:
                continue
            eng.waited[key] = val
            eng.items.append(("wait", sem, val))

    def _finish(self, tok, reads, writes):
        for r in reads:
            k = id(tok[0])
            old = r.rs.get(k)
            if old is None or old[1] < tok[1]:
                r.rs[k] = tok
        for r in writes:
            r.w = tok
            r.rs = {}

    def op(self, engname, fn, reads=(), writes=(), signal=True):
        eng = self.engs[engname]
        self.nops += 1
        self._emit_waits(eng, self._deps(eng, reads, writes))
        if eng.count >= SEM_CAP:
            eng.new_sem()
        if signal:
            eng.count += 1
            tok = (eng.sem, eng.count, eng)
            eng.items.append(("op", fn, eng.sem, 1))
        else:
            tok = (eng.sem, eng.count + 1, eng)
            eng.items.append(("op", fn, None, 0))
        self._finish(tok, reads, writes)
        return tok

    def dma(self, qname, fn, reads=(), writes=()):
        eng = self.engs[qname]
        self.nops += 1
        slot = eng.slots[eng.slot_i]
        eng.slot_i = (eng.slot_i + 1) % len(eng.slots)
        if slot[1] >= SEM_CAP:
            slot[0] = self.new_sem(f"{qname}_dma")
            slot[1] = 0
        toks = self._deps(eng, reads, writes)
        if slot[1] > 0:
            toks.append((slot[0], slot[1], None))
        self._emit_waits(eng, toks)
        slot[1] += 16
        tok = (slot[0], slot[1], None)
        eng.items.append(("op", fn, slot[0], 16))
        self._finish(tok, reads, writes)
        return tok

    def barrier(self):
        toks = []
        for e in self.engs.values():
            if e.count > 0:
                toks.append((e.sem, e.count, None))
            for s in e.slots:
                if s[1] > 0:
                    toks.append((s[0], s[1], None))
        for e in self.engs.values():
            self._emit_waits(e, toks)

    def final_wait(self, toks, engname="sp"):
        self._emit_waits(self.engs[engname], toks)

    def flush(self):
        nc = self.nc

        def run(items, h):
            for it in items:
                if it[0] == "wait":
                    h.wait_ge(it[1], it[2])
                else:
                    ins = it[1](h)
                    if it[2] is not None:
                        ins.then_inc(it[2], it[3])

        with nc.Block() as block:
            @block.tensor
            def _(h):
                run(self.engs["pe"].items, h)

            @block.scalar
            def _(h):
                run(self.engs["act"].items, h)

            @block.vector
            def _(h):
                run(self.engs["dve"].items, h)

            @block.gpsimd
            def _(h):
                run(self.engs["pool"].items, h)

            @block.sync
            def _(h):
                run(self.engs["sp"].items, h)
        for e in self.engs.values():
            e.items = []

    def close(self):
        self.stack.close()


class Core:
    def __init__(self, nc):
        self.nc = nc
        self.P = Prog(nc)
        P = self.P
        self.xT = P.sbuf("xT_sb", [128, NCH, TOK], F32)
        self.r_xT = [Reg(f"xT{c}") for c in range(NCH)]
        self.ones = P.sbuf("ones_bf", [128, 128], BF16)
        self.r_ones = Reg("ones")
        self.ident = P.sbuf("ident_bf", [128, 128], BF16)
        self.r_ident = Reg("ident")
        self.wbuf = [P.sbuf(f"wbuf{i}", [128, NCH, 512], BF16) for i in range(2)]
        self.r_wbuf = [[Reg(f"wbuf{i}_{k}") for k in range(2)] for i in range(2)]
        self.wi = 0
        self.banks = [P.psum(f"bank{i}", [128, 512], F32) for i in range(7)]
        self.r_banks = [Reg(f"bank{i}") for i in range(7)]
        self.tbank = P.psum("tbank", [128, 1024], BF16)
        self.r_tbank = Reg("tbank")
        self.dram = {}

    def din(self, name, shape, dtype=F32):
        t = self.nc.dram_tensor(name, list(shape), dtype, kind="ExternalInput")
        self.dram[name] = t
        return t

    def dout(self, name, shape, dtype=F32):
        t = self.nc.dram_tensor(name, list(shape), dtype, kind="ExternalOutput")
        self.dram[name] = t
        return t

    def dscratch(self, name, shape, dtype=F32):
        t = self.nc.dram_tensor(name, list(shape), dtype, kind="Internal")
        self.dram[name] = t
        return t

    def init_consts(self, ident_dram):
        P = self.P
        ones = self.ones
        P.op("pool", lambda h: h.memset(ones[:], 1.0), writes=[self.r_ones])
        ident = self.ident
        P.dma("pool", lambda h: h.dma_start(out=ident[:], in_=ident_dram.ap()), writes=[self.r_ident])

    def load_x(self, x_dram):
        P = self.P
        xT = self.xT
        src = x_dram.rearrange("(c p) t -> p c t", p=128)
        for c0 in range(0, NCH, 4):
            P.dma("sp", lambda h, c0=c0: h.dma_start(out=xT[:, c0:c0 + 4, :], in_=src[:, c0:c0 + 4, :]),
                  writes=self.r_xT[c0:c0 + 4])

    def store_x(self, y_dram):
        P = self.P
        xT = self.xT
        dst = y_dram.rearrange("(c p) t -> p c t", p=128)
        toks = []
        for c0 in range(0, NCH, 4):
            toks.append(P.dma("sp", lambda h, c0=c0: h.dma_start(out=dst[:, c0:c0 + 4, :], in_=xT[:, c0:c0 + 4, :]),
                              reads=self.r_xT[c0:c0 + 4]))
        return toks

    def next_wbuf(self):
        i = self.wi
        self.wi ^= 1
        return self.wbuf[i], self.r_wbuf[i]

    def load_w(self, src_ap, ncols=512, nch=NCH):
        P = self.P
        wb, rw = self.next_wbuf()
        src = src_ap.rearrange("(c p) f -> p c f", p=128)
        hc = max(nch // 2, 1)
        for c0 in range(0, nch, hc):
            P.dma("pool", lambda h, c0=c0: h.dma_start(out=wb[:, c0:c0 + hc, 0:ncols], in_=src[:, c0:c0 + hc, :]),
                  writes=[rw[c0 // 8]] if nch == NCH else rw)
        return wb, rw

    def rmsnorm(self, g_sb, r_g, xn, r_xn, st):
        P = self.P
        xT, ones = self.xT, self.ones
        sq = [P.sbuf(f"rn_sq{i}", [128, 512], BF16, st) for i in range(2)]
        r_sq = [Reg() for _ in range(2)]
        lnv = P.sbuf("rn_ln", [128, 512], F32, st)
        r_ln = Reg()
        rstd = [P.sbuf(f"rn_rstd{i}", [128, 512], F32, st) for i in range(2)]
        r_rstd = [Reg() for _ in range(2)]
        for th in range(2):
            ts = slice(th * 512, (th + 1) * 512)
            bank, rb = self.banks[th], self.r_banks[th]
            for c in range(NCH):
                s, rs = sq[c % 2], r_sq[c % 2]
                P.op("act", lambda h, s=s, c=c, ts=ts: h.activation(out=s[:], in_=xT[:, c, ts], func=AF.Square),
                     reads=[self.r_xT[c]], writes=[rs])
                P.op("pe", lambda h, s=s, c=c, bank=bank: h.matmul(bank[:], ones[:], s[:], start=(c == 0), stop=(c == NCH - 1)),
                     reads=[rs, self.r_ones], writes=[rb])
            P.op("act", lambda h, bank=bank: h.activation(out=lnv[:], in_=bank[:], func=AF.Ln, bias=EPS, scale=1.0 / D_MODEL),
                 reads=[rb], writes=[r_ln])
            rs_t, r_rs = rstd[th], r_rstd[th]
            P.op("act", lambda h, rs_t=rs_t: h.activation(out=rs_t[:], in_=lnv[:], func=AF.Exp, scale=-0.5),
                 reads=[r_ln], writes=[r_rs])
            for c in range(NCH):
                P.op("dve", lambda h, c=c, ts=ts, rs_t=rs_t: h.scalar_tensor_tensor(
                    out=xn[:, c, ts], in0=xT[:, c, ts], scalar=g_sb[:, c:c + 1], in1=rs_t[:], op0=ALU.mult, op1=ALU.mult),
                    reads=[self.r_xT[c], r_rs, r_g], writes=[r_xn[c]])

    def mlp(self, g_dram_row, w1_ap, w2_ap):
        P = self.P
        with ExitStack() as st:
            g_sb = P.sbuf("mlp_g", [128, NCH], F32, st)
            r_g = Reg()
            P.dma("sp", lambda h: h.dma_start(out=g_sb[:], in_=g_dram_row),
                  writes=[r_g])
            xn = P.sbuf("mlp_xn", [128, NCH, TOK], BF16, st)
            r_xn = [Reg() for _ in range(NCH)]
            self.rmsnorm(g_sb, r_g, xn, r_xn, st)
            h1 = P.sbuf("mlp_h1", [128, 16, TOK], BF16, st)
            r_h1 = [Reg() for _ in range(16)]
            rl = [P.sbuf(f"mlp_r{i}", [128, 512], F32, st) for i in range(2)]
            r_rl = [Reg() for _ in range(2)]
            xT = self.xT
            nb = 0
            nr = 0
            for fq in range(4):
                for fg in range(4):
                    f0 = fq * 2048 + fg * 512
                    wb, rw = self.load_w(w1_ap[:, f0:f0 + 512])
                    for fc in range(4):
                        for th in range(2):
                            ts = slice(th * 512, (th + 1) * 512)
                            bank, rb = self.banks[2 + nb % 4], self.r_banks[2 + nb % 4]
                            nb += 1
                            for c in range(NCH):
                                P.op("pe", lambda h, wb=wb, c=c, fc=fc, ts=ts, bank=bank: h.matmul(
                                    bank[:], wb[:, c, fc * 128:(fc + 1) * 128], xn[:, c, ts], start=(c == 0), stop=(c == NCH - 1)),
                                    reads=[rw[c // 8], r_xn[c]], writes=[rb], signal=(c == NCH - 1))
                            r_t, r_r = rl[nr % 2], r_rl[nr % 2]
                            nr += 1
                            P.op("act", lambda h, r_t=r_t, bank=bank: h.activation(out=r_t[:], in_=bank[:], func=AF.Relu),
                                 reads=[rb], writes=[r_r])
                            hc = fg * 4 + fc
                            P.op("dve", lambda h, r_t=r_t, hc=hc, ts=ts: h.tensor_tensor(
                                out=h1[:, hc, ts], in0=r_t[:], in1=r_t[:], op=ALU.mult),
                                reads=[r_r], writes=[r_h1[hc]])
                for dg in range(4):
                    wb, rw = self.load_w(w2_ap[fq * 2048:(fq + 1) * 2048, dg * 512:(dg + 1) * 512])
                    for dc in range(4):
                        for th in range(2):
                            ts = slice(th * 512, (th + 1) * 512)
                            bank, rb = self.banks[2 + nb % 4], self.r_banks[2 + nb % 4]
                            nb += 1
                            for c in range(16):
                                P.op("pe", lambda h, wb=wb, c=c, dc=dc, ts=ts, bank=bank: h.matmul(
                                    bank[:], wb[:, c, dc * 128:(dc + 1) * 128], h1[:, c, ts], start=(c == 0), stop=(c == 15)),
                                    reads=[rw[c // 8], r_h1[c]], writes=[rb], signal=(c == 15))
                            xc = dg * 4 + dc
                            P.op("dve", lambda h, xc=xc, ts=ts, bank=bank: h.tensor_tensor(
                                out=xT[:, xc, ts], in0=xT[:, xc, ts], in1=bank[:], op=ALU.add),
                                reads=[rb], writes=[self.r_xT[xc]])
            P.barrier()
            P.flush()

    def load_small(self, name, dram_ap, shape, dtype, st, q="sp"):
        P = self.P
        t = P.sbuf("s_" + name, shape, dtype, st)
        r = Reg(name)
        P.dma(q, lambda h: h.dma_start(out=t[:], in_=dram_ap), writes=[r])
        return t, r

    def proj_fm(self, w_ap, xn, r_xn, ncols, evac):
        P = self.P
        nb = 0
        for g0 in range(0, ncols, 512):
            wb, rw = self.load_w(w_ap[:, g0:g0 + 512])
            for jc in range(4):
                for th in range(2):
                    ts = slice(th * 512, (th + 1) * 512)
                    bank, rb = self.banks[nb % 3], self.r_banks[nb % 3]
                    nb += 1
                    for c in range(NCH):
                        P.op("pe", lambda h, wb=wb, c=c, jc=jc, ts=ts, bank=bank: h.matmul(
                            bank[:], wb[:, c, jc * 128:(jc + 1) * 128], xn[:, c, ts], start=(c == 0), stop=(c == NCH - 1)),
                            reads=[rw[c // 8], r_xn[c]], writes=[rb], signal=(c == NCH - 1))
                    evac(g0 // 128 + jc, th, bank, rb)

    def proj_tm(self, w_ap, xn, r_xn, ncols, evac):
        P = self.P
        nb = 0
        for g0 in range(0, ncols, 512):
            wb, rw = self.load_w(w_ap[:, g0:g0 + 512])
            for tt in range(NQB):
                bank, rb = self.banks[nb % 3], self.r_banks[nb % 3]
                nb += 1
                for c in range(NCH):
                    P.op("pe", lambda h, wb=wb, c=c, tt=tt, bank=bank: h.matmul(
                        bank[:], xn[:, c, tt * 128:(tt + 1) * 128], wb[:, c, :], start=(c == 0), stop=(c == NCH - 1)),
                        reads=[rw[c // 8], r_xn[c]], writes=[rb], signal=(c == NCH - 1))
                evac(g0 // 512, tt, bank, rb)

    def out_proj(self, w_ap, oT, r_oT):
        P = self.P
        xT = self.xT
        nb = 0
        for dg in range(4):
            wb, rw = self.load_w(w_ap[:, dg * 512:(dg + 1) * 512])
            for dc in range(4):
                for th in range(2):
                    ts = slice(th * 512, (th + 1) * 512)
                    bank, rb = self.banks[nb % 3], self.r_banks[nb % 3]
                    nb += 1
                    for c in range(16):
                        P.op("pe", lambda h, wb=wb, c=c, dc=dc, ts=ts, bank=bank: h.matmul(
                            bank[:], wb[:, c, dc * 128:(dc + 1) * 128], oT[:, c, ts], start=(c == 0), stop=(c == 15)),
                            reads=[rw[c // 8], r_oT[c]], writes=[rb], signal=(c == 15))
                    xc = dg * 4 + dc
                    P.op("dve", lambda h, xc=xc, ts=ts, bank=bank: h.tensor_tensor(
                        out=xT[:, xc, ts], in0=xT[:, xc, ts], in1=bank[:], op=ALU.add),
                        reads=[rb], writes=[self.r_xT[xc]])

    def sb_proj(self, g_ap, wqkv_ap, qT_out, kT_out, v_out):
        P = self.P
        with ExitStack() as st:
            g_sb, r_g = self.load_small("sb_g", g_ap, [128, NCH], F32, st)
            xn = P.sbuf("sb_xn", [128, NCH, TOK], BF16, st)
            r_xn = [Reg() for _ in range(NCH)]
            self.rmsnorm(g_sb, r_g, xn, r_xn, st)
            stg = [P.sbuf(f"sb_stg{i}", [128, 512], BF16, st) for i in range(4)]
            r_stg = [Reg() for _ in range(4)]
            cnt = [0]
            outs = []

            def evac_fm(dst, scale):
                def f(j, th, bank, rb):
                    i = cnt[0] % 4
                    cnt[0] += 1
                    s, rs = stg[i], r_stg[i]
                    P.op("act", lambda h: h.activation(out=s[:], in_=bank[:], func=AF.Copy, scale=scale),
                         reads=[rb], writes=[rs])
                    outs.append(P.dma("sp", lambda h: h.dma_start(out=dst.ap()[j, :, th * 512:(th + 1) * 512], in_=s[:]),
                                      reads=[rs]))
                return f

            def evac_tm(g, tt, bank, rb):
                i = cnt[0] % 4
                cnt[0] += 1
                s, rs = stg[i], r_stg[i]
                P.op("dve", lambda h: h.tensor_copy(out=s[:], in_=bank[:]), reads=[rb], writes=[rs])
                outs.append(P.dma("sp", lambda h: h.dma_start(
                    out=v_out.ap()[tt * 128:(tt + 1) * 128, g * 512:(g + 1) * 512], in_=s[:]), reads=[rs]))

            self.proj_fm(wqkv_ap[:, 0:2048], xn, r_xn, 2048, evac_fm(qT_out, SCALE))
            self.proj_fm(wqkv_ap[:, 2048:4096], xn, r_xn, 2048, evac_fm(kT_out, 1.0))
            self.proj_tm(wqkv_ap[:, 4096:6144], xn, r_xn, 2048, evac_tm)
            P.barrier()
            P.flush()
        return outs

    def sb_attn(self, ck, qT_in, KT_all, V_all, consts, wout_ap):
        P = self.P
        with ExitStack() as st:
            negtri, r_negtri = self.load_small("negtri", consts["negtri"].ap(), [128, 128], BF16, st)
            negones, r_negones = self.load_small("negones", consts["negones"].ap(), [128, 128], BF16, st)
            m01, r_m01 = self.load_small("m01", consts["m01"].ap(), [128, 128], BF16, st)
            nm, r_nm = self.load_small("nm", consts["nm"].ap(), [128, 128], BF16, st)
            nk = 8 * ck + 8
            ident, r_ident = self.ident, self.r_ident
            qq = [P.sbuf(f"sb_q{i}", [128, TOK], BF16, st) for i in range(2)]
            r_qq = [Reg() for _ in range(2)]
            oT = P.sbuf("sb_oT", [128, HEADS, TOK], BF16, st)
            r_oT = [Reg() for _ in range(HEADS)]
            kt = [P.sbuf(f"sb_kt{i}", [128, SEQ], BF16, st) for i in range(2)]
            r_kt = [Reg() for _ in range(2)]
            vv = [P.sbuf(f"sb_v{i}", [128, 32, 128], BF16, st) for i in range(2)]
            r_vv = [Reg() for _ in range(2)]
            E = [P.sbuf(f"sb_E{i}", [128, 512], F32, st) for i in range(1)]
            r_E = [Reg() for _ in range(1)]
            SP = [P.sbuf(f"sb_SP{i}", [128, 512], BF16, st) for i in range(3)]
            r_SP = [Reg() for _ in range(3)]
            PP = [P.sbuf(f"sb_P{i}", [128, 512], BF16, st) for i in range(3)]
            r_PP = [Reg() for _ in range(3)]
            LS = [P.sbuf(f"sb_LS{i}", [128, 512], BF16, st) for i in range(2)]
            r_LS = [Reg() for _ in range(2)]
            E0, r_E0 = E[0], r_E[0]

            def load_head(hd):
                ktb, r_ktb = kt[hd % 2], r_kt[hd % 2]
                vb, r_vb = vv[hd % 2], r_vv[hd % 2]
                qb, r_qb = qq[hd % 2], r_qq[hd % 2]
                P.dma("sp", lambda h: h.dma_start(out=ktb[:, 0:nk * 128], in_=KT_all.ap()[hd][:, 0:nk * 128]), writes=[r_ktb])
                P.dma("sp", lambda h: h.dma_start(out=qb[:], in_=qT_in.ap()[hd]), writes=[r_qb])
                hk = nk // 2
                for half in range(2):
                    P.dma("sp", lambda h, half=half: h.dma_start(
                        out=vb[:, half * hk:(half + 1) * hk, :],
                        in_=V_all.ap()[half * hk * 128:(half + 1) * hk * 128, hd * 128:(hd + 1) * 128].rearrange("(i k) d -> k i d", k=128)),
                        writes=[r_vb])

            def mk_step(hd, grp, i, b0, n, first_av, last, obank, r_ob, ls, r_ls, first_of_head):
                ktb, r_ktb = kt[hd % 2], r_kt[hd % 2]
                vb, r_vb = vv[hd % 2], r_vv[hd % 2]
                qb, r_qb = qq[hd % 2], r_qq[hd % 2]
                p0 = grp * 4
                smin = max(0, i - b0)
                c_lo = smin * 128
                cols = slice(c_lo, 512)
                gcols = slice(p0 * 128 + c_lo, p0 * 128 + 512)
                has_top = i >= b0
                tcols = slice(c_lo, c_lo + 128)
                cc_lo = c_lo + (128 if has_top else 0)
                ccols = slice(cc_lo, 512)
                abank, r_ab = self.banks[n % 3], self.r_banks[n % 3]
                sp, r_sp = SP[n % 3], r_SP[n % 3]
                pp, r_pp = PP[n % 3], r_PP[n % 3]

                def A():
                    if first_of_head == 0 and hd == 0:
                        load_head(0)
                    if first_of_head == 3 and hd + 1 < HEADS:
                        load_head(hd + 1)
                    P.op("pe", lambda h: h.matmul(abank[:, cols], ktb[:, i * 128:(i + 1) * 128], qb[:, gcols], start=True, stop=True),
                         reads=[r_ktb, r_qb], writes=[r_ab])
                    P.op("act", lambda h: h.activation(out=E0[:, cols], in_=abank[:, cols], func=AF.Exp), reads=[r_ab], writes=[r_E0])
                    P.op("act", lambda h: h.activation(out=sp[:, cols], in_=E0[:, cols], func=AF.Ln, bias=1.0), reads=[r_E0], writes=[r_sp])
                    if has_top:
                        P.op("pool", lambda h: h.tensor_tensor(out=sp[:, tcols], in0=sp[:, tcols], in1=m01[:], op=ALU.mult),
                             reads=[r_m01], writes=[r_sp])

                def B():
                    P.op("pe", lambda h: h.matmul(abank[:, cols], negtri[:], sp[:, cols], start=False, stop=True, skip_group_check=True),
                         reads=[r_sp, r_negtri], writes=[r_ab])
                    if cc_lo < 512:
                        P.op("pe", lambda h: h.matmul(abank[:, ccols], negones[:], ls[:, ccols], start=False, stop=True, skip_group_check=True),
                             reads=[r_ls, r_negones], writes=[r_ab])
                    if has_top:
                        P.op("pe", lambda h: h.matmul(abank[:, tcols], ident[:], nm[:], start=False, stop=True, skip_group_check=True),
                             reads=[r_nm, r_ident], writes=[r_ab])
                    P.op("act", lambda h: h.activation(out=pp[:, cols], in_=abank[:, cols], func=AF.Exp), reads=[r_ab], writes=[r_pp])
                    if has_top:
                        P.op("dve", lambda h: h.tensor_copy(out=ls[:, tcols], in_=sp[:, tcols]), reads=[r_sp], writes=[r_ls])
                    if cc_lo < 512 and i > 0:
                        P.op("dve", lambda h: h.tensor_tensor(out=ls[:, ccols], in0=ls[:, ccols], in1=sp[:, ccols], op=ALU.add),
                             reads=[r_sp], writes=[r_ls])

                def C():
                    P.op("pe", lambda h: h.matmul(obank[:, cols], vb[:, i, :], pp[:, cols], start=first_av, stop=True, skip_group_check=True),
                         reads=[r_vb, r_pp], writes=[r_ob])
                    if last:
                        P.op("dve", lambda h: h.tensor_copy(out=oT[:, hd, p0 * 128:p0 * 128 + 512], in_=obank[:]),
                             reads=[r_ob], writes=[r_oT[hd]])
                return A, B, C

            steps = []
            nO = 0
            for hd in range(HEADS):
                kh = 0
                for grp in range(2):
                    b0 = 8 * ck + 4 * grp
                    obank, r_ob = self.banks[3 + nO % 2], self.r_banks[3 + nO % 2]
                    ls, r_ls = LS[nO % 2], r_LS[nO % 2]
                    nO += 1
                    for i in range(b0 + 3, -1, -1):
                        steps.append(mk_step(hd, grp, i, b0, len(steps), i == b0 + 3, i == 0, obank, r_ob, ls, r_ls, kh))
                        kh += 1
            for n in range(len(steps) + 2):
                if n < len(steps):
                    steps[n][0]()
                if 0 <= n - 1 < len(steps):
                    steps[n - 1][1]()
                if 0 <= n - 2 < len(steps):
                    steps[n - 2][2]()
            self.out_proj(wout_ap, oT, r_oT)
            P.barrier()
            P.flush()

def _nsa_proj(self, g_ap, win_ap, qg_ap, kg_ap, gb_ap, outs):
    P = self.P
    ones = self.ones
    with ExitStack() as st:
        g_sb, r_g = self.load_small("na_g", g_ap, [128, NCH], F32, st)
        qg, r_qg = self.load_small("na_qg", qg_ap, [128, 1], F32, st)
        kg, r_kg = self.load_small("na_kg", kg_ap, [128, 3], F32, st)
        gb, r_gb = self.load_small("na_gb", gb_ap, [48, 1], F32, st)
        qgs = P.sbuf("na_qgs", [128, 1], F32, st)
        r_qgs = Reg()
        P.op("dve", lambda h: h.tensor_scalar(out=qgs[:], in0=qg[:], scalar1=SCALE, scalar2=None, op0=ALU.mult),
             reads=[r_qg], writes=[r_qgs])
        ngb = P.sbuf("na_ngb", [48, 1], F32, st)
        r_ngb = Reg()
        P.op("dve", lambda h: h.tensor_scalar(out=ngb[:], in0=gb[:], scalar1=-1.0, scalar2=None, op0=ALU.mult),
             reads=[r_gb], writes=[r_ngb])
        xn = P.sbuf("na_xn", [128, NCH, TOK], BF16, st)
        r_xn = [Reg() for _ in range(NCH)]
        self.rmsnorm(g_sb, r_g, xn, r_xn, st)
        stg = [P.sbuf(f"na_stg{i}", [128, 512], BF16, st) for i in range(4)]
        r_stg = [Reg() for _ in range(4)]
        sqb = [P.sbuf(f"na_sq{i}", [128, 512], BF16, st) for i in range(2)]
        r_sqb = [Reg() for _ in range(2)]
        lnv = P.sbuf("na_ln", [128, 512], F32, st)
        r_ln = Reg()
        rstd = [P.sbuf(f"na_rstd{i}", [128, 512], F32, st) for i in range(2)]
        r_rstd = [Reg() for _ in range(2)]
        cnt = [0]
        cn = [0]

        def evac_copy(dst):
            def f(j, th, bank, rb):
                i = cnt[0] % 4
                cnt[0] += 1
                s, rs = stg[i], r_stg[i]
                P.op("act", lambda h: h.activation(out=s[:], in_=bank[:], func=AF.Copy), reads=[rb], writes=[rs])
                P.dma("sp", lambda h: h.dma_start(out=dst.ap()[j, :, th * 512:(th + 1) * 512], in_=s[:]), reads=[rs])
            return f

        def evac_norm(dst, gvec, r_gvec, extra_scale):
            lnscale = float(np.log(extra_scale))

            def f(j, th, bank, rb):
                i = cnt[0] % 4
                cnt[0] += 1
                s, rs = stg[i], r_stg[i]
                k = cn[0] % 2
                cn[0] += 1
                sq, r_sq = sqb[k], r_sqb[k]
                rs_t, r_rs = rstd[k], r_rstd[k]
                sbank, r_sb = self.banks[3 + k], self.r_banks[3 + k]
                P.op("act", lambda h: h.activation(out=sq[:], in_=bank[:], func=AF.Square), reads=[rb], writes=[r_sq])
                P.op("pe", lambda h: h.matmul(sbank[:], ones[:], sq[:], start=True, stop=True),
                     reads=[r_sq, self.r_ones], writes=[r_sb])
                P.op("act", lambda h: h.activation(out=lnv[:], in_=sbank[:], func=AF.Ln, bias=EPS, scale=1.0 / DH),
                     reads=[r_sb], writes=[r_ln])
                P.op("act", lambda h: h.activation(out=rs_t[:], in_=lnv[:], func=AF.Exp, scale=-0.5),
                     reads=[r_ln], writes=[r_rs])
                P.op("dve", lambda h: h.scalar_tensor_tensor(out=s[:], in0=bank[:], scalar=gvec, in1=rs_t[:],
                                                             op0=ALU.mult, op1=ALU.mult),
                     reads=[rb, r_rs, r_gvec], writes=[rs])
                P.dma("sp", lambda h: h.dma_start(out=dst.ap()[j, :, th * 512:(th + 1) * 512], in_=s[:]), reads=[rs])
            return f

        def evac_tm(dst):
            def f(g, tt, bank, rb):
                i = cnt[0] % 4
                cnt[0] += 1
                s, rs = stg[i], r_stg[i]
                P.op("dve", lambda h: h.tensor_copy(out=s[:], in_=bank[:]), reads=[rb], writes=[rs])
                P.dma("sp", lambda h: h.dma_start(out=dst.ap()[tt * 128:(tt + 1) * 128, :], in_=s[:]), reads=[rs])
            return f

        parts = _DBG_PARTS
        if parts is None or "q" in parts:
            self.proj_fm(win_ap[:, 0:2048], xn, r_xn, 2048, evac_norm(outs["qn"], qgs[:, 0:1], r_qgs, 1.0))
        if parts is None or "kc" in parts:
            self.proj_fm(win_ap[:, 2048:2560], xn, r_xn, 512, evac_copy(outs["kc"]))
            self.proj_fm(win_ap[:, 2560:3072], xn, r_xn, 512, evac_copy(outs["vc"]))
        if parts is None or "ks" in parts:
            self.proj_fm(win_ap[:, 3072:3584], xn, r_xn, 512, evac_norm(outs["ks"], kg[:, 1:2], r_kg, 1.0))
            self.proj_fm(win_ap[:, 4096:4608], xn, r_xn, 512, evac_norm(outs["kw"], kg[:, 2:3], r_kg, 1.0))
        if parts is None or "vs" in parts:
            self.proj_tm(win_ap[:, 3584:4096], xn, r_xn, 512, evac_tm(outs["vs"]))
            self.proj_tm(win_ap[:, 4608:5120], xn, r_xn, 512, evac_tm(outs["vw"]))
        if parts is not None and "gates" not in parts:
            P.barrier()
            P.flush()
            return
        wb, rw = self.load_w(win_ap[:, 5120:5168], ncols=48)
        ge = P.sbuf("na_ge", [48, 512], F32, st)
        r_ge = Reg()
        for th in range(2):
            ts = slice(th * 512, (th + 1) * 512)
            bank, rb = self.banks[5 + th], self.r_banks[5 + th]
            for c in range(NCH):
                P.op("pe", lambda h, c=c, ts=ts, bank=bank: h.matmul(
                    bank[0:48, :], wb[:, c, 0:48], xn[:, c, ts], start=(c == 0), stop=(c == NCH - 1)),
                    reads=[rw[c // 8], r_xn[c]], writes=[rb], signal=(c == NCH - 1))
            P.op("act", lambda h, bank=bank: h.activation(out=ge[:], in_=bank[0:48, :], func=AF.Exp, scale=-1.0, bias=ngb[:, 0:1]),
                 reads=[rb, r_ngb], writes=[r_ge])
            P.op("dve", lambda h: h.tensor_scalar(out=ge[:], in0=ge[:], scalar1=1.0, scalar2=None, op0=ALU.add),
                 reads=[r_ge], writes=[r_ge])
            P.op("dve", lambda h: h.reciprocal(out=ge[:], in_=ge[:]), reads=[r_ge], writes=[r_ge])
            P.dma("sp", lambda h, ts=ts: h.dma_start(out=outs["gates"].ap()[:, ts], in_=ge[:]), reads=[r_ge])
        P.barrier()
        P.flush()


Core.nsa_proj = _nsa_proj


def _nsa_compress(self, D):
    P = self.P
    ones, r_ones = self.ones, self.r_ones
    banks, r_banks = self.banks, self.r_banks
    if not hasattr(self, "kcn"):
        self.kcn = P.sbuf("nb_kcn", [128, 4, 256], BF16)
        self.r_kcn = Reg()
        self.vcb = P.sbuf("nb_vcb", [128, 2, 4, 128], BF16)
        self.r_vcb = Reg()
    kcn, r_kcn, vcb, r_vcb = self.kcn, self.r_kcn, self.vcb, self.r_vcb
    with ExitStack() as st:
        kg0, r_kg0 = self.load_small("nb_kg0", D["kg0"].ap(), [128, 1], F32, st)
        P.op("pool", lambda h: h.memset(kcn[:], 0.0), writes=[r_kcn])
        P.op("pool", lambda h: h.memset(vcb[:], 0.0), writes=[r_vcb])
        with ExitStack() as st2:
            xc = P.sbuf("nb_xc", [128, 4, 256, 16], BF16, st2)
            r_xc = Reg()
            w1c = P.sbuf("nb_w1c", [128, 32, 128], BF16, st2)
            r_w1c = Reg()
            w2c = P.sbuf("nb_w2c", [128, 128], BF16, st2)
            r_w2c = Reg()
            peT = P.sbuf("nb_peT", [128, 32], BF16, st2)
            r_peT = Reg()
            bias_sb = P.sbuf("nb_cb", [128, 1], F32, st2)
            r_bias = Reg()
            fa = [P.sbuf(f"nb_f{i}", [128, 256], F32, st2) for i in range(4)]
            r_fa = [Reg() for _ in range(4)]
            H2 = P.sbuf("nb_H2", [128, 256], BF16, st2)
            r_H2 = Reg()
            sq = P.sbuf("nb_csq", [128, 256], BF16, st2)
            r_sq = Reg()
            P.op("pool", lambda h: h.memset(H2[:], 0.0), writes=[r_H2])
            for kv in range(2):
                src = D["KC"] if kv == 0 else D["VC"]
                for g in range(4):
                    P.dma("sp", lambda h, g=g, src=src: h.dma_start(
                        out=xc[:, g, :, :], in_=src.ap()[g].rearrange("d (n r) -> d n r", r=16)), writes=[r_xc])
                P.dma("pool", lambda h, kv=kv: h.dma_start(
                    out=w1c[:], in_=D["cw1"].ap()[kv].rearrange("(l d) h -> d l h", d=128)), writes=[r_w1c])
                P.dma("pool", lambda h, kv=kv: h.dma_start(out=w2c[:], in_=D["cw2"].ap()[kv]), writes=[r_w2c])
                P.dma("pool", lambda h, kv=kv: h.dma_start(out=peT[:], in_=D["peT"].ap()[kv]), writes=[r_peT])
                bb, r_bb = banks[3], r_banks[3]
                for l in range(32):
                    P.op("pe", lambda h, l=l: h.matmul(bb[:, 0:1], w1c[:, l, :], peT[:, l:l + 1], start=(l == 0), stop=(l == 31)),
                         reads=[r_w1c, r_peT], writes=[r_bb], signal=(l == 31))
                P.op("dve", lambda h: h.tensor_copy(out=bias_sb[:], in_=bb[:, 0:1]), reads=[r_bb], writes=[r_bias])
                for g in range(4):
                    hb, r_hb = banks[5 + g % 2], r_banks[5 + g % 2]
                    for l in range(32):
                        n0, rr = (0, l) if l < 16 else (1, l - 16)
                        P.op("pe", lambda h, l=l, g=g, n0=n0, rr=rr, hb=hb: h.matmul(
                            hb[:, 0:255], w1c[:, l, :], xc[:, g, n0:n0 + 255, rr], start=(l == 0), stop=(l == 31)),
                            reads=[r_w1c, r_xc], writes=[r_hb], signal=(l == 31))
                    a, a2, u, th = fa
                    r_a, r_a2, r_u, r_th = r_fa
                    P.op("act", lambda h, hb=hb: h.activation(out=a[:, 0:255], in_=hb[:, 0:255], func=AF.Identity, bias=bias_sb[:, 0:1]),
                         reads=[r_hb, r_bias], writes=[r_a])
                    P.op("dve", lambda h: h.tensor_tensor(out=a2[:, 0:255], in0=a[:, 0:255], in1=a[:, 0:255], op=ALU.mult),
                         reads=[r_a], writes=[r_a2])
                    P.op("dve", lambda h: h.tensor_scalar(out=a2[:, 0:255], in0=a2[:, 0:255], scalar1=0.044715, scalar2=1.0,
                                                          op0=ALU.mult, op1=ALU.add), reads=[r_a2], writes=[r_a2])
                    P.op("dve", lambda h: h.tensor_tensor(out=u[:, 0:255], in0=a2[:, 0:255], in1=a[:, 0:255], op=ALU.mult),
                         reads=[r_a2, r_a], writes=[r_u])
                    P.op("act", lambda h: h.activation(out=th[:, 0:255], in_=u[:, 0:255], func=AF.Tanh, scale=0.7978845608028654),
                         reads=[r_u], writes=[r_th])
                    P.op("dve", lambda h: h.scalar_tensor_tensor(out=H2[:, 0:255], in0=th[:, 0:255], scalar=1.0, in1=a[:, 0:255],
                                                                 op0=ALU.add, op1=ALU.mult), reads=[r_th, r_a], writes=[r_H2])
                    if kv == 0:
                        kb, r_kb = banks[3], r_banks[3]
                        sb_, r_sb = banks[4], r_banks[4]
                        P.op("pe", lambda h: h.matmul(kb[:, 0:255], w2c[:], H2[:, 0:255], start=True, stop=True),
                             reads=[r_w2c, r_H2], writes=[r_kb])
                        P.op("act", lambda h: h.activation(out=a2[:, 0:255], in_=kb[:, 0:255], func=AF.Copy, scale=0.5),
                             reads=[r_kb], writes=[r_a2])
                        P.op("act", lambda h: h.activation(out=sq[:, 0:255], in_=kb[:, 0:255], func=AF.Square, scale=0.5),
                             reads=[r_kb], writes=[r_sq])
                        P.op("pe", lambda h: h.matmul(sb_[:, 0:255], ones[:], sq[:, 0:255], start=True, stop=True),
                             reads=[r_sq, r_ones], writes=[r_sb])
                        P.op("act", lambda h: h.activation(out=u[:, 0:255], in_=sb_[:, 0:255], func=AF.Ln, bias=EPS, scale=1.0 / DH),
                             reads=[r_sb], writes=[r_u])
                        P.op("act", lambda h: h.activation(out=th[:, 0:255], in_=u[:, 0:255], func=AF.Exp, scale=-0.5),
                             reads=[r_u], writes=[r_th])
                        P.op("dve", lambda h, g=g: h.scalar_tensor_tensor(out=kcn[:, g, 0:255], in0=a2[:, 0:255], scalar=kg0[:, 0:1],
                                                                          in1=th[:, 0:255], op0=ALU.mult, op1=ALU.mult),
                             reads=[r_a2, r_th, r_kg0], writes=[r_kcn])
                    else:
                        for nc_ in range(2):
                            M = 128 if nc_ == 0 else 127
                            vbk, r_vbk = banks[3 + nc_], r_banks[3 + nc_]
                            P.op("pe", lambda h, nc_=nc_, M=M, vbk=vbk: h.matmul(
                                vbk[0:M, 0:128], H2[:, nc_ * 128:nc_ * 128 + M], w2c[:], start=True, stop=True),
                                reads=[r_w2c, r_H2], writes=[r_vbk])
                            P.op("act", lambda h, nc_=nc_, M=M, g=g, vbk=vbk: h.activation(
                                out=vcb[0:M, nc_, g, :], in_=vbk[0:M, 0:128], func=AF.Copy, scale=0.5),
                                reads=[r_vbk], writes=[r_vcb])
            P.barrier()
            P.flush()


Core.nsa_compress = _nsa_compress


def _nsa_attn(self, ck, D, wout_ap):
    P = self.P
    ones, ident = self.ones, self.ident
    r_ones, r_ident = self.r_ones, self.r_ident
    banks, r_banks = self.banks, self.r_banks
    xT = self.xT
    with ExitStack() as st:
        L = lambda n, ap, sh, dt: self.load_small("nb_" + n, ap, sh, dt, st)
        aL, r_aL = L("aL", D["alibiL"].ap(), [6, HEADS, 128], BF16)
        aR, r_aR = L("aR", D["alibiR"].ap()[ck], [6, TOK], BF16)
        bKI, r_bKI = L("bKI", D["biasKI"].ap(), [128, HEADS, 32], F32)
        bC, r_bC = L("bC", D["biasC"].ap(), [128, HEADS, 2], F32)
        nmc, r_nmc = L("nmc", D["negmask_c"].ap()[ck], [128, 2, TOK], BF16)
        ovl, r_ovl = L("ovl", D["overlap"].ap(), [128, 2, 64], BF16)
        keep, r_keep = L("keep", D["keep01"].ap()[ck], [128, NQB, 64], F32)
        addc, r_addc = L("addc", D["addc"].ap()[ck], [128, NQB, 64], F32)
        lcomb = [P.sbuf(f"nb_lcomb{i}", [70, 32, 128], BF16, st) for i in range(1)]
        r_lcomb = [Reg() for _ in range(1)]
        for i_ in range(1):
            P.dma("sp", lambda h, i_=i_: h.dma_start(out=lcomb[i_][0:64], in_=D["eexp"].ap()), writes=[r_lcomb[i_]])
        nmi, r_nmi = L("nmi", D["nm_incl"].ap(), [128, 128], BF16)
        nma, r_nma = L("nma", D["nm_after"].ap(), [128, 128], BF16)
        nk = 8 * ck + 8
        sel3, r_sel3 = L("sel3", D["sel3"].ap(), [3, 3, 128], F32)
        gat = P.sbuf("nb_gat", [3, TOK], F32, st)
        r_gat = Reg()
        gates_v = D["gates"].ap().rearrange("(r h) t -> r h t", r=3)

        def load_gate(h_):
            P.dma("sp", lambda h: h.dma_start(out=gat[:], in_=gates_v[:, h_, :]), writes=[r_gat])

        kcn, r_kcn, vcb, r_vcb = self.kcn, self.r_kcn, self.vcb, self.r_vcb

        ksb = P.sbuf("nb_ksb", [128, SEQ], BF16, st)
        r_ksb = Reg()
        vsb = P.sbuf("nb_vsb", [128, 32, 128], BF16, st)
        r_vsb = Reg()
        qg4 = P.sbuf("nb_q4", [128, 4, TOK], BF16, st)
        r_qg4 = Reg()
        acc = P.sbuf("nb_acc", [128, 4, TOK], F32, st)
        r_acc = [Reg() for _ in range(4)]
        ob16 = [P.sbuf(f"nb_ob{i}", [128, TOK], BF16, st) for i in range(4)]
        r_ob16 = [Reg() for _ in range(4)]
        Pc = [P.sbuf(f"nb_Pc{i}", [128, 512], BF16, st) for i in range(4)]
        r_Pc = [Reg() for _ in range(4)]
        Pn = [P.sbuf(f"nb_Pn{i}", [128, 512], BF16, st) for i in range(2)]
        r_Pn = [Reg() for _ in range(2)]
        PS = [P.sbuf(f"nb_PS{i}", [128, 512], BF16, st) for i in range(4)]
        r_PS = [Reg() for _ in range(4)]
        rden = P.sbuf("nb_rden", [128, 512], F32, st)
        r_rden = Reg()
        Gs = P.sbuf("nb_Gs", [128, 512], F32, st)
        r_Gs = Reg()
        Osb = [P.sbuf(f"nb_osb{i}", [128, 512], F32, st) for i in range(2)]
        r_Osb = [Reg() for _ in range(2)]
        impf = P.sbuf("nb_impf", [128, NQB, 64], F32, st)
        r_impf = Reg()
        wrk = P.sbuf("nb_wrk", [128, 64], F32, st)
        r_wrk = Reg()
        m8 = P.sbuf("nb_m8", [128, 8], F32, st)
        r_m8 = Reg()
        s01 = P.sbuf("nb_s01", [128, 64], F32, st)
        r_s01 = Reg()
        nsel = P.sbuf("nb_nsel", [128, 64], BF16, st)
        r_nsel = Reg()
        nselT = P.sbuf("nb_nselT", [70, TOK], BF16, st)
        r_nselT = Reg()
        P.dma("sp", lambda h: h.dma_start(out=nselT[64:70, :], in_=D["alibiR"].ap()[ck]), writes=[r_nselT])
        tpsum = self.tbank[0:64, 0:128]
        r_tps = self.r_tbank
        nA = [0]
        nPS = [0]

        def next_A():
            k = nA[0] % 3
            nA[0] += 1
            return banks[k], r_banks[k]

        nEp = [0]

        def gate_epilogue(hl, h_, branch, cols, obank, r_ob, dbank, r_db, first_branch):
            gb, r_gb = banks[5], r_banks[5]
            osb, r_osb = Osb[nEp[0] % 2], r_Osb[nEp[0] % 2]
            nEp[0] += 1
            P.op("act", lambda h: h.activation(out=osb[:], in_=obank[:], func=AF.Copy), reads=[r_ob], writes=[r_osb])
            if dbank is not None:
                P.op("dve", lambda h: h.tensor_scalar(out=rden[:], in0=dbank[:], scalar1=1e-30, scalar2=None, op0=ALU.max),
                     reads=[r_db], writes=[r_rden])
            P.op("pe", lambda h: h.matmul(gb[:], sel3[:, branch, :], gat[:, cols], start=True, stop=True),
                 reads=[r_sel3, r_gat], writes=[r_gb])
            P.op("act", lambda h: h.activation(out=Gs[:], in_=gb[:], func=AF.Copy), reads=[r_gb], writes=[r_Gs])
            if dbank is not None:
                P.op("act", lambda h: h.activation(out=rden[:], in_=rden[:], func=AF.Ln), reads=[r_rden], writes=[r_rden])
                P.op("act", lambda h: h.activation(out=rden[:], in_=rden[:], func=AF.Exp, scale=-1.0), reads=[r_rden], writes=[r_rden])
                P.op("dve", lambda h: h.tensor_tensor(out=rden[:], in0=rden[:], in1=Gs[:], op=ALU.mult),
                     reads=[r_rden, r_Gs], writes=[r_rden])
                mul = rden
                r_mul = r_rden
            else:
                mul = Gs
                r_mul = r_Gs
            if first_branch:
                P.op("dve", lambda h: h.tensor_tensor(out=acc[:, hl, cols], in0=osb[:], in1=mul[:], op=ALU.mult),
                     reads=[r_osb, r_mul], writes=[r_acc[hl]])
            else:
                P.op("dve", lambda h: h.tensor_tensor(out=osb[:], in0=osb[:], in1=mul[:], op=ALU.mult),
                     reads=[r_osb, r_mul], writes=[r_osb])
                P.op("dve", lambda h: h.tensor_tensor(out=acc[:, hl, cols], in0=acc[:, hl, cols], in1=osb[:], op=ALU.add),
                     reads=[r_osb], writes=[r_acc[hl]])

        db, r_db = banks[3], r_banks[3]
        ob, r_ob = banks[4], r_banks[4]

        def mk_step(branch, hl, h_, grp, p0, b0, i, first, last, n):
            smin = max(0, i - b0)
            if branch == 1:
                smax = 3
                masks = [(smin, nmi, r_nmi)] if i >= b0 else []
            else:
                smax = min(3, i - b0 + 4)
                masks = []
                for s_ in range(smin, smax + 1):
                    rr_ = b0 + s_ - i
                    if rr_ == 0:
                        masks.append((s_, nmi, r_nmi))
                    elif rr_ == 4:
                        masks.append((s_, nma, r_nma))
            c_lo = smin * 128
            c_hi = (smax + 1) * 128
            cols = slice(c_lo, c_hi)
            gcols = slice(p0 * 128 + c_lo, p0 * 128 + c_hi)
            gcols_all = slice(p0 * 128, p0 * 128 + 512)
            ABK = (0, 1, 2, 6)
            ab, r_ab = banks[ABK[n % 4]], r_banks[ABK[n % 4]]
            ps, r_ps = PS[n % 4], r_PS[n % 4]
            lc, r_lc = lcomb[0], r_lcomb[0]

            def A():
                if branch == 1 and first and grp == 0:
                    P.dma("sp", lambda h: h.dma_start(out=lc[64:70], in_=D["alibiLx"].ap()[h_]), writes=[r_lc])
                P.op("pe", lambda h: h.matmul(ab[:, cols], ksb[:, i * 128:(i + 1) * 128], qg4[:, hl, gcols], start=True, stop=True),
                     reads=[r_ksb, r_qg4], writes=[r_ab])
                if branch == 1:
                    P.op("pe", lambda h: h.matmul(ab[:, cols], lc[:, i, :], nselT[:, gcols], start=False, stop=True, skip_group_check=True),
                         reads=[r_lc, r_nselT], writes=[r_ab])
                else:
                    P.op("pe", lambda h: h.matmul(ab[:, cols], aL[:, h_, :], aR[:, gcols], start=False, stop=True, skip_group_check=True),
                         reads=[r_aL, r_aR], writes=[r_ab])
                for (s_, mt, r_mt) in masks:
                    tcols = slice(s_ * 128, (s_ + 1) * 128)
                    P.op("pe", lambda h, tcols=tcols, mt=mt: h.matmul(ab[:, tcols], ident[:], mt[:], start=False, stop=True, skip_group_check=True),
                         reads=[r_mt, r_ident], writes=[r_ab])
                P.op("act", lambda h: h.activation(out=ps[:, cols], in_=ab[:, cols], func=AF.Exp, bias=bKI[:, h_, i:i + 1]),
                     reads=[r_ab, r_bKI], writes=[r_ps])

            def B():
                P.op("pe", lambda h: h.matmul(db[:, cols], ones[:], ps[:, cols], start=first, stop=True, skip_group_check=True),
                     reads=[r_ps, r_ones], writes=[r_db])
                P.op("pe", lambda h: h.matmul(ob[:, cols], vsb[:, i, :], ps[:, cols], start=first, stop=True, skip_group_check=True),
                     reads=[r_ps, r_vsb], writes=[r_ob])
                if last:
                    if grp == 0:
                        load_gate(h_)
                    gate_epilogue(hl, h_, branch, gcols_all, ob, r_ob, db, r_db, False)
            return A, B

        for g in range(4):
            hk = nk // 2
            P.dma("sp", lambda h, g=g: h.dma_start(out=ksb[:, 0:nk * 128], in_=D["KS"].ap()[g][:, 0:nk * 128]), writes=[r_ksb])
            for half in range(2):
                P.dma("sp", lambda h, g=g, half=half: h.dma_start(
                    out=vsb[:, half * hk:(half + 1) * hk, :],
                    in_=D["VS"].ap()[half * hk * 128:(half + 1) * hk * 128, g * 128:(g + 1) * 128].rearrange("(i k) d -> k i d", k=128)),
                    writes=[r_vsb])
            P.dma("sp", lambda h, g=g: h.dma_start(out=qg4[:], in_=D["qn"].ap()[4 * g:4 * g + 4].rearrange("h d t -> d h t")),
                  writes=[r_qg4])
            ib, r_ib = banks[6], r_banks[6]

            def mk_cstep(hl, h_, qg, n, g=g):
                cols = slice(qg * 512, (qg + 1) * 512)
                cdb, r_cdb = banks[2 + n % 2], r_banks[2 + n % 2]
                cob, r_cob = banks[4], r_banks[4]
                pcs = [(Pc[2 * (n % 2) + k_], r_Pc[2 * (n % 2) + k_]) for k_ in range(2)]

                def S1():
                    for nc_ in range(2):
                        M = 128 if nc_ == 0 else 127
                        ab, r_ab = banks[nc_], r_banks[nc_]
                        pc, r_pc = pcs[nc_]
                        P.op("pe", lambda h, ab=ab, nc_=nc_, M=M: h.matmul(
                            ab[0:M, :], kcn[:, g, nc_ * 128:nc_ * 128 + M], qg4[:, hl, cols], start=True, stop=True),
                            reads=[r_kcn, r_qg4], writes=[r_ab])
                        P.op("pe", lambda h, ab=ab, M=M: h.matmul(
                            ab[0:M, :], aL[:, h_, 0:M], aR[:, cols], start=False, stop=True, skip_group_check=True),
                            reads=[r_aL, r_aR], writes=[r_ab])
                        P.op("pe", lambda h, ab=ab, M=M, nc_=nc_: h.matmul(
                            ab[0:M, :], ident[0:M, 0:M], nmc[0:M, nc_, cols], start=False, stop=True, skip_group_check=True),
                            reads=[r_nmc, r_ident], writes=[r_ab])
                        P.op("act", lambda h, ab=ab, M=M, pc=pc, nc_=nc_: h.activation(
                            out=pc[0:M, :], in_=ab[0:M, :], func=AF.Exp, bias=bC[0:M, h_, nc_:nc_ + 1]),
                            reads=[r_ab, r_bC], writes=[r_pc])
                        P.op("pe", lambda h, M=M, pc=pc, nc_=nc_: h.matmul(
                            cdb[:], ones[0:M, :], pc[0:M, :], start=(nc_ == 0), stop=(nc_ == 1)),
                            reads=[r_pc, r_ones], writes=[r_cdb])

                def S2():
                    if qg == 0:
                        load_gate(h_)
                    P.op("dve", lambda h: h.tensor_scalar(out=rden[:], in0=cdb[:], scalar1=1e-30, scalar2=None, op0=ALU.max),
                         reads=[r_cdb], writes=[r_rden])
                    P.op("dve", lambda h: h.reciprocal(out=rden[:], in_=rden[:]), reads=[r_rden], writes=[r_rden])
                    for nc_ in range(2):
                        M = 128 if nc_ == 0 else 127
                        pc, r_pc = pcs[nc_]
                        P.op("dve", lambda h, nc_=nc_, M=M, pc=pc: h.tensor_tensor(
                            out=Pn[nc_][0:M, :], in0=pc[0:M, :], in1=rden[0:M, :], op=ALU.mult),
                            reads=[r_pc, r_rden], writes=[r_Pn[nc_]])
                    for nc_ in range(2):
                        M = 128 if nc_ == 0 else 127
                        P.op("pe", lambda h, nc_=nc_, M=M: h.matmul(
                            cob[:], vcb[0:M, nc_, g, :], Pn[nc_][0:M, :], start=(nc_ == 0), stop=(nc_ == 1)),
                            reads=[r_vcb, r_Pn[nc_]], writes=[r_cob])
                    for qb in range(4):
                        for nc_ in range(2):
                            M = 128 if nc_ == 0 else 127
                            fi = (n == 0 and qb == 0 and nc_ == 0)
                            P.op("pe", lambda h, qb=qb, nc_=nc_, M=M, fi=fi: h.matmul(
                                ib[:, (qg * 4 + qb) * 64:(qg * 4 + qb + 1) * 64], Pn[nc_][0:M, qb * 128:(qb + 1) * 128],
                                ovl[0:M, nc_, :], start=fi, stop=True, skip_group_check=True),
                                reads=[r_Pn[nc_], r_ovl], writes=[r_ib])
                    gate_epilogue(hl, h_, 0, cols, cob, r_cob, None, None, True)
                return S1, S2

            csteps = []
            for hl in range(4):
                for qg in range(2):
                    csteps.append(mk_cstep(hl, 4 * g + hl, qg, len(csteps)))
            csteps[0][0]()
            for n in range(len(csteps)):
                if n + 1 < len(csteps):
                    csteps[n + 1][0]()
                csteps[n][1]()
            ibv = ib[:].rearrange("p (q j) -> p q j", j=64)
            P.op("dve", lambda h: h.tensor_tensor(out=impf[:], in0=ibv, in1=keep[:], op=ALU.mult),
                 reads=[r_ib, r_keep], writes=[r_impf])
            P.op("dve", lambda h: h.tensor_tensor(out=impf[:], in0=impf[:], in1=addc[:], op=ALU.add),
                 reads=[r_impf, r_addc], writes=[r_impf])
            for qbi in range(NQB):
                P.op("dve", lambda h, qbi=qbi: h.max(out=m8[:], in_=impf[:, qbi, :]), reads=[r_impf], writes=[r_m8])
                P.op("dve", lambda h, qbi=qbi: h.match_replace(out=wrk[:], in_to_replace=m8[:], in_values=impf[:, qbi, :],
                                                               imm_value=-1e30), reads=[r_m8, r_impf], writes=[r_wrk])
                P.op("dve", lambda h: h.max(out=m8[:], in_=wrk[:]), reads=[r_wrk], writes=[r_m8])
                P.op("dve", lambda h: h.match_replace(out=s01[:], in_to_replace=m8[:], in_values=wrk[:], imm_value=-1e30),
                     reads=[r_m8, r_wrk], writes=[r_s01])
                P.op("dve", lambda h: h.tensor_scalar(out=nsel[:], in0=s01[:], scalar1=-1e29, scalar2=NEG,
                                                      op0=ALU.is_gt, op1=ALU.mult), reads=[r_s01], writes=[r_nsel])
                P.op("pe", lambda h: h.transpose(tpsum, nsel[:], ident[:]), reads=[r_nsel, r_ident], writes=[r_tps])
                P.op("act", lambda h, qbi=qbi: h.activation(out=nselT[0:64, qbi * 128:(qbi + 1) * 128], in_=tpsum, func=AF.Copy),
                     reads=[r_tps], writes=[r_nselT])
            for branch in (1, 2):
                if branch == 2:
                    P.dma("sp", lambda h, g=g: h.dma_start(out=ksb[:, 0:nk * 128], in_=D["KW"].ap()[g][:, 0:nk * 128]), writes=[r_ksb])
                    for half in range(2):
                        P.dma("sp", lambda h, g=g, half=half: h.dma_start(
                            out=vsb[:, half * hk:(half + 1) * hk, :],
                            in_=D["VW"].ap()[half * hk * 128:(half + 1) * hk * 128, g * 128:(g + 1) * 128].rearrange("(i k) d -> k i d", k=128)),
                            writes=[r_vsb])
                steps = []
                for hl in range(4):
                    h_ = 4 * g + hl
                    for grp in range(2):
                        p0 = grp * 4
                        b0 = 8 * ck + 4 * grp
                        itop = b0 + 3
                        ilow = 0 if branch == 1 else max(0, b0 - 4)
                        for i in range(itop, ilow - 1, -1):
                            steps.append(mk_step(branch, hl, h_, grp, p0, b0, i, i == itop, i == ilow, len(steps)))
                for n in range(len(steps) + 2):
                    if n < len(steps):
                        steps[n][0]()
                    if 0 <= n - 2 < len(steps):
                        steps[n - 2][1]()
            for hl in range(4):
                P.op("act", lambda h, hl=hl: h.activation(out=ob16[hl][:], in_=acc[:, hl, :], func=AF.Copy),
                     reads=[r_acc[hl]], writes=[r_ob16[hl]])
            nb = 0
            for dg in range(4):
                wb, rw = self.load_w(wout_ap[g * 512:(g + 1) * 512, dg * 512:(dg + 1) * 512], nch=4)
                for dc in range(4):
                    for th in range(2):
                        ts = slice(th * 512, (th + 1) * 512)
                        bank, rb = banks[5 + nb % 2], r_banks[5 + nb % 2]
                        nb += 1
                        for c in range(4):
                            P.op("pe", lambda h, wb=wb, c=c, dc=dc, ts=ts, bank=bank: h.matmul(
                                bank[:], wb[:, c, dc * 128:(dc + 1) * 128], ob16[c][:, ts], start=(c == 0), stop=(c == 3)),
                                reads=rw + [r_ob16[c]], writes=[rb], signal=(c == 3))
                        xc_ = dg * 4 + dc
                        P.op("dve", lambda h, xc_=xc_, ts=ts, bank=bank: h.tensor_tensor(
                            out=xT[:, xc_, ts], in0=xT[:, xc_, ts], in1=bank[:], op=ALU.add),
                            reads=[rb], writes=[self.r_xT[xc_]])
        P.barrier()
        P.flush()


Core.nsa_attn = _nsa_attn


import ml_dtypes
BF = ml_dtypes.bfloat16
NCK = SEQ // TOK


class View:
    def __init__(self, ap):
        self._ap = ap

    def ap(self):
        return self._ap


def _pc(v):
    return np.ascontiguousarray(np.asarray(v, np.float32).reshape(NCH, 128).T)


def _split3(v):
    v = np.float32(v)
    hi = np.float32(BF(v))
    mid = np.float32(BF(np.float32(v - hi)))
    lo = np.float32(BF(np.float32(v - hi - mid)))
    return hi, mid, lo


def _consts():
    k = np.arange(128)
    kk, tt = k[:, None], k[None, :]
    tri = (kk < tt).astype(np.float32)
    out = {"ident": np.eye(128, dtype=np.float32), "negtri": -(kk >= tt).astype(np.float32),
           "negones": np.full((128, 128), -1.0, np.float32), "m01": tri, "nm": (tri - 1.0) * 30000.0,
           "nm_incl": np.where(kk <= tt, 0.0, NEG), "nm_after": np.where(kk > tt, 0.0, NEG)}
    out = {n: v.astype(BF) for n, v in out.items()}
    slopes = np.exp2(-8.0 * np.arange(1, HEADS + 1, dtype=np.float32) / HEADS).astype(np.float32)
    tok = np.arange(SEQ)
    aR = np.zeros((6, SEQ), np.float32)
    aR[0:3] = -128.0 * (tok // 128)
    aR[3:6] = -1.0 * (tok % 128)
    aL = np.zeros((6, HEADS, 128), np.float32)
    for h in range(HEADS):
        sp = _split3(slopes[h])
        for r in range(6):
            aL[r, h, :] = sp[r % 3]
    s64 = slopes.astype(np.float64)
    bKI = (s64[None, :, None] * (128.0 * np.arange(32)[None, None, :] + k[:, None, None])).astype(np.float32)
    bC = (s64[None, :, None] * (16.0 * (128.0 * np.arange(2)[None, None, :] + k[:, None, None]) + 31.0)).astype(np.float32)
    n = (128 * np.arange(2)[None, :] + k[:, None])
    valid_c = (n[:, :, None] <= 254) & (tok[None, None, :] >= 16 * n[:, :, None] + 31)
    nmc = np.where(valid_c, 0.0, NEG).astype(np.float32)
    nn = np.arange(256)
    jj = np.arange(64)
    ov = np.clip(np.minimum(16 * nn[:, None] + 32, 64 * jj[None, :] + 64) - np.maximum(16 * nn[:, None], 64 * jj[None, :]), 0, None) / 32.0
    ov[255] = 0.0
    ovl = np.ascontiguousarray(ov.reshape(2, 128, 64).transpose(1, 0, 2)).astype(np.float32)
    tq = tok.reshape(NCK, NQB, 128).transpose(0, 2, 1)
    cur = tq // 64
    J = jj[None, None, None, :]
    fut = J > cur[..., None]
    forced = ((J == 0) | (J == cur[..., None]) | (J == cur[..., None] - 1)) & ~fut
    keep = (~(forced | fut)).astype(np.float32)
    addc = np.where(fut, -1.0 - 1e-3 * J, np.where(forced, 1e4 + (64.0 - J), 0.0)).astype(np.float32)
    eexp = ((128 * np.arange(32)[None, :, None] + k[None, None, :]) // 64 == jj[:, None, None]).astype(np.float32)
    sel3 = np.zeros((3, 3, 128), np.float32)
    for r in range(3):
        sel3[r, r, :] = 1.0
    aLx = np.ascontiguousarray(np.broadcast_to(aL.transpose(1, 0, 2)[:, :, None, :], (HEADS, 6, 32, 128)))
    out.update({"alibiL": aL.astype(BF), "alibiLx": aLx.astype(BF),
                "alibiR": np.ascontiguousarray(aR.reshape(6, NCK, TOK).transpose(1, 0, 2)).astype(BF),
                "biasKI": bKI, "biasC": bC,
                "negmask_c": np.ascontiguousarray(nmc.reshape(128, 2, NCK, TOK).transpose(2, 0, 1, 3)).astype(BF),
                "overlap": ovl.astype(BF), "keep01": np.ascontiguousarray(keep), "addc": np.ascontiguousarray(addc),
                "eexp": eexp.astype(BF), "sel3": sel3})
    return out


_CONST_SPECS = (("ident", [128, 128], BF16), ("negtri", [128, 128], BF16), ("negones", [128, 128], BF16),
                ("m01", [128, 128], BF16), ("nm", [128, 128], BF16), ("nm_incl", [128, 128], BF16),
                ("nm_after", [128, 128], BF16), ("alibiL", [6, HEADS, 128], BF16), ("alibiLx", [HEADS, 6, 32, 128], BF16), ("alibiR", [NCK, 6, TOK], BF16),
                ("biasKI", [128, HEADS, 32], F32), ("biasC", [128, HEADS, 2], F32),
                ("negmask_c", [NCK, 128, 2, TOK], BF16), ("overlap", [128, 2, 64], BF16),
                ("keep01", [NCK, 128, NQB, 64], F32), ("addc", [NCK, 128, NQB, 64], F32),
                ("eexp", [64, 32, 128], BF16), ("sel3", [3, 3, 128], F32))
_W_SPECS = (("sb_g", [2, 128, NCH]), ("sb_wqkv", [2, D_MODEL, 3 * D_MODEL]), ("sb_wout", [2, D_MODEL, D_MODEL]),
            ("nsa_g", [2, 128, NCH]), ("nsa_win", [2, D_MODEL, 5168]), ("nsa_qg", [2, 128, 1]), ("nsa_kg", [2, 128, 3]),
            ("nsa_gb", [2, 48, 1]), ("cw1", [2, 2, 4096, 128]), ("cw2", [2, 2, 128, 128]), ("peT", [2, 2, 128, 32]),
            ("kg0", [2, 128, 1]), ("nsa_wout", [2, D_MODEL, D_MODEL]), ("mlp_g", [4, 128, NCH]),
            ("mlp_w1", [4, D_MODEL, D_FF]), ("mlp_w2", [4, D_FF, D_MODEL]))

_DEBUG_LAYERS = None
_DBG_PARTS = None


def _build_fused(C, depth):
    P = C.P
    xin = C.din("xin", [NCK, D_MODEL, TOK])
    K = {n: C.din(n, sh, dt) for n, sh, dt in _CONST_SPECS}
    W = {n: C.din(n, sh, F32) for n, sh in _W_SPECS}
    yout = C.dout("yout", [NCK, D_MODEL, TOK])
    xs = C.dscratch("xs", [NCK, D_MODEL, TOK])
    qs = C.dscratch("qs", [NCK, HEADS, 128, TOK], BF16)
    KT = C.dscratch("KT", [HEADS, 128, SEQ], BF16)
    Vv = C.dscratch("Vv", [SEQ, D_MODEL], BF16)
    kv = {n: C.dscratch(n, [4, 128, SEQ], BF16) for n in ("KC", "VC", "KS", "KW")}
    kv.update({n: C.dscratch(n, [SEQ, 512], BF16) for n in ("VS", "VW")})
    gts = C.dscratch("gts", [NCK, 48, TOK], F32)
    C.init_consts(K["ident"])
    for layer in range(depth):
        slot = layer // 2
        src = xin if layer == 0 else xs
        dst = yout if layer == depth - 1 else xs
        if layer % 2 == 0:
            for ck in range(NCK):
                tsl = slice(ck * TOK, (ck + 1) * TOK)
                C.load_x(src.ap()[ck])
                C.sb_proj(W["sb_g"].ap()[slot], W["sb_wqkv"].ap()[slot], View(qs.ap()[ck]), View(KT.ap()[:, :, tsl]),
                          View(Vv.ap()[tsl, :]))
            for ck in range(NCK):
                C.load_x(src.ap()[ck])
                C.sb_attn(ck, View(qs.ap()[ck]), KT, Vv, K, W["sb_wout"].ap()[slot])
                C.mlp(W["mlp_g"].ap()[layer], W["mlp_w1"].ap()[layer], W["mlp_w2"].ap()[layer])
                C.store_x(dst.ap()[ck])
                P.barrier()
                P.flush()
        else:
            for ck in range(NCK):
                tsl = slice(ck * TOK, (ck + 1) * TOK)
                C.load_x(src.ap()[ck])
                outs = {"qn": View(qs.ap()[ck]), "gates": View(gts.ap()[ck]),
                        "kc": View(kv["KC"].ap()[:, :, tsl]), "vc": View(kv["VC"].ap()[:, :, tsl]),
                        "ks": View(kv["KS"].ap()[:, :, tsl]), "kw": View(kv["KW"].ap()[:, :, tsl]),
                        "vs": View(kv["VS"].ap()[tsl, :]), "vw": View(kv["VW"].ap()[tsl, :])}
                C.nsa_proj(W["nsa_g"].ap()[slot], W["nsa_win"].ap()[slot], W["nsa_qg"].ap()[slot], W["nsa_kg"].ap()[slot],
                           W["nsa_gb"].ap()[slot], outs)
            D = dict(K)
            D.update(kv)
            D["cw1"] = View(W["cw1"].ap()[slot])
            D["cw2"] = View(W["cw2"].ap()[slot])
            D["peT"] = View(W["peT"].ap()[slot])
            D["kg0"] = View(W["kg0"].ap()[slot])
            C.nsa_compress(D)
            for ck in range(NCK):
                C.load_x(src.ap()[ck])
                D["qn"] = View(qs.ap()[ck])
                D["gates"] = View(gts.ap()[ck])
                C.nsa_attn(ck, D, W["nsa_wout"].ap()[slot])
                C.mlp(W["mlp_g"].ap()[layer], W["mlp_w1"].ap()[layer], W["mlp_w2"].ap()[layer])
                C.store_x(dst.ap()[ck])
                P.barrier()
                P.flush()


_PROGS = {}


def _prog(depth):
    if depth not in _PROGS:
        nc = bass.Bass("TRN2", target_bir_lowering=False)
        C = Core(nc)
        _build_fused(C, depth)
        C.P.barrier()
        C.P.flush()
        C.P.close()
        _PROGS[depth] = nc
    return _PROGS[depth]


def kernel(x, sb_norm_g, sb_w_qkv, sb_w_out, nsa_norm_g, nsa_w_in, nsa_gate_b, nsa_q_norm_g, nsa_k_norm_g,
           nsa_cmp_pe, nsa_cmp_w1, nsa_cmp_w2, nsa_w_out, mlp_norm_g, mlp_w1, mlp_w2):
    f32 = lambda a: np.ascontiguousarray(np.asarray(a, np.float32))
    x = f32(x)
    depth = 4 if _DEBUG_LAYERS is None else _DEBUG_LAYERS
    shared = dict(_consts())
    kgT = f32(np.asarray(nsa_k_norm_g).transpose(0, 2, 1))
    shared.update({
        "sb_g": np.stack([_pc(g) for g in np.asarray(sb_norm_g)]), "sb_wqkv": f32(sb_w_qkv), "sb_wout": f32(sb_w_out),
        "nsa_g": np.stack([_pc(g) for g in np.asarray(nsa_norm_g)]), "nsa_win": f32(nsa_w_in),
        "nsa_qg": f32(nsa_q_norm_g).reshape(2, 128, 1), "nsa_kg": kgT, "nsa_gb": f32(nsa_gate_b).reshape(2, 48, 1),
        "cw1": f32(nsa_cmp_w1), "cw2": f32(nsa_cmp_w2), "peT": f32(np.asarray(nsa_cmp_pe).transpose(0, 1, 3, 2)),
        "kg0": f32(kgT[:, :, 0:1]), "nsa_wout": f32(nsa_w_out),
        "mlp_g": np.stack([_pc(g) for g in np.asarray(mlp_norm_g)]), "mlp_w1": f32(mlp_w1), "mlp_w2": f32(mlp_w2)})
    xin = [np.ascontiguousarray(x[b].reshape(NCK, TOK, D_MODEL).transpose(0, 2, 1)) for b in range(BATCH)]
    in_maps = []
    for c in range(NCORES):
        m = dict(shared)
        m["xin"] = xin[c // 4]
        in_maps.append(m)
    nc = _prog(depth)
    res = run_bass_kernel_spmd(nc, in_maps, core_ids=list(range(NCORES))).results
    out = np.empty((BATCH, SEQ, D_MODEL), np.float32)
    for b in range(BATCH):
        out[b] = res[4 * b]["yout"].transpose(0, 2, 1).reshape(SEQ, D_MODEL)
    return out
```

```python
import numpy as np
from contextlib import ExitStack
import concourse.bass as bass
import concourse.mybir as mybir
from concourse.bass_utils import run_bass_kernel_spmd

F32 = mybir.dt.float32
BF16 = mybir.dt.bfloat16
ALU = mybir.AluOpType
AF = mybir.ActivationFunctionType
AX = mybir.AxisListType

D_MODEL = 2048
SEQ = 4096
BATCH = 2
NCORES = 8
TOK = 1024
NQB = 8
NCH = 16
D_FF = 8192
HEADS = 16
DH = 128
EPS = 1e-6
SCALE = DH ** -0.5
NEG = -30000.0

SEM_CAP = 30000


def core_blocks(j):
    return [j, j + 4, j + 8, j + 12, 19 - j, 23 - j, 27 - j, 31 - j]


class Reg:
    __slots__ = ("w", "rs", "name")

    def __init__(self, name=""):
        self.w = None
        self.rs = {}
        self.name = name


class Eng:
    def __init__(self, prog, name, kind):
        self.prog = prog
        self.name = name
        self.kind = kind
        self.items = []
        self.sem = None
        self.count = 0
        self.waited = {}
        self.slots = []
        self.slot_i = 0

    def new_sem(self):
        self.sem = self.prog.new_sem(self.name)
        self.count = 0


class Prog:
    NSLOT = 6

    def __init__(self, nc):
        self.nc = nc
        self.stack = ExitStack()
        self.nsem = 0
        self.engs = {}
        for name in ("pe", "act", "dve", "pool", "sp"):
            e = Eng(self, name, name)
            self.engs[name] = e
            e.new_sem()
        for name in ("act", "pool", "sp"):
            e = self.engs[name]
            e.slots = [[self.new_sem(f"{name}_dma{i}"), 0] for i in range(self.NSLOT)]
        self.nops = 0

    def new_sem(self, name):
        self.nsem += 1
        return self.stack.enter_context(self.nc.semaphore(f"s{self.nsem}_{name}"))

    def sbuf(self, name, shape, dtype, stack=None):
        st = stack if stack is not None else self.stack
        self.uid = getattr(self, "uid", 0) + 1
        return st.enter_context(self.nc.sbuf_tensor(f"{name}_u{self.uid}", list(shape), dtype))

    def psum(self, name, shape, dtype, stack=None):
        st = stack if stack is not None else self.stack
        return st.enter_context(self.nc.psum_tensor(name, list(shape), dtype))

    def _deps(self, eng, reads, writes):
        toks = []
        for r in reads:
            if r.w is not None:
                toks.append(r.w)
        for r in writes:
            if r.w is not None:
                toks.append(r.w)
            toks.extend(r.rs.values())
        return toks

    def _emit_waits(self, eng, toks):
        for tok in toks:
            sem, val, src = tok
            if src is eng and eng.kind == "pe":
                continue
            key = id(sem)
            if eng.waited.get(key, 0) >= val:
                continue
            eng.waited[key] = val
            eng.items.append(("wait", sem, val))

    def _finish(self, tok, reads, writes):
        for r in reads:
            k = id(tok[0])
            old = r.rs.get(k)
            if old is None or old[1] < tok[1]:
                r.rs[k] = tok
        for r in writes:
            r.w = tok
            r.rs = {}

    def op(self, engname, fn, reads=(), writes=(), signal=True):
        eng = self.engs[engname]
        self.nops += 1
        self._emit_waits(eng, self._deps(eng, reads, writes))
        if eng.count >= SEM_CAP:
            eng.new_sem()
        if signal:
            eng.count += 1
            tok = (eng.sem, eng.count, eng)
            eng.items.append(("op", fn, eng.sem, 1))
        else:
            tok = (eng.sem, eng.count + 1, eng)
            eng.items.append(("op", fn, None, 0))
        self._finish(tok, reads, writes)
        return tok

    def dma(self, qname, fn, reads=(), writes=()):
        eng = self.engs[qname]
        self.nops += 1
        slot = eng.slots[eng.slot_i]
        eng.slot_i = (eng.slot_i + 1) % len(eng.slots)
        if slot[1] >= SEM_CAP:
            slot[0] = self.new_sem(f"{qname}_dma")
            slot[1] = 0
        toks = self._deps(eng, reads, writes)
        if slot[1] > 0:
            toks.append((slot[0], slot[1], None))
        self._emit_waits(eng, toks)
        slot[1] += 16
        tok = (slot[0], slot[1], None)
        eng.items.append(("op", fn, slot[0], 16))
        self._finish(tok, reads, writes)
        return tok

    def barrier(self):
        toks = []
        for e in self.engs.values():
            if e.count > 0:
                toks.append((e.sem, e.count, None))
            for s in e.slots:
                if s[1] > 0:
                    toks.append((s[0], s[1], None))
        for e in self.engs.values():
            self._emit_waits(e, toks)

    def final_wait(self, toks, engname="sp"):
        self._emit_waits(self.engs[engname], toks)

    def flush(self):
        nc = self.nc

        def run(items, h):
            for it in items:
                if it[0] == "wait":
                    h.wait_ge(it[1], it[2])
                else:
                    ins = it[1](h)
                    if it[2] is not None:
                        ins.then_inc(it[2], it[3])

        with nc.Block() as block:
            @block.tensor
            def _(h):
                run(self.engs["pe"].items, h)

            @block.scalar
            def _(h):
                run(self.engs["act"].items, h)

            @block.vector
            def _(h):
                run(self.engs["dve"].items, h)

            @block.gpsimd
            def _(h):
                run(self.engs["pool"].items, h)

            @block.sync
            def _(h):
                run(self.engs["sp"].items, h)
        for e in self.engs.values():
            e.items = []

    def close(self):
        self.stack.close()


class Core:
    def __init__(self, nc):
        self.nc = nc
        self.P = Prog(nc)
        P = self.P
        self.xT = P.sbuf("xT_sb", [128, NCH, TOK], F32)
        self.r_xT = [Reg(f"xT{c}") for c in range(NCH)]
        self.ones = P.sbuf("ones_bf", [128, 128], BF16)
        self.r_ones = Reg("ones")
        self.ident = P.sbuf("ident_bf", [128, 128], BF16)
        self.r_ident = Reg("ident")
        self.wbuf = [P.sbuf(f"wbuf{i}", [128, NCH, 512], BF16) for i in range(2)]
        self.r_wbuf = [[Reg(f"wbuf{i}_{k}") for k in range(2)] for i in range(2)]
        self.wi = 0
        self.banks = [P.psum(f"bank{i}", [128, 512], F32) for i in range(7)]
        self.r_banks = [Reg(f"bank{i}") for i in range(7)]
        self.tbank = P.psum("tbank", [128, 1024], BF16)
        self.r_tbank = Reg("tbank")
        self.dram = {}

    def din(self, name, shape, dtype=F32):
        t = self.nc.dram_tensor(name, list(shape), dtype, kind="ExternalInput")
        self.dram[name] = t
        return t

    def dout(self, name, shape, dtype=F32):
        t = self.nc.dram_tensor(name, list(shape), dtype, kind="ExternalOutput")
        self.dram[name] = t
        return t

    def dscratch(self, name, shape, dtype=F32):
        t = self.nc.dram_tensor(name, list(shape), dtype, kind="Internal")
        self.dram[name] = t
        return t

    def init_consts(self, ident_dram):
        P = self.P
        ones = self.ones
        P.op("pool", lambda h: h.memset(ones[:], 1.0), writes=[self.r_ones])
        ident = self.ident
        P.dma("pool", lambda h: h.dma_start(out=ident[:], in_=ident_dram.ap()), writes=[self.r_ident])

    def load_x(self, x_dram):
        P = self.P
        xT = self.xT
        src = x_dram.rearrange("(c p) t -> p c t", p=128)
        for c0 in range(0, NCH, 4):
            P.dma("sp", lambda h, c0=c0: h.dma_start(out=xT[:, c0:c0 + 4, :], in_=src[:, c0:c0 + 4, :]),
                  writes=self.r_xT[c0:c0 + 4])

    def store_x(self, y_dram):
        P = self.P
        xT = self.xT
        dst = y_dram.rearrange("(c p) t -> p c t", p=128)
        toks = []
        for c0 in range(0, NCH, 4):
            toks.append(P.dma("sp", lambda h, c0=c0: h.dma_start(out=dst[:, c0:c0 + 4, :], in_=xT[:, c0:c0 + 4, :]),
                              reads=self.r_xT[c0:c0 + 4]))
        return toks

    def next_wbuf(self):
        i = self.wi
        self.wi ^= 1
        return self.wbuf[i], self.r_wbuf[i]

    def load_w(self, src_ap, ncols=512, nch=NCH):
        P = self.P
        wb, rw = self.next_wbuf()
        src = src_ap.rearrange("(c p) f -> p c f", p=128)
        hc = max(nch // 2, 1)
        for c0 in range(0, nch, hc):
            P.dma("pool", lambda h, c0=c0: h.dma_start(out=wb[:, c0:c0 + hc, 0:ncols], in_=src[:, c0:c0 + hc, :]),
                  writes=[rw[c0 // 8]] if nch == NCH else rw)
        return wb, rw

    def rmsnorm(self, g_sb, r_g, xn, r_xn, st):
        P = self.P
        xT, ones = self.xT, self.ones
        sq = [P.sbuf(f"rn_sq{i}", [128, 512], BF16, st) for i in range(2)]
        r_sq = [Reg() for _ in range(2)]
        lnv = P.sbuf("rn_ln", [128, 512], F32, st)
        r_ln = Reg()
        rstd = [P.sbuf(f"rn_rstd{i}", [128, 512], F32, st) for i in range(2)]
        r_rstd = [Reg() for _ in range(2)]
        for th in range(2):
            ts = slice(th * 512, (th + 1) * 512)
            bank, rb = self.banks[th], self.r_banks[th]
            for c in range(NCH):
                s, rs = sq[c % 2], r_sq[c % 2]
                P.op("act", lambda h, s=s, c=c, ts=ts: h.activation(out=s[:], in_=xT[:, c, ts], func=AF.Square),
                     reads=[self.r_xT[c]], writes=[rs])
                P.op("pe", lambda h, s=s, c=c, bank=bank: h.matmul(bank[:], ones[:], s[:], start=(c == 0), stop=(c == NCH - 1)),
                     reads=[rs, self.r_ones], writes=[rb])
            P.op("act", lambda h, bank=bank: h.activation(out=lnv[:], in_=bank[:], func=AF.Ln, bias=EPS, scale=1.0 / D_MODEL),
                 reads=[rb], writes=[r_ln])
            rs_t, r_rs = rstd[th], r_rstd[th]
            P.op("act", lambda h, rs_t=rs_t: h.activation(out=rs_t[:], in_=lnv[:], func=AF.Exp, scale=-0.5),
                 reads=[r_ln], writes=[r_rs])
            for c in range(NCH):
                P.op("dve", lambda h, c=c, ts=ts, rs_t=rs_t: h.scalar_tensor_tensor(
                    out=xn[:, c, ts], in0=xT[:, c, ts], scalar=g_sb[:, c:c + 1], in1=rs_t[:], op0=ALU.mult, op1=ALU.mult),
                    reads=[self.r_xT[c], r_rs, r_g], writes=[r_xn[c]])

    def mlp(self, g_dram_row, w1_ap, w2_ap):
        P = self.P
        with ExitStack() as st:
            g_sb = P.sbuf("mlp_g", [128, NCH], F32, st)
            r_g = Reg()
            P.dma("sp", lambda h: h.dma_start(out=g_sb[:], in_=g_dram_row),
                  writes=[r_g])
            xn = P.sbuf("mlp_xn", [128, NCH, TOK], BF16, st)
            r_xn = [Reg() for _ in range(NCH)]
            self.rmsnorm(g_sb, r_g, xn, r_xn, st)
            h1 = P.sbuf("mlp_h1", [128, 16, TOK], BF16, st)
            r_h1 = [Reg() for _ in range(16)]
            rl = [P.sbuf(f"mlp_r{i}", [128, 512], F32, st) for i in range(2)]
            r_rl = [Reg() for _ in range(2)]
            xT = self.xT
            nb = 0
            nr = 0
            for fq in range(4):
                for fg in range(4):
                    f0 = fq * 2048 + fg * 512
                    wb, rw = self.load_w(w1_ap[:, f0:f0 + 512])
                    for fc in range(4):
                        for th in range(2):
                            ts = slice(th * 512, (th + 1) * 512)
                            bank, rb = self.banks[2 + nb % 4], self.r_banks[2 + nb % 4]
                            nb += 1
                            for c in range(NCH):
                                P.op("pe", lambda h, wb=wb, c=c, fc=fc, ts=ts, bank=bank: h.matmul(
                                    bank[:], wb[:, c, fc * 128:(fc + 1) * 128], xn[:, c, ts], start=(c == 0), stop=(c == NCH - 1)),
                                    reads=[rw[c // 8], r_xn[c]], writes=[rb], signal=(c == NCH - 1))
                            r_t, r_r = rl[nr % 2], r_rl[nr % 2]
                            nr += 1
                            P.op("act", lambda h, r_t=r_t, bank=bank: h.activation(out=r_t[:], in_=bank[:], func=AF.Relu),
                                 reads=[rb], writes=[r_r])
                            hc = fg * 4 + fc
                            P.op("dve", lambda h, r_t=r_t, hc=hc, ts=ts: h.tensor_tensor(
                                out=h1[:, hc, ts], in0=r_t[:], in1=r_t[:], op=ALU.mult),
                                reads=[r_r], writes=[r_h1[hc]])
                for dg in range(4):
                    wb, rw = self.load_w(w2_ap[fq * 2048:(fq + 1) * 2048, dg * 512:(dg + 1) * 512])
                    for dc in range(4):
                        for th in range(2):
                            ts = slice(th * 512, (th + 1) * 512)
                            bank, rb = self.banks[2 + nb % 4], self.r_banks[2 + nb % 4]
                            nb += 1
                            for c in range(16):
                                P.op("pe", lambda h, wb=wb, c=c, dc=dc, ts=ts, bank=bank: h.matmul(
                                    bank[:], wb[:, c, dc * 128:(dc + 1) * 128], h1[:, c, ts], start=(c == 0), stop=(c == 15)),
                                    reads=[rw[c // 8], r_h1[c]], writes=[rb], signal=(c == 15))
                            xc = dg * 4 + dc
                            P.op("dve", lambda h, xc=xc, ts=ts, bank=bank: h.tensor_tensor(
                                out=xT[:, xc, ts], in0=xT[:, xc, ts], in1=bank[:], op=ALU.add),
                                reads=[rb], writes=[self.r_xT[xc]])
            P.barrier()
            P.flush()

    def load_small(self, name, dram_ap, shape, dtype, st, q="sp"):
        P = self.P
        t = P.sbuf("s_" + name, shape, dtype, st)
        r = Reg(name)
        P.dma(q, lambda h: h.dma_start(out=t[:], in_=dram_ap), writes=[r])
        return t, r

    def proj_fm(self, w_ap, xn, r_xn, ncols, evac):
        P = self.P
        nb = 0
        for g0 in range(0, ncols, 512):
            wb, rw = self.load_w(w_ap[:, g0:g0 + 512])
            for jc in range(4):
                for th in range(2):
                    ts = slice(th * 512, (th + 1) * 512)
                    bank, rb = self.banks[nb % 3], self.r_banks[nb % 3]
                    nb += 1
                    for c in range(NCH):
                        P.op("pe", lambda h, wb=wb, c=c, jc=jc, ts=ts, bank=bank: h.matmul(
                            bank[:], wb[:, c, jc * 128:(jc + 1) * 128], xn[:, c, ts], start=(c == 0), stop=(c == NCH - 1)),
                            reads=[rw[c // 8], r_xn[c]], writes=[rb], signal=(c == NCH - 1))
                    evac(g0 // 128 + jc, th, bank, rb)

    def proj_tm(self, w_ap, xn, r_xn, ncols, evac):
        P = self.P
        nb = 0
        for g0 in range(0, ncols, 512):
            wb, rw = self.load_w(w_ap[:, g0:g0 + 512])
            for tt in range(NQB):
                bank, rb = self.banks[nb % 3], self.r_banks[nb % 3]
                nb += 1
                for c in range(NCH):
                    P.op("pe", lambda h, wb=wb, c=c, tt=tt, bank=bank: h.matmul(
                        bank[:], xn[:, c, tt * 128:(tt + 1) * 128], wb[:, c, :], start=(c == 0), stop=(c == NCH - 1)),
                        reads=[rw[c // 8], r_xn[c]], writes=[rb], signal=(c == NCH - 1))
                evac(g0 // 512, tt, bank, rb)

    def out_proj(self, w_ap, oT, r_oT):
        P = self.P
        xT = self.xT
        nb = 0
        for dg in range(4):
            wb, rw = self.load_w(w_ap[:, dg * 512:(dg + 1) * 512])
            for dc in range(4):
                for th in range(2):
                    ts = slice(th * 512, (th + 1) * 512)
                    bank, rb = self.banks[nb % 3], self.r_banks[nb % 3]
                    nb += 1
                    for c in range(16):
                        P.op("pe", lambda h, wb=wb, c=c, dc=dc, ts=ts, bank=bank: h.matmul(
                            bank[:], wb[:, c, dc * 128:(dc + 1) * 128], oT[:, c, ts], start=(c == 0), stop=(c == 15)),
                            reads=[rw[c // 8], r_oT[c]], writes=[rb], signal=(c == 15))
                    xc = dg * 4 + dc
                    P.op("dve", lambda h, xc=xc, ts=ts, bank=bank: h.tensor_tensor(
                        out=xT[:, xc, ts], in0=xT[:, xc, ts], in1=bank[:], op=ALU.add),
                        reads=[rb], writes=[self.r_xT[xc]])

    def sb_proj(self, g_ap, wqkv_ap, qT_out, kT_out, v_out):
        P = self.P
        with ExitStack() as st:
            g_sb, r_g = self.load_small("sb_g", g_ap, [128, NCH], F32, st)
            xn = P.sbuf("sb_xn", [128, NCH, TOK], BF16, st)
            r_xn = [Reg() for _ in range(NCH)]
            self.rmsnorm(g_sb, r_g, xn, r_xn, st)
            stg = [P.sbuf(f"sb_stg{i}", [128, 512], BF16, st) for i in range(4)]
            r_stg = [Reg() for _ in range(4)]
            cnt = [0]
            outs = []

            def evac_fm(dst, scale):
                def f(j, th, bank, rb):
                    i = cnt[0] % 4
                    cnt[0] += 1
                    s, rs = stg[i], r_stg[i]
                    P.op("act", lambda h: h.activation(out=s[:], in_=bank[:], func=AF.Copy, scale=scale),
                         reads=[rb], writes=[rs])
                    outs.append(P.dma("sp", lambda h: h.dma_start(out=dst.ap()[j, :, th * 512:(th + 1) * 512], in_=s[:]),
                                      reads=[rs]))
                return f

            def evac_tm(g, tt, bank, rb):
                i = cnt[0] % 4
                cnt[0] += 1
                s, rs = stg[i], r_stg[i]
                P.op("dve", lambda h: h.tensor_copy(out=s[:], in_=bank[:]), reads=[rb], writes=[rs])
                outs.append(P.dma("sp", lambda h: h.dma_start(
                    out=v_out.ap()[tt * 128:(tt + 1) * 128, g * 512:(g + 1) * 512], in_=s[:]), reads=[rs]))

            self.proj_fm(wqkv_ap[:, 0:2048], xn, r_xn, 2048, evac_fm(qT_out, SCALE))
            self.proj_fm(wqkv_ap[:, 2048:4096], xn, r_xn, 2048, evac_fm(kT_out, 1.0))
            self.proj_tm(wqkv_ap[:, 4096:6144], xn, r_xn, 2048, evac_tm)
            P.barrier()
            P.flush()
        return outs

    def sb_attn(self, ck, qT_in, KT_all, V_all, consts, wout_ap):
        P = self.P
        with ExitStack() as st:
            negtri, r_negtri = self.load_small("negtri", consts["negtri"].ap(), [128, 128], BF16, st)
            negones, r_negones = self.load_small("negones", consts["negones"].ap(), [128, 128], BF16, st)
            m01, r_m01 = self.load_small("m01", consts["m01"].ap(), [128, 128], BF16, st)
            nm, r_nm = self.load_small("nm", consts["nm"].ap(), [128, 128], BF16, st)
            nk = 8 * ck + 8
            ident, r_ident = self.ident, self.r_ident
            qq = [P.sbuf(f"sb_q{i}", [128, TOK], BF16, st) for i in range(2)]
            r_qq = [Reg() for _ in range(2)]
            oT = P.sbuf("sb_oT", [128, HEADS, TOK], BF16, st)
            r_oT = [Reg() for _ in range(HEADS)]
            kt = [P.sbuf(f"sb_kt{i}", [128, SEQ], BF16, st) for i in range(2)]
            r_kt = [Reg() for _ in range(2)]
            vv = [P.sbuf(f"sb_v{i}", [128, 32, 128], BF16, st) for i in range(2)]
            r_vv = [Reg() for _ in range(2)]
            E = [P.sbuf(f"sb_E{i}", [128, 512], F32, st) for i in range(1)]
            r_E = [Reg() for _ in range(1)]
            SP = [P.sbuf(f"sb_SP{i}", [128, 512], BF16, st) for i in range(3)]
            r_SP = [Reg() for _ in range(3)]
            PP = [P.sbuf(f"sb_P{i}", [128, 512], BF16, st) for i in range(3)]
            r_PP = [Reg() for _ in range(3)]
            LS = [P.sbuf(f"sb_LS{i}", [128, 512], BF16, st) for i in range(2)]
            r_LS = [Reg() for _ in range(2)]
            E0, r_E0 = E[0], r_E[0]

            def load_head(hd):
                ktb, r_ktb = kt[hd % 2], r_kt[hd % 2]
                vb, r_vb = vv[hd % 2], r_vv[hd % 2]
                qb, r_qb = qq[hd % 2], r_qq[hd % 2]
                P.dma("sp", lambda h: h.dma_start(out=ktb[:, 0:nk * 128], in_=KT_all.ap()[hd][:, 0:nk * 128]), writes=[r_ktb])
                P.dma("sp", lambda h: h.dma_start(out=qb[:], in_=qT_in.ap()[hd]), writes=[r_qb])
                hk = nk // 2
                for half in range(2):
                    P.dma("sp", lambda h, half=half: h.dma_start(
                        out=vb[:, half * hk:(half + 1) * hk, :],
                        in_=V_all.ap()[half * hk * 128:(half + 1) * hk * 128, hd * 128:(hd + 1) * 128].rearrange("(i k) d -> k i d", k=128)),
                        writes=[r_vb])

            def mk_step(hd, grp, i, b0, n, first_av, last, obank, r_ob, ls, r_ls, first_of_head):
                ktb, r_ktb = kt[hd % 2], r_kt[hd % 2]
                vb, r_vb = vv[hd % 2], r_vv[hd % 2]
                qb, r_qb = qq[hd % 2], r_qq[hd % 2]
                p0 = grp * 4
                smin = max(0, i - b0)
                c_lo = smin * 128
                cols = slice(c_lo, 512)
                gcols = slice(p0 * 128 + c_lo, p0 * 128 + 512)
                has_top = i >= b0
                tcols = slice(c_lo, c_lo + 128)
                cc_lo = c_lo + (128 if has_top else 0)
                ccols = slice(cc_lo, 512)
                abank, r_ab = self.banks[n % 3], self.r_banks[n % 3]
                sp, r_sp = SP[n % 3], r_SP[n % 3]
                pp, r_pp = PP[n % 3], r_PP[n % 3]

                def A():
                    if first_of_head == 0 and hd == 0:
                        load_head(0)
                    if first_of_head == 3 and hd + 1 < HEADS:
                        load_head(hd + 1)
                    P.op("pe", lambda h: h.matmul(abank[:, cols], ktb[:, i * 128:(i + 1) * 128], qb[:, gcols], start=True, stop=True),
                         reads=[r_ktb, r_qb], writes=[r_ab])
                    P.op("act", lambda h: h.activation(out=E0[:, cols], in_=abank[:, cols], func=AF.Exp), reads=[r_ab], writes=[r_E0])
                    P.op("act", lambda h: h.activation(out=sp[:, cols], in_=E0[:, cols], func=AF.Ln, bias=1.0), reads=[r_E0], writes=[r_sp])
                    if has_top:
                        P.op("pool", lambda h: h.tensor_tensor(out=sp[:, tcols], in0=sp[:, tcols], in1=m01[:], op=ALU.mult),
                             reads=[r_m01], writes=[r_sp])

                def B():
                    P.op("pe", lambda h: h.matmul(abank[:, cols], negtri[:], sp[:, cols], start=False, stop=True, skip_group_check=True),
                         reads=[r_sp, r_negtri], writes=[r_ab])
                    if cc_lo < 512:
                        P.op("pe", lambda h: h.matmul(abank[:, ccols], negones[:], ls[:, ccols], start=False, stop=True, skip_group_check=True),
                             reads=[r_ls, r_negones], writes=[r_ab])
                    if has_top:
                        P.op("pe", lambda h: h.matmul(abank[:, tcols], ident[:], nm[:], start=False, stop=True, skip_group_check=True),
                             reads=[r_nm, r_ident], writes=[r_ab])
                    P.op("act", lambda h: h.activation(out=pp[:, cols], in_=abank[:, cols], func=AF.Exp), reads=[r_ab], writes=[r_pp])
                    if has_top:
                        P.op("dve", lambda h: h.tensor_copy(out=ls[:, tcols], in_=sp[:, tcols]), reads=[r_sp], writes=[r_ls])
                    if cc_lo < 512 and i > 0:
                        P.op("dve", lambda h: h.tensor_tensor(out=ls[:, ccols], in0=ls[:, ccols], in1=sp[:, ccols], op=ALU.add),
                             reads=[r_sp], writes=[r_ls])

                def C():
                    P.op("pe", lambda h: h.matmul(obank[:, cols], vb[:, i, :], pp[:, cols], start=first_av, stop=True, skip_group_check=True),
                         reads=[r_vb, r_pp], writes=[r_ob])
                    if last:
                        P.op("dve", lambda h: h.tensor_copy(out=oT[:, hd, p0 * 128:p0 * 128 + 512], in_=obank[:]),
                             reads=[r_ob], writes=[r_oT[hd]])
                return A, B, C

            steps = []
            nO = 0
            for hd in range(HEADS):
                kh = 0
                for grp in range(2):
                    b0 = 8 * ck + 4 * grp
                    obank, r_ob = self.banks[3 + nO % 2], self.r_banks[3 + nO % 2]
                    ls, r_ls = LS[nO % 2], r_LS[nO % 2]
                    nO += 1
                    for i in range(b0 + 3, -1, -1):
                        steps.append(mk_step(hd, grp, i, b0, len(steps), i == b0 + 3, i == 0, obank, r_ob, ls, r_ls, kh))
                        kh += 1
            for n in range(len(steps) + 2):
                if n < len(steps):
                    steps[n][0]()
                if 0 <= n - 1 < len(steps):
                    steps[n - 1][1]()
                if 0 <= n - 2 < len(steps):
                    steps[n - 2][2]()
            self.out_proj(wout_ap, oT, r_oT)
            P.barrier()
            P.flush()

def _nsa_proj(self, g_ap, win_ap, qg_ap, kg_ap, gb_ap, outs):
    P = self.P
    ones = self.ones
    with ExitStack() as st:
        g_sb, r_g = self.load_small("na_g", g_ap, [128, NCH], F32, st)
        qg, r_qg = self.load_small("na_qg", qg_ap, [128, 1], F32, st)
        kg, r_kg = self.load_small("na_kg", kg_ap, [128, 3], F32, st)
        gb, r_gb = self.load_small("na_gb", gb_ap, [48, 1], F32, st)
        qgs = P.sbuf("na_qgs", [128, 1], F32, st)
        r_qgs = Reg()
        P.op("dve", lambda h: h.tensor_scalar(out=qgs[:], in0=qg[:], scalar1=SCALE, scalar2=None, op0=ALU.mult),
             reads=[r_qg], writes=[r_qgs])
        ngb = P.sbuf("na_ngb", [48, 1], F32, st)
        r_ngb = Reg()
        P.op("dve", lambda h: h.tensor_scalar(out=ngb[:], in0=gb[:], scalar1=-1.0, scalar2=None, op0=ALU.mult),
             reads=[r_gb], writes=[r_ngb])
        xn = P.sbuf("na_xn", [128, NCH, TOK], BF16, st)
        r_xn = [Reg() for _ in range(NCH)]
        self.rmsnorm(g_sb, r_g, xn, r_xn, st)
        stg = [P.sbuf(f"na_stg{i}", [128, 512], BF16, st) for i in range(4)]
        r_stg = [Reg() for _ in range(4)]
        sqb = [P.sbuf(f"na_sq{i}", [128, 512], BF16, st) for i in range(2)]
        r_sqb = [Reg() for _ in range(2)]
        lnv = P.sbuf("na_ln", [128, 512], F32, st)
        r_ln = Reg()
        rstd = [P.sbuf(f"na_rstd{i}", [128, 512], F32, st) for i in range(2)]
        r_rstd = [Reg() for _ in range(2)]
        cnt = [0]
        cn = [0]

        def evac_copy(dst):
            def f(j, th, bank, rb):
                i = cnt[0] % 4
                cnt[0] += 1
                s, rs = stg[i], r_stg[i]
                P.op("act", lambda h: h.activation(out=s[:], in_=bank[:], func=AF.Copy), reads=[rb], writes=[rs])
                P.dma("sp", lambda h: h.dma_start(out=dst.ap()[j, :, th * 512:(th + 1) * 512], in_=s[:]), reads=[rs])
            return f

        def evac_norm(dst, gvec, r_gvec, extra_scale):
            lnscale = float(np.log(extra_scale))

            def f(j, th, bank, rb):
                i = cnt[0] % 4
                cnt[0] += 1
                s, rs = stg[i], r_stg[i]
                k = cn[0] % 2
                cn[0] += 1
                sq, r_sq = sqb[k], r_sqb[k]
                rs_t, r_rs = rstd[k], r_rstd[k]
                sbank, r_sb = self.banks[3 + k], self.r_banks[3 + k]
                P.op("act", lambda h: h.activation(out=sq[:], in_=bank[:], func=AF.Square), reads=[rb], writes=[r_sq])
                P.op("pe", lambda h: h.matmul(sbank[:], ones[:], sq[:], start=True, stop=True),
                     reads=[r_sq, self.r_ones], writes=[r_sb])
                P.op("act", lambda h: h.activation(out=lnv[:], in_=sbank[:], func=AF.Ln, bias=EPS, scale=1.0 / DH),
                     reads=[r_sb], writes=[r_ln])
                P.op("act", lambda h: h.activation(out=rs_t[:], in_=lnv[:], func=AF.Exp, scale=-0.5),
                     reads=[r_ln], writes=[r_rs])
                P.op("dve", lambda h: h.scalar_tensor_tensor(out=s[:], in0=bank[:], scalar=gvec, in1=rs_t[:],
                                                             op0=ALU.mult, op1=ALU.mult),
                     reads=[rb, r_rs, r_gvec], writes=[rs])
                P.dma("sp", lambda h: h.dma_start(out=dst.ap()[j, :, th * 512:(th + 1) * 512], in_=s[:]), reads=[rs])
            return f

        def evac_tm(dst):
            def f(g, tt, bank, rb):
                i = cnt[0] % 4
                cnt[0] += 1
                s, rs = stg[i], r_stg[i]
                P.op("dve", lambda h: h.tensor_copy(out=s[:], in_=bank[:]), reads=[rb], writes=[rs])
                P.dma("sp", lambda h: h.dma_start(out=dst.ap()[tt * 128:(tt + 1) * 128, :], in_=s[:]), reads=[rs])
            return f

        parts = _DBG_PARTS
        if parts is None or "q" in parts:
            self.proj_fm(win_ap[:, 0:2048], xn, r_xn, 2048, evac_norm(outs["qn"], qgs[:, 0:1], r_qgs, 1.0))
        if parts is None or "kc" in parts:
            self.proj_fm(win_ap[:, 2048:2560], xn, r_xn, 512, evac_copy(outs["kc"]))
            self.proj_fm(win_ap[:, 2560:3072], xn, r_xn, 512, evac_copy(outs["vc"]))
        if parts is None or "ks" in parts:
            self.proj_fm(win_ap[:, 3072:3584], xn, r_xn, 512, evac_norm(outs["ks"], kg[:, 1:2], r_kg, 1.0))
            self.proj_fm(win_ap[:, 4096:4608], xn, r_xn, 512, evac_norm(outs["kw"], kg[:, 2:3], r_kg, 1.0))
        if parts is None or "vs" in parts:
            self.proj_tm(win_ap[:, 3584:4096], xn, r_xn, 512, evac_tm(outs["vs"]))
            self.proj_tm(win_ap[:, 4608:5120], xn, r_xn, 512, evac_tm(outs["vw"]))
        if parts is not None and "gates" not in parts:
            P.barrier()
            P.flush()
            return
        wb, rw = self.load_w(win_ap[:, 5120:5168], ncols=48)
        ge = P.sbuf("na_ge", [48, 512], F32, st)
        r_ge = Reg()
        for th in range(2):
            ts = slice(th * 512, (th + 1) * 512)
            bank, rb = self.banks[5 + th], self.r_banks[5 + th]
            for c in range(NCH):
                P.op("pe", lambda h, c=c, ts=ts, bank=bank: h.matmul(
                    bank[0:48, :], wb[:, c, 0:48], xn[:, c, ts], start=(c == 0), stop=(c == NCH - 1)),
                    reads=[rw[c // 8], r_xn[c]], writes=[rb], signal=(c == NCH - 1))
            P.op("act", lambda h, bank=bank: h.activation(out=ge[:], in_=bank[0:48, :], func=AF.Exp, scale=-1.0, bias=ngb[:, 0:1]),
                 reads=[rb, r_ngb], writes=[r_ge])
            P.op("dve", lambda h: h.tensor_scalar(out=ge[:], in0=ge[:], scalar1=1.0, scalar2=None, op0=ALU.add),
                 reads=[r_ge], writes=[r_ge])
            P.op("dve", lambda h: h.reciprocal(out=ge[:], in_=ge[:]), reads=[r_ge], writes=[r_ge])
            P.dma("sp", lambda h, ts=ts: h.dma_start(out=outs["gates"].ap()[:, ts], in_=ge[:]), reads=[r_ge])
        P.barrier()
        P.flush()


Core.nsa_proj = _nsa_proj


def _nsa_compress(self, D):
    P = self.P
    ones, r_ones = self.ones, self.r_ones
    banks, r_banks = self.banks, self.r_banks
    if not hasattr(self, "kcn"):
        self.kcn = P.sbuf("nb_kcn", [128, 4, 256], BF16)
        self.r_kcn = Reg()
        self.vcb = P.sbuf("nb_vcb", [128, 2, 4, 128], BF16)
        self.r_vcb = Reg()
    kcn, r_kcn, vcb, r_vcb = self.kcn, self.r_kcn, self.vcb, self.r_vcb
    with ExitStack() as st:
        kg0, r_kg0 = self.load_small("nb_kg0", D["kg0"].ap(), [128, 1], F32, st)
        P.op("pool", lambda h: h.memset(kcn[:], 0.0), writes=[r_kcn])
        P.op("pool", lambda h: h.memset(vcb[:], 0.0), writes=[r_vcb])
        with ExitStack() as st2:
            xc = P.sbuf("nb_xc", [128, 4, 256, 16], BF16, st2)
            r_xc = Reg()
            w1c = P.sbuf("nb_w1c", [128, 32, 128], BF16, st2)
            r_w1c = Reg()
            w2c = P.sbuf("nb_w2c", [128, 128], BF16, st2)
            r_w2c = Reg()
            peT = P.sbuf("nb_peT", [128, 32], BF16, st2)
            r_peT = Reg()
            bias_sb = P.sbuf("nb_cb", [128, 1], F32, st2)
            r_bias = Reg()
            fa = [P.sbuf(f"nb_f{i}", [128, 256], F32, st2) for i in range(4)]
            r_fa = [Reg() for _ in range(4)]
            H2 = P.sbuf("nb_H2", [128, 256], BF16, st2)
            r_H2 = Reg()
            sq = P.sbuf("nb_csq", [128, 256], BF16, st2)
            r_sq = Reg()
            P.op("pool", lambda h: h.memset(H2[:], 0.0), writes=[r_H2])
            for kv in range(2):
                src = D["KC"] if kv == 0 else D["VC"]
                for g in range(4):
                    P.dma("sp", lambda h, g=g, src=src: h.dma_start(
                        out=xc[:, g, :, :], in_=src.ap()[g].rearrange("d (n r) -> d n r", r=16)), writes=[r_xc])
                P.dma("pool", lambda h, kv=kv: h.dma_start(
                    out=w1c[:], in_=D["cw1"].ap()[kv].rearrange("(l d) h -> d l h", d=128)), writes=[r_w1c])
                P.dma("pool", lambda h, kv=kv: h.dma_start(out=w2c[:], in_=D["cw2"].ap()[kv]), writes=[r_w2c])
                P.dma("pool", lambda h, kv=kv: h.dma_start(out=peT[:], in_=D["peT"].ap()[kv]), writes=[r_peT])
                bb, r_bb = banks[3], r_banks[3]
                for l in range(32):
                    P.op("pe", lambda h, l=l: h.matmul(bb[:, 0:1], w1c[:, l, :], peT[:, l:l + 1], start=(l == 0), stop=(l == 31)),
                         reads=[r_w1c, r_peT], writes=[r_bb], signal=(l == 31))
                P.op("dve", lambda h: h.tensor_copy(out=bias_sb[:], in_=bb[:, 0:1]), reads=[r_bb], writes=[r_bias])
                for g in range(4):
                    hb, r_hb = banks[5 + g % 2], r_banks[5 + g % 2]
                    for l in range(32):
                        n0, rr = (0, l) if l < 16 else (1, l - 16)
                        P.op("pe", lambda h, l=l, g=g, n0=n0, rr=rr, hb=hb: h.matmul(
                            hb[:, 0:255], w1c[:, l, :], xc[:, g, n0:n0 + 255, rr], start=(l == 0), stop=(l == 31)),
                            reads=[r_w1c, r_xc], writes=[r_hb], signal=(l == 31))
                    a, a2, u, th = fa
                    r_a, r_a2, r_u, r_th = r_fa
                    P.op("act", lambda h, hb=hb: h.activation(out=a[:, 0:255], in_=hb[:, 0:255], func=AF.Identity, bias=bias_sb[:, 0:1]),
                         reads=[r_hb, r_bias], writes=[r_a])
                    P.op("dve", lambda h: h.tensor_tensor(out=a2[:, 0:255], in0=a[:, 0:255], in1=a[:, 0:255], op=ALU.mult),
                         reads=[r_a], writes=[r_a2])
                    P.op("dve", lambda h: h.tensor_scalar(out=a2[:, 0:255], in0=a2[:, 0:255], scalar1=0.044715, scalar2=1.0,
                                                          op0=ALU.mult, op1=ALU.add), reads=[r_a2], writes=[r_a2])
                    P.op("dve", lambda h: h.tensor_tensor(out=u[:, 0:255], in0=a2[:, 0:255], in1=a[:, 0:255], op=ALU.mult),
                         reads=[r_a2, r_a], writes=[r_u])
                    P.op("act", lambda h: h.activation(out=th[:, 0:255], in_=u[:, 0:255], func=AF.Tanh, scale=0.7978845608028654),
                         reads=[r_u], writes=[r_th])
                    P.op("dve", lambda h: h.scalar_tensor_tensor(out=H2[:, 0:255], in0=th[:, 0:255], scalar=1.0, in1=a[:, 0:255],
                                                                 op0=ALU.add, op1=ALU.mult), reads=[r_th, r_a], writes=[r_H2])
                    if kv == 0:
                        kb, r_kb = banks[3], r_banks[3]
                        sb_, r_sb = banks[4], r_banks[4]
                        P.op("pe", lambda h: h.matmul(kb[:, 0:255], w2c[:], H2[:, 0:255], start=True, stop=True),
                             reads=[r_w2c, r_H2], writes=[r_kb])
                        P.op("act", lambda h: h.activation(out=a2[:, 0:255], in_=kb[:, 0:255], func=AF.Copy, scale=0.5),
                             reads=[r_kb], writes=[r_a2])
                        P.op("act", lambda h: h.activation(out=sq[:, 0:255], in_=kb[:, 0:255], func=AF.Square, scale=0.5),
                             reads=[r_kb], writes=[r_sq])
                        P.op("pe", lambda h: h.matmul(sb_[:, 0:255], ones[:], sq[:, 0:255], start=True, stop=True),
                             reads=[r_sq, r_ones], writes=[r_sb])
                        P.op("act", lambda h: h.activation(out=u[:, 0:255], in_=sb_[:, 0:255], func=AF.Ln, bias=EPS, scale=1.0 / DH),
                             reads=[r_sb], writes=[r_u])
                        P.op("act", lambda h: h.activation(out=th[:, 0:255], in_=u[:, 0:255], func=AF.Exp, scale=-0.5),
                             reads=[r_u], writes=[r_th])
                        P.op("dve", lambda h, g=g: h.scalar_tensor_tensor(out=kcn[:, g, 0:255], in0=a2[:, 0:255], scalar=kg0[:, 0:1],
                                                                          in1=th[:, 0:255], op0=ALU.mult, op1=ALU.mult),
                             reads=[r_a2, r_th, r_kg0], writes=[r_kcn])
                    else:
                        for nc_ in range(2):
                            M = 128 if nc_ == 0 else 127
                            vbk, r_vbk = banks[3 + nc_], r_banks[3 + nc_]
                            P.op("pe", lambda h, nc_=nc_, M=M, vbk=vbk: h.matmul(
                                vbk[0:M, 0:128], H2[:, nc_ * 128:nc_ * 128 + M], w2c[:], start=True, stop=True),
                                reads=[r_w2c, r_H2], writes=[r_vbk])
                            P.op("act", lambda h, nc_=nc_, M=M, g=g, vbk=vbk: h.activation(
                                out=vcb[0:M, nc_, g, :], in_=vbk[0:M, 0:128], func=AF.Copy, scale=0.5),
                                reads=[r_vbk], writes=[r_vcb])
            P.barrier()
            P.flush()


Core.nsa_compress = _nsa_compress


def _nsa_attn(self, ck, D, wout_ap):
    P = self.P
    ones, ident = self.ones, self.ident
    r_ones, r_ident = self.r_ones, self.r_ident
    banks, r_banks = self.banks, self.r_banks
    xT = self.xT
    with ExitStack() as st:
        L = lambda n, ap, sh, dt: self.load_small("nb_" + n, ap, sh, dt, st)
        aL, r_aL = L("aL", D["alibiL"].ap(), [6, HEADS, 128], BF16)
        aR, r_aR = L("aR", D["alibiR"].ap()[ck], [6, TOK], BF16)
        bKI, r_bKI = L("bKI", D["biasKI"].ap(), [128, HEADS, 32], F32)
        bC, r_bC = L("bC", D["biasC"].ap(), [128, HEADS, 2], F32)
        nmc, r_nmc = L("nmc", D["negmask_c"].ap()[ck], [128, 2, TOK], BF16)
        ovl, r_ovl = L("ovl", D["overlap"].ap(), [128, 2, 64], BF16)
        keep, r_keep = L("keep", D["keep01"].ap()[ck], [128, NQB, 64], F32)
        addc, r_addc = L("addc", D["addc"].ap()[ck], [128, NQB, 64], F32)
        lcomb = [P.sbuf(f"nb_lcomb{i}", [70, 32, 128], BF16, st) for i in range(1)]
        r_lcomb = [Reg() for _ in range(1)]
        for i_ in range(1):
            P.dma("sp", lambda h, i_=i_: h.dma_start(out=lcomb[i_][0:64], in_=D["eexp"].ap()), writes=[r_lcomb[i_]])
        nmi, r_nmi = L("nmi", D["nm_incl"].ap(), [128, 128], BF16)
        nma, r_nma = L("nma", D["nm_after"].ap(), [128, 128], BF16)
        nk = 8 * ck + 8
        sel3, r_sel3 = L("sel3", D["sel3"].ap(), [3, 3, 128], F32)
        gat = P.sbuf("nb_gat", [3, TOK], F32, st)
        r_gat = Reg()
        gates_v = D["gates"].ap().rearrange("(r h) t -> r h t", r=3)

        def load_gate(h_):
            P.dma("sp", lambda h: h.dma_start(out=gat[:], in_=gates_v[:, h_, :]), writes=[r_gat])

        kcn, r_kcn, vcb, r_vcb = self.kcn, self.r_kcn, self.vcb, self.r_vcb

        ksb = P.sbuf("nb_ksb", [128, SEQ], BF16, st)
        r_ksb = Reg()
        vsb = P.sbuf("nb_vsb", [128, 32, 128], BF16, st)
        r_vsb = Reg()
        qg4 = P.sbuf("nb_q4", [128, 4, TOK], BF16, st)
        r_qg4 = Reg()
        acc = P.sbuf("nb_acc", [128, 4, TOK], F32, st)
        r_acc = [Reg() for _ in range(4)]
        ob16 = [P.sbuf(f"nb_ob{i}", [128, TOK], BF16, st) for i in range(4)]
        r_ob16 = [Reg() for _ in range(4)]
        Pc = [P.sbuf(f"nb_Pc{i}", [128, 512], BF16, st) for i in range(4)]
        r_Pc = [Reg() for _ in range(4)]
        Pn = [P.sbuf(f"nb_Pn{i}", [128, 512], BF16, st) for i in range(2)]
        r_Pn = [Reg() for _ in range(2)]
        PS = [P.sbuf(f"nb_PS{i}", [128, 512], BF16, st) for i in range(4)]
        r_PS = [Reg() for _ in range(4)]
        rden = P.sbuf("nb_rden", [128, 512], F32, st)
        r_rden = Reg()
        Gs = P.sbuf("nb_Gs", [128, 512], F32, st)
        r_Gs = Reg()
        Osb = [P.sbuf(f"nb_osb{i}", [128, 512], F32, st) for i in range(2)]
        r_Osb = [Reg() for _ in range(2)]
        Pacc = [P.sbuf(f"nb_pacc{i}", [128, 512], F32, st) for i in range(2)]
        r_Pacc = [Reg() for _ in range(2)]
        ones32 = P.sbuf("nb_ones32", [128, 128], F32, st)
        r_ones32 = Reg()
        P.op("pool", lambda h: h.memset(ones32[:], 1.0), writes=[r_ones32])
        impf = P.sbuf("nb_impf", [128, NQB, 64], F32, st)
        r_impf = Reg()
        wrk = P.sbuf("nb_wrk", [128, 64], F32, st)
        r_wrk = Reg()
        m8 = P.sbuf("nb_m8", [128, 8], F32, st)
        r_m8 = Reg()
        s01 = P.sbuf("nb_s01", [128, 64], F32, st)
        r_s01 = Reg()
        nsel = P.sbuf("nb_nsel", [128, 64], BF16, st)
        r_nsel = Reg()
        nselT = P.sbuf("nb_nselT", [70, TOK], BF16, st)
        r_nselT = Reg()
        P.dma("sp", lambda h: h.dma_start(out=nselT[64:70, :], in_=D["alibiR"].ap()[ck]), writes=[r_nselT])
        tpsum = self.tbank[0:64, 0:128]
        r_tps = self.r_tbank
        nA = [0]
        nPS = [0]

        def next_A():
            k = nA[0] % 3
            nA[0] += 1
            return banks[k], r_banks[k]

        nEp = [0]

        def gate_epilogue(hl, h_, branch, cols, obank, r_ob, dbank, r_db, first_branch):
            gb, r_gb = banks[5], r_banks[5]
            osb, r_osb = Osb[nEp[0] % 2], r_Osb[nEp[0] % 2]
            nEp[0] += 1
            P.op("act", lambda h: h.activation(out=osb[:], in_=obank[:], func=AF.Copy), reads=[r_ob], writes=[r_osb])
            if dbank is not None:
                P.op("dve", lambda h: h.tensor_scalar(out=rden[:], in0=dbank[:], scalar1=1e-30, scalar2=None, op0=ALU.max),
                     reads=[r_db], writes=[r_rden])
            P.op("pe", lambda h: h.matmul(gb[:], sel3[:, branch, :], gat[:, cols], start=True, stop=True),
                 reads=[r_sel3, r_gat], writes=[r_gb])
            P.op("act", lambda h: h.activation(out=Gs[:], in_=gb[:], func=AF.Copy), reads=[r_gb], writes=[r_Gs])
            if dbank is not None:
                P.op("act", lambda h: h.activation(out=rden[:], in_=rden[:], func=AF.Ln), reads=[r_rden], writes=[r_rden])
                P.op("act", lambda h: h.activation(out=rden[:], in_=rden[:], func=AF.Exp, scale=-1.0), reads=[r_rden], writes=[r_rden])
                P.op("dve", lambda h: h.tensor_tensor(out=rden[:], in0=rden[:], in1=Gs[:], op=ALU.mult),
                     reads=[r_rden, r_Gs], writes=[r_rden])
                mul = rden
                r_mul = r_rden
            else:
                mul = Gs
                r_mul = r_Gs
            if first_branch:
                P.op("dve", lambda h: h.tensor_tensor(out=acc[:, hl, cols], in0=osb[:], in1=mul[:], op=ALU.mult),
                     reads=[r_osb, r_mul], writes=[r_acc[hl]])
            else:
                P.op("dve", lambda h: h.tensor_tensor(out=osb[:], in0=osb[:], in1=mul[:], op=ALU.mult),
                     reads=[r_osb, r_mul], writes=[r_osb])
                P.op("dve", lambda h: h.tensor_tensor(out=acc[:, hl, cols], in0=acc[:, hl, cols], in1=osb[:], op=ALU.add),
                     reads=[r_osb], writes=[r_acc[hl]])

        db, r_db = banks[3], r_banks[3]
        ob, r_ob = banks[4], r_banks[4]

        def mk_step(branch, hl, h_, grp, p0, b0, i, first, last, n):
            smin = max(0, i - b0)
            if branch == 1:
                smax = 3
                masks = [(smin, nmi, r_nmi)] if i >= b0 else []
            else:
                smax = min(3, i - b0 + 4)
                masks = []
                for s_ in range(smin, smax + 1):
                    rr_ = b0 + s_ - i
                    if rr_ == 0:
                        masks.append((s_, nmi, r_nmi))
                    elif rr_ == 4:
                        masks.append((s_, nma, r_nma))
            c_lo = smin * 128
            c_hi = (smax + 1) * 128
            cols = slice(c_lo, c_hi)
            gcols = slice(p0 * 128 + c_lo, p0 * 128 + c_hi)
            gcols_all = slice(p0 * 128, p0 * 128 + 512)
            ABK = (0, 1, 2, 6)
            ab, r_ab = banks[ABK[n % 4]], r_banks[ABK[n % 4]]
            ps, r_ps = PS[n % 4], r_PS[n % 4]
            lc, r_lc = lcomb[0], r_lcomb[0]
            pacc, r_pacc = Pacc[(2 * hl + grp) % 2], r_Pacc[(2 * hl + grp) % 2]

            def A():
                if first:
                    P.op("pool", lambda h: h.memset(pacc[:], 0.0), writes=[r_pacc])
                if branch == 1 and first and grp == 0:
                    P.dma("sp", lambda h: h.dma_start(out=lc[64:70], in_=D["alibiLx"].ap()[h_]), writes=[r_lc])
                P.op("pe", lambda h: h.matmul(ab[:, cols], ksb[:, i * 128:(i + 1) * 128], qg4[:, hl, gcols], start=True, stop=True),
                     reads=[r_ksb, r_qg4], writes=[r_ab])
                if branch == 1:
                    P.op("pe", lambda h: h.matmul(ab[:, cols], lc[:, i, :], nselT[:, gcols], start=False, stop=True, skip_group_check=True),
                         reads=[r_lc, r_nselT], writes=[r_ab])
                else:
                    P.op("pe", lambda h: h.matmul(ab[:, cols], aL[:, h_, :], aR[:, gcols], start=False, stop=True, skip_group_check=True),
                         reads=[r_aL, r_aR], writes=[r_ab])
                for (s_, mt, r_mt) in masks:
                    tcols = slice(s_ * 128, (s_ + 1) * 128)
                    P.op("pe", lambda h, tcols=tcols, mt=mt: h.matmul(ab[:, tcols], ident[:], mt[:], start=False, stop=True, skip_group_check=True),
                         reads=[r_mt, r_ident], writes=[r_ab])
                P.op("act", lambda h: h.activation(out=ps[:, cols], in_=ab[:, cols], func=AF.Exp, bias=bKI[:, h_, i:i + 1]),
                     reads=[r_ab, r_bKI], writes=[r_ps])

            def B():
                P.op("dve", lambda h: h.tensor_tensor(out=pacc[:, cols], in0=pacc[:, cols], in1=ps[:, cols], op=ALU.add),
                     reads=[r_ps], writes=[r_pacc])
                P.op("pe", lambda h: h.matmul(ob[:, cols], vsb[:, i, :], ps[:, cols], start=first, stop=True, skip_group_check=True),
                     reads=[r_ps, r_vsb], writes=[r_ob])
                if last:
                    P.op("pe", lambda h: h.matmul(db[:], ones32[:], pacc[:], start=True, stop=True),
                         reads=[r_pacc, r_ones32], writes=[r_db])
                    if grp == 0:
                        load_gate(h_)
                    gate_epilogue(hl, h_, branch, gcols_all, ob, r_ob, db, r_db, False)
            return A, B

        for g in range(4):
            hk = nk // 2
            P.dma("sp", lambda h, g=g: h.dma_start(out=ksb[:, 0:nk * 128], in_=D["KS"].ap()[g][:, 0:nk * 128]), writes=[r_ksb])
            for half in range(2):
                P.dma("sp", lambda h, g=g, half=half: h.dma_start(
                    out=vsb[:, half * hk:(half + 1) * hk, :],
                    in_=D["VS"].ap()[half * hk * 128:(half + 1) * hk * 128, g * 128:(g + 1) * 128].rearrange("(i k) d -> k i d", k=128)),
                    writes=[r_vsb])
            P.dma("sp", lambda h, g=g: h.dma_start(out=qg4[:], in_=D["qn"].ap()[4 * g:4 * g + 4].rearrange("h d t -> d h t")),
                  writes=[r_qg4])
            ib, r_ib = banks[6], r_banks[6]

            def mk_cstep(hl, h_, qg, n, g=g):
                cols = slice(qg * 512, (qg + 1) * 512)
                cdb, r_cdb = banks[2 + n % 2], r_banks[2 + n % 2]
                cob, r_cob = banks[4], r_banks[4]
                pcs = [(Pc[2 * (n % 2) + k_], r_Pc[2 * (n % 2) + k_]) for k_ in range(2)]

                def S1():
                    for nc_ in range(2):
                        M = 128 if nc_ == 0 else 127
                        ab, r_ab = banks[nc_], r_banks[nc_]
                        pc, r_pc = pcs[nc_]
                        P.op("pe", lambda h, ab=ab, nc_=nc_, M=M: h.matmul(
                            ab[0:M, :], kcn[:, g, nc_ * 128:nc_ * 128 + M], qg4[:, hl, cols], start=True, stop=True),
                            reads=[r_kcn, r_qg4], writes=[r_ab])
                        P.op("pe", lambda h, ab=ab, M=M: h.matmul(
                            ab[0:M, :], aL[:, h_, 0:M], aR[:, cols], start=False, stop=True, skip_group_check=True),
                            reads=[r_aL, r_aR], writes=[r_ab])
                        P.op("pe", lambda h, ab=ab, M=M, nc_=nc_: h.matmul(
                            ab[0:M, :], ident[0:M, 0:M], nmc[0:M, nc_, cols], start=False, stop=True, skip_group_check=True),
                            reads=[r_nmc, r_ident], writes=[r_ab])
                        P.op("act", lambda h, ab=ab, M=M, pc=pc, nc_=nc_: h.activation(
                            out=pc[0:M, :], in_=ab[0:M, :], func=AF.Exp, bias=bC[0:M, h_, nc_:nc_ + 1]),
                            reads=[r_ab, r_bC], writes=[r_pc])
                        P.op("pe", lambda h, M=M, pc=pc, nc_=nc_: h.matmul(
                            cdb[:], ones[0:M, :], pc[0:M, :], start=(nc_ == 0), stop=(nc_ == 1)),
                            reads=[r_pc, r_ones], writes=[r_cdb])

                def S2():
                    if qg == 0:
                        load_gate(h_)
                    P.op("dve", lambda h: h.tensor_scalar(out=rden[:], in0=cdb[:], scalar1=1e-30, scalar2=None, op0=ALU.max),
                         reads=[r_cdb], writes=[r_rden])
                    P.op("dve", lambda h: h.reciprocal(out=rden[:], in_=rden[:]), reads=[r_rden], writes=[r_rden])
                    for nc_ in range(2):
                        M = 128 if nc_ == 0 else 127
                        pc, r_pc = pcs[nc_]
                        P.op("dve", lambda h, nc_=nc_, M=M, pc=pc: h.tensor_tensor(
                            out=Pn[nc_][0:M, :], in0=pc[0:M, :], in1=rden[0:M, :], op=ALU.mult),
                            reads=[r_pc, r_rden], writes=[r_Pn[nc_]])
                    for nc_ in range(2):
                        M = 128 if nc_ == 0 else 127
                        P.op("pe", lambda h, nc_=nc_, M=M: h.matmul(
                            cob[:], vcb[0:M, nc_, g, :], Pn[nc_][0:M, :], start=(nc_ == 0), stop=(nc_ == 1)),
                            reads=[r_vcb, r_Pn[nc_]], writes=[r_cob])
                    for qb in range(4):
                        for nc_ in range(2):
                            M = 128 if nc_ == 0 else 127
                            fi = (n == 0 and qb == 0 and nc_ == 0)
                            P.op("pe", lambda h, qb=qb, nc_=nc_, M=M, fi=fi: h.matmul(
                                ib[:, (qg * 4 + qb) * 64:(qg * 4 + qb + 1) * 64], Pn[nc_][0:M, qb * 128:(qb + 1) * 128],
                                ovl[0:M, nc_, :], start=fi, stop=True, skip_group_check=True),
                                reads=[r_Pn[nc_], r_ovl], writes=[r_ib])
                    gate_epilogue(hl, h_, 0, cols, cob, r_cob, None, None, True)
                return S1, S2

            csteps = []
            for hl in range(4):
                for qg in range(2):
                    csteps.append(mk_cstep(hl, 4 * g + hl, qg, len(csteps)))
            csteps[0][0]()
            for n in range(len(csteps)):
                if n + 1 < len(csteps):
                    csteps[n + 1][0]()
                csteps[n][1]()
            ibv = ib[:].rearrange("p (q j) -> p q j", j=64)
            P.op("dve", lambda h: h.tensor_tensor(out=impf[:], in0=ibv, in1=keep[:], op=ALU.mult),
                 reads=[r_ib, r_keep], writes=[r_impf])
            P.op("dve", lambda h: h.tensor_tensor(out=impf[:], in0=impf[:], in1=addc[:], op=ALU.add),
                 reads=[r_impf, r_addc], writes=[r_impf])
            for qbi in range(NQB):
                P.op("dve", lambda h, qbi=qbi: h.max(out=m8[:], in_=impf[:, qbi, :]), reads=[r_impf], writes=[r_m8])
                P.op("dve", lambda h, qbi=qbi: h.match_replace(out=wrk[:], in_to_replace=m8[:], in_values=impf[:, qbi, :],
                                                               imm_value=-1e30), reads=[r_m8, r_impf], writes=[r_wrk])
                P.op("dve", lambda h: h.max(out=m8[:], in_=wrk[:]), reads=[r_wrk], writes=[r_m8])
                P.op("dve", lambda h: h.match_replace(out=s01[:], in_to_replace=m8[:], in_values=wrk[:], imm_value=-1e30),
                     reads=[r_m8, r_wrk], writes=[r_s01])
                P.op("dve", lambda h: h.tensor_scalar(out=nsel[:], in0=s01[:], scalar1=-1e29, scalar2=NEG,
                                                      op0=ALU.is_gt, op1=ALU.mult), reads=[r_s01], writes=[r_nsel])
                P.op("pe", lambda h: h.transpose(tpsum, nsel[:], ident[:]), reads=[r_nsel, r_ident], writes=[r_tps])
                P.op("act", lambda h, qbi=qbi: h.activation(out=nselT[0:64, qbi * 128:(qbi + 1) * 128], in_=tpsum, func=AF.Copy),
                     reads=[r_tps], writes=[r_nselT])
            for branch in (1, 2):
                if branch == 2:
                    P.dma("sp", lambda h, g=g: h.dma_start(out=ksb[:, 0:nk * 128], in_=D["KW"].ap()[g][:, 0:nk * 128]), writes=[r_ksb])
                    for half in range(2):
                        P.dma("sp", lambda h, g=g, half=half: h.dma_start(
                            out=vsb[:, half * hk:(half + 1) * hk, :],
                            in_=D["VW"].ap()[half * hk * 128:(half + 1) * hk * 128, g * 128:(g + 1) * 128].rearrange("(i k) d -> k i d", k=128)),
                            writes=[r_vsb])
                steps = []
                for hl in range(4):
                    h_ = 4 * g + hl
                    for grp in range(2):
                        p0 = grp * 4
                        b0 = 8 * ck + 4 * grp
                        itop = b0 + 3
                        ilow = 0 if branch == 1 else max(0, b0 - 4)
                        for i in range(itop, ilow - 1, -1):
                            steps.append(mk_step(branch, hl, h_, grp, p0, b0, i, i == itop, i == ilow, len(steps)))
                for n in range(len(steps) + 2):
                    if n < len(steps):
                        steps[n][0]()
                    if 0 <= n - 2 < len(steps):
                        steps[n - 2][1]()
            for hl in range(4):
                P.op("act", lambda h, hl=hl: h.activation(out=ob16[hl][:], in_=acc[:, hl, :], func=AF.Copy),
                     reads=[r_acc[hl]], writes=[r_ob16[hl]])
            nb = 0
            for dg in range(4):
                wb, rw = self.load_w(wout_ap[g * 512:(g + 1) * 512, dg * 512:(dg + 1) * 512], nch=4)
                for dc in range(4):
                    for th in range(2):
                        ts = slice(th * 512, (th + 1) * 512)
                        bank, rb = banks[5 + nb % 2], r_banks[5 + nb % 2]
                        nb += 1
                        for c in range(4):
                            P.op("pe", lambda h, wb=wb, c=c, dc=dc, ts=ts, bank=bank: h.matmul(
                                bank[:], wb[:, c, dc * 128:(dc + 1) * 128], ob16[c][:, ts], start=(c == 0), stop=(c == 3)),
                                reads=rw + [r_ob16[c]], writes=[rb], signal=(c == 3))
                        xc_ = dg * 4 + dc
                        P.op("dve", lambda h, xc_=xc_, ts=ts, bank=bank: h.tensor_tensor(
                            out=xT[:, xc_, ts], in0=xT[:, xc_, ts], in1=bank[:], op=ALU.add),
                            reads=[rb], writes=[self.r_xT[xc_]])
        P.barrier()
        P.flush()


Core.nsa_attn = _nsa_attn


import ml_dtypes
BF = ml_dtypes.bfloat16
NCK = SEQ // TOK


class View:
    def __init__(self, ap):
        self._ap = ap

    def ap(self):
        return self._ap


def _pc(v):
    return np.ascontiguousarray(np.asarray(v, np.float32).reshape(NCH, 128).T)


def _split3(v):
    v = np.float32(v)
    hi = np.float32(BF(v))
    mid = np.float32(BF(np.float32(v - hi)))
    lo = np.float32(BF(np.float32(v - hi - mid)))
    return hi, mid, lo


def _consts():
    k = np.arange(128)
    kk, tt = k[:, None], k[None, :]
    tri = (kk < tt).astype(np.float32)
    out = {"ident": np.eye(128, dtype=np.float32), "negtri": -(kk >= tt).astype(np.float32),
           "negones": np.full((128, 128), -1.0, np.float32), "m01": tri, "nm": (tri - 1.0) * 30000.0,
           "nm_incl": np.where(kk <= tt, 0.0, NEG), "nm_after": np.where(kk > tt, 0.0, NEG)}
    out = {n: v.astype(BF) for n, v in out.items()}
    slopes = np.exp2(-8.0 * np.arange(1, HEADS + 1, dtype=np.float32) / HEADS).astype(np.float32)
    tok = np.arange(SEQ)
    aR = np.zeros((6, SEQ), np.float32)
    aR[0:3] = -128.0 * (tok // 128)
    aR[3:6] = -1.0 * (tok % 128)
    aL = np.zeros((6, HEADS, 128), np.float32)
    for h in range(HEADS):
        sp = _split3(slopes[h])
        for r in range(6):
            aL[r, h, :] = sp[r % 3]
    s64 = slopes.astype(np.float64)
    bKI = (s64[None, :, None] * (128.0 * np.arange(32)[None, None, :] + k[:, None, None])).astype(np.float32)
    bC = (s64[None, :, None] * (16.0 * (128.0 * np.arange(2)[None, None, :] + k[:, None, None]) + 31.0)).astype(np.float32)
    n = (128 * np.arange(2)[None, :] + k[:, None])
    valid_c = (n[:, :, None] <= 254) & (tok[None, None, :] >= 16 * n[:, :, None] + 31)
    nmc = np.where(valid_c, 0.0, NEG).astype(np.float32)
    nn = np.arange(256)
    jj = np.arange(64)
    ov = np.clip(np.minimum(16 * nn[:, None] + 32, 64 * jj[None, :] + 64) - np.maximum(16 * nn[:, None], 64 * jj[None, :]), 0, None) / 32.0
    ov[255] = 0.0
    ovl = np.ascontiguousarray(ov.reshape(2, 128, 64).transpose(1, 0, 2)).astype(np.float32)
    tq = tok.reshape(NCK, NQB, 128).transpose(0, 2, 1)
    cur = tq // 64
    J = jj[None, None, None, :]
    fut = J > cur[..., None]
    forced = ((J == 0) | (J == cur[..., None]) | (J == cur[..., None] - 1)) & ~fut
    keep = (~(forced | fut)).astype(np.float32)
    addc = np.where(fut, -1.0 - 1e-3 * J, np.where(forced, 1e4 + (64.0 - J), 0.0)).astype(np.float32)
    eexp = ((128 * np.arange(32)[None, :, None] + k[None, None, :]) // 64 == jj[:, None, None]).astype(np.float32)
    sel3 = np.zeros((3, 3, 128), np.float32)
    for r in range(3):
        sel3[r, r, :] = 1.0
    aLx = np.ascontiguousarray(np.broadcast_to(aL.transpose(1, 0, 2)[:, :, None, :], (HEADS, 6, 32, 128)))
    out.update({"alibiL": aL.astype(BF), "alibiLx": aLx.astype(BF),
                "alibiR": np.ascontiguousarray(aR.reshape(6, NCK, TOK).transpose(1, 0, 2)).astype(BF),
                "biasKI": bKI, "biasC": bC,
                "negmask_c": np.ascontiguousarray(nmc.reshape(128, 2, NCK, TOK).transpose(2, 0, 1, 3)).astype(BF),
                "overlap": ovl.astype(BF), "keep01": np.ascontiguousarray(keep), "addc": np.ascontiguousarray(addc),
                "eexp": eexp.astype(BF), "sel3": sel3})
    return out


_CONST_SPECS = (("ident", [128, 128], BF16), ("negtri", [128, 128], BF16), ("negones", [128, 128], BF16),
                ("m01", [128, 128], BF16), ("nm", [128, 128], BF16), ("nm_incl", [128, 128], BF16),
                ("nm_after", [128, 128], BF16), ("alibiL", [6, HEADS, 128], BF16), ("alibiLx", [HEADS, 6, 32, 128], BF16), ("alibiR", [NCK, 6, TOK], BF16),
                ("biasKI", [128, HEADS, 32], F32), ("biasC", [128, HEADS, 2], F32),
                ("negmask_c", [NCK, 128, 2, TOK], BF16), ("overlap", [128, 2, 64], BF16),
                ("keep01", [NCK, 128, NQB, 64], F32), ("addc", [NCK, 128, NQB, 64], F32),
                ("eexp", [64, 32, 128], BF16), ("sel3", [3, 3, 128], F32))
_W_SPECS = (("sb_g", [2, 128, NCH]), ("sb_wqkv", [2, D_MODEL, 3 * D_MODEL]), ("sb_wout", [2, D_MODEL, D_MODEL]),
            ("nsa_g", [2, 128, NCH]), ("nsa_win", [2, D_MODEL, 5168]), ("nsa_qg", [2, 128, 1]), ("nsa_kg", [2, 128, 3]),
            ("nsa_gb", [2, 48, 1]), ("cw1", [2, 2, 4096, 128]), ("cw2", [2, 2, 128, 128]), ("peT", [2, 2, 128, 32]),
            ("kg0", [2, 128, 1]), ("nsa_wout", [2, D_MODEL, D_MODEL]), ("mlp_g", [4, 128, NCH]),
            ("mlp_w1", [4, D_MODEL, D_FF]), ("mlp_w2", [4, D_FF, D_MODEL]))

_DEBUG_LAYERS = None
_DBG_PARTS = None


def _build_fused(C, depth):
    P = C.P
    xin = C.din("xin", [NCK, D_MODEL, TOK])
    K = {n: C.din(n, sh, dt) for n, sh, dt in _CONST_SPECS}
    W = {n: C.din(n, sh, F32) for n, sh in _W_SPECS}
    yout = C.dout("yout", [NCK, D_MODEL, TOK])
    xs = C.dscratch("xs", [NCK, D_MODEL, TOK])
    qs = C.dscratch("qs", [NCK, HEADS, 128, TOK], BF16)
    KT = C.dscratch("KT", [HEADS, 128, SEQ], BF16)
    Vv = C.dscratch("Vv", [SEQ, D_MODEL], BF16)
    kv = {n: C.dscratch(n, [4, 128, SEQ], BF16) for n in ("KC", "VC", "KS", "KW")}
    kv.update({n: C.dscratch(n, [SEQ, 512], BF16) for n in ("VS", "VW")})
    gts = C.dscratch("gts", [NCK, 48, TOK], F32)
    C.init_consts(K["ident"])
    for layer in range(depth):
        slot = layer // 2
        src = xin if layer == 0 else xs
        dst = yout if layer == depth - 1 else xs
        if layer % 2 == 0:
            for ck in range(NCK):
                tsl = slice(ck * TOK, (ck + 1) * TOK)
                C.load_x(src.ap()[ck])
                C.sb_proj(W["sb_g"].ap()[slot], W["sb_wqkv"].ap()[slot], View(qs.ap()[ck]), View(KT.ap()[:, :, tsl]),
                          View(Vv.ap()[tsl, :]))
            for ck in range(NCK):
                C.load_x(src.ap()[ck])
                C.sb_attn(ck, View(qs.ap()[ck]), KT, Vv, K, W["sb_wout"].ap()[slot])
                C.mlp(W["mlp_g"].ap()[layer], W["mlp_w1"].ap()[layer], W["mlp_w2"].ap()[layer])
                C.store_x(dst.ap()[ck])
                P.barrier()
                P.flush()
        else:
            for ck in range(NCK):
                tsl = slice(ck * TOK, (ck + 1) * TOK)
                C.load_x(src.ap()[ck])
                outs = {"qn": View(qs.ap()[ck]), "gates": View(gts.ap()[ck]),
                        "kc": View(kv["KC"].ap()[:, :, tsl]), "vc": View(kv["VC"].ap()[:, :, tsl]),
                        "ks": View(kv["KS"].ap()[:, :, tsl]), "kw": View(kv["KW"].ap()[:, :, tsl]),
                        "vs": View(kv["VS"].ap()[tsl, :]), "vw": View(kv["VW"].ap()[tsl, :])}
                C.nsa_proj(W["nsa_g"].ap()[slot], W["nsa_win"].ap()[slot], W["nsa_qg"].ap()[slot], W["nsa_kg"].ap()[slot],
                           W["nsa_gb"].ap()[slot], outs)
            D = dict(K)
            D.update(kv)
            D["cw1"] = View(W["cw1"].ap()[slot])
            D["cw2"] = View(W["cw2"].ap()[slot])
            D["peT"] = View(W["peT"].ap()[slot])
            D["kg0"] = View(W["kg0"].ap()[slot])
            C.nsa_compress(D)
            for ck in range(NCK):
                C.load_x(src.ap()[ck])
                D["qn"] = View(qs.ap()[ck])
                D["gates"] = View(gts.ap()[ck])
                C.nsa_attn(ck, D, W["nsa_wout"].ap()[slot])
                C.mlp(W["mlp_g"].ap()[layer], W["mlp_w1"].ap()[layer], W["mlp_w2"].ap()[layer])
                C.store_x(dst.ap()[ck])
                P.barrier()
                P.flush()


_PROGS = {}


def _prog(depth):
    if depth not in _PROGS:
        nc = bass.Bass("TRN2", target_bir_lowering=False)
        C = Core(nc)
        _build_fused(C, depth)
        C.P.barrier()
        C.P.flush()
        C.P.close()
        _PROGS[depth] = nc
    return _PROGS[depth]


def kernel(x, sb_norm_g, sb_w_qkv, sb_w_out, nsa_norm_g, nsa_w_in, nsa_gate_b, nsa_q_norm_g, nsa_k_norm_g,
           nsa_cmp_pe, nsa_cmp_w1, nsa_cmp_w2, nsa_w_out, mlp_norm_g, mlp_w1, mlp_w2):
    f32 = lambda a: np.ascontiguousarray(np.asarray(a, np.float32))
    x = f32(x)
    depth = 4 if _DEBUG_LAYERS is None else _DEBUG_LAYERS
    shared = dict(_consts())
    kgT = f32(np.asarray(nsa_k_norm_g).transpose(0, 2, 1))
    shared.update({
        "sb_g": np.stack([_pc(g) for g in np.asarray(sb_norm_g)]), "sb_wqkv": f32(sb_w_qkv), "sb_wout": f32(sb_w_out),
        "nsa_g": np.stack([_pc(g) for g in np.asarray(nsa_norm_g)]), "nsa_win": f32(nsa_w_in),
        "nsa_qg": f32(nsa_q_norm_g).reshape(2, 128, 1), "nsa_kg": kgT, "nsa_gb": f32(nsa_gate_b).reshape(2, 48, 1),
        "cw1": f32(nsa_cmp_w1), "cw2": f32(nsa_cmp_w2), "peT": f32(np.asarray(nsa_cmp_pe).transpose(0, 1, 3, 2)),
        "kg0": f32(kgT[:, :, 0:1]), "nsa_wout": f32(nsa_w_out),
        "mlp_g": np.stack([_pc(g) for g in np.asarray(mlp_norm_g)]), "mlp_w1": f32(mlp_w1), "mlp_w2": f32(mlp_w2)})
    xin = [np.ascontiguousarray(x[b].reshape(NCK, TOK, D_MODEL).transpose(0, 2, 1)) for b in range(BATCH)]
    in_maps = []
    for c in range(NCORES):
        m = dict(shared)
        m["xin"] = xin[c // 4]
        in_maps.append(m)
    nc = _prog(depth)
    res = run_bass_kernel_spmd(nc, in_maps, core_ids=list(range(NCORES))).results
    out = np.empty((BATCH, SEQ, D_MODEL), np.float32)
    for b in range(BATCH):
        out[b] = res[4 * b]["yout"].transpose(0, 2, 1).reshape(SEQ, D_MODEL)
    return out
```

```python
import numpy as np
from contextlib import ExitStack
import concourse.bass as bass
import concourse.mybir as mybir
from concourse.bass_utils import run_bass_kernel_spmd

F32 = mybir.dt.float32
BF16 = mybir.dt.bfloat16
ALU = mybir.AluOpType
AF = mybir.ActivationFunctionType
AX = mybir.AxisListType

D_MODEL = 2048
SEQ = 4096
BATCH = 2
NCORES = 8
TOK = 1024
NQB = 8
NCH = 16
D_FF = 8192
HEADS = 16
DH = 128
EPS = 1e-6
SCALE = DH ** -0.5
NEG = -30000.0

SEM_CAP = 30000


def core_blocks(j):
    return [j, j + 4, j + 8, j + 12, 19 - j, 23 - j, 27 - j, 31 - j]


class Reg:
    __slots__ = ("w", "rs", "name")

    def __init__(self, name=""):
        self.w = None
        self.rs = {}
        self.name = name


class Eng:
    def __init__(self, prog, name, kind):
        self.prog = prog
        self.name = name
        self.kind = kind
        self.items = []
        self.sem = None
        self.count = 0
        self.waited = {}
        self.slots = []
        self.slot_i = 0

    def new_sem(self):
        self.sem = self.prog.new_sem(self.name)
        self.count = 0


class Prog:
    NSLOT = 6

    def __init__(self, nc):
        self.nc = nc
        self.stack = ExitStack()
        self.nsem = 0
        self.engs = {}
        for name in ("pe", "act", "dve", "pool", "sp"):
            e = Eng(self, name, name)
            self.engs[name] = e
            e.new_sem()
        for name in ("act", "pool", "sp"):
            e = self.engs[name]
            e.slots = [[self.new_sem(f"{name}_dma{i}"), 0] for i in range(self.NSLOT)]
        self.nops = 0

    def new_sem(self, name):
        self.nsem += 1
        return self.stack.enter_context(self.nc.semaphore(f"s{self.nsem}_{name}"))

    def sbuf(self, name, shape, dtype, stack=None):
        st = stack if stack is not None else self.stack
        self.uid = getattr(self, "uid", 0) + 1
        return st.enter_context(self.nc.sbuf_tensor(f"{name}_u{self.uid}", list(shape), dtype))

    def psum(self, name, shape, dtype, stack=None):
        st = stack if stack is not None else self.stack
        return st.enter_context(self.nc.psum_tensor(name, list(shape), dtype))

    def _deps(self, eng, reads, writes):
        toks = []
        for r in reads:
            if r.w is not None:
                toks.append(r.w)
        for r in writes:
            if r.w is not None:
                toks.append(r.w)
            toks.extend(r.rs.values())
        return toks

    def _emit_waits(self, eng, toks):
        for tok in toks:
            sem, val, src = tok
            if src is eng and eng.kind == "pe":
                continue
            key = id(sem)
            if eng.waited.get(key, 0) >= val:
                continue
            eng.waited[key] = val
            eng.items.append(("wait", sem, val))

    def _finish(self, tok, reads, writes):
        for r in reads:
            k = id(tok[0])
            old = r.rs.get(k)
            if old is None or old[1] < tok[1]:
                r.rs[k] = tok
        for r in writes:
            r.w = tok
            r.rs = {}

    def op(self, engname, fn, reads=(), writes=(), signal=True):
        eng = self.engs[engname]
        self.nops += 1
        self._emit_waits(eng, self._deps(eng, reads, writes))
        if eng.count >= SEM_CAP:
            eng.new_sem()
        if signal:
            eng.count += 1
            tok = (eng.sem, eng.count, eng)
            eng.items.append(("op", fn, eng.sem, 1))
        else:
            tok = (eng.sem, eng.count + 1, eng)
            eng.items.append(("op", fn, None, 0))
        self._finish(tok, reads, writes)
        return tok

    def dma(self, qname, fn, reads=(), writes=()):
        eng = self.engs[qname]
        self.nops += 1
        slot = eng.slots[eng.slot_i]
        eng.slot_i = (eng.slot_i + 1) % len(eng.slots)
        if slot[1] >= SEM_CAP:
            slot[0] = self.new_sem(f"{qname}_dma")
            slot[1] = 0
        toks = self._deps(eng, reads, writes)
        if slot[1] > 0:
            toks.append((slot[0], slot[1], None))
        self._emit_waits(eng, toks)
        slot[1] += 16
        tok = (slot[0], slot[1], None)
        eng.items.append(("op", fn, slot[0], 16))
        self._finish(tok, reads, writes)
        return tok

    def barrier(self):
        toks = []
        for e in self.engs.values():
            if e.count > 0:
                toks.append((e.sem, e.count, None))
            for s in e.slots:
                if s[1] > 0:
                    toks.append((s[0], s[1], None))
        for e in self.engs.values():
            self._emit_waits(e, toks)

    def final_wait(self, toks, engname="sp"):
        self._emit_waits(self.engs[engname], toks)

    def flush(self):
        nc = self.nc

        def run(items, h):
            for it in items:
                if it[0] == "wait":
                    h.wait_ge(it[1], it[2])
                else:
                    ins = it[1](h)
                    if it[2] is not None:
                        ins.then_inc(it[2], it[3])

        with nc.Block() as block:
            @block.tensor
            def _(h):
                run(self.engs["pe"].items, h)

            @block.scalar
            def _(h):
                run(self.engs["act"].items, h)

            @block.vector
            def _(h):
                run(self.engs["dve"].items, h)

            @block.gpsimd
            def _(h):
                run(self.engs["pool"].items, h)

            @block.sync
            def _(h):
                run(self.engs["sp"].items, h)
        for e in self.engs.values():
            e.items = []

    def close(self):
        self.stack.close()


class Core:
    def __init__(self, nc):
        self.nc = nc
        self.P = Prog(nc)
        P = self.P
        self.xT = P.sbuf("xT_sb", [128, NCH, TOK], F32)
        self.r_xT = [Reg(f"xT{c}") for c in range(NCH)]
        self.ones = P.sbuf("ones_bf", [128, 128], BF16)
        self.r_ones = Reg("ones")
        self.ident = P.sbuf("ident_bf", [128, 128], BF16)
        self.r_ident = Reg("ident")
        self.wbuf = [P.sbuf(f"wbuf{i}", [128, NCH, 512], BF16) for i in range(2)]
        self.r_wbuf = [[Reg(f"wbuf{i}_{k}") for k in range(2)] for i in range(2)]
        self.wi = 0
        self.banks = [P.psum(f"bank{i}", [128, 512], F32) for i in range(7)]
        self.r_banks = [Reg(f"bank{i}") for i in range(7)]
        self.tbank = P.psum("tbank", [128, 1024], BF16)
        self.r_tbank = Reg("tbank")
        self.dram = {}

    def din(self, name, shape, dtype=F32):
        t = self.nc.dram_tensor(name, list(shape), dtype, kind="ExternalInput")
        self.dram[name] = t
        return t

    def dout(self, name, shape, dtype=F32):
        t = self.nc.dram_tensor(name, list(shape), dtype, kind="ExternalOutput")
        self.dram[name] = t
        return t

    def dscratch(self, name, shape, dtype=F32):
        t = self.nc.dram_tensor(name, list(shape), dtype, kind="Internal")
        self.dram[name] = t
        return t

    def init_consts(self, ident_dram):
        P = self.P
        ones = self.ones
        P.op("pool", lambda h: h.memset(ones[:], 1.0), writes=[self.r_ones])
        ident = self.ident
        P.dma("pool", lambda h: h.dma_start(out=ident[:], in_=ident_dram.ap()), writes=[self.r_ident])

    def load_x(self, x_dram):
        P = self.P
        xT = self.xT
        src = x_dram.rearrange("(c p) t -> p c t", p=128)
        for c0 in range(0, NCH, 4):
            P.dma("sp", lambda h, c0=c0: h.dma_start(out=xT[:, c0:c0 + 4, :], in_=src[:, c0:c0 + 4, :]),
                  writes=self.r_xT[c0:c0 + 4])

    def store_x(self, y_dram):
        P = self.P
        xT = self.xT
        dst = y_dram.rearrange("(c p) t -> p c t", p=128)
        toks = []
        for c0 in range(0, NCH, 4):
            toks.append(P.dma("sp", lambda h, c0=c0: h.dma_start(out=dst[:, c0:c0 + 4, :], in_=xT[:, c0:c0 + 4, :]),
                              reads=self.r_xT[c0:c0 + 4]))
        return toks

    def next_wbuf(self):
        i = self.wi
        self.wi ^= 1
        return self.wbuf[i], self.r_wbuf[i]

    def load_w(self, src_ap, ncols=512, nch=NCH):
        P = self.P
        wb, rw = self.next_wbuf()
        src = src_ap.rearrange("(c p) f -> p c f", p=128)
        hc = max(nch // 2, 1)
        for c0 in range(0, nch, hc):
            P.dma("pool", lambda h, c0=c0: h.dma_start(out=wb[:, c0:c0 + hc, 0:ncols], in_=src[:, c0:c0 + hc, :]),
                  writes=[rw[c0 // 8]] if nch == NCH else rw)
        return wb, rw

    def rmsnorm(self, g_sb, r_g, xn, r_xn, st):
        P = self.P
        xT, ones = self.xT, self.ones
        sq = [P.sbuf(f"rn_sq{i}", [128, 512], BF16, st) for i in range(2)]
        r_sq = [Reg() for _ in range(2)]
        lnv = P.sbuf("rn_ln", [128, 512], F32, st)
        r_ln = Reg()
        rstd = [P.sbuf(f"rn_rstd{i}", [128, 512], F32, st) for i in range(2)]
        r_rstd = [Reg() for _ in range(2)]
        for th in range(2):
            ts = slice(th * 512, (th + 1) * 512)
            bank, rb = self.banks[th], self.r_banks[th]
            for c in range(NCH):
                s, rs = sq[c % 2], r_sq[c % 2]
                P.op("act", lambda h, s=s, c=c, ts=ts: h.activation(out=s[:], in_=xT[:, c, ts], func=AF.Square),
                     reads=[self.r_xT[c]], writes=[rs])
                P.op("pe", lambda h, s=s, c=c, bank=bank: h.matmul(bank[:], ones[:], s[:], start=(c == 0), stop=(c == NCH - 1)),
                     reads=[rs, self.r_ones], writes=[rb])
            P.op("act", lambda h, bank=bank: h.activation(out=lnv[:], in_=bank[:], func=AF.Ln, bias=EPS, scale=1.0 / D_MODEL),
                 reads=[rb], writes=[r_ln])
            rs_t, r_rs = rstd[th], r_rstd[th]
            P.op("act", lambda h, rs_t=rs_t: h.activation(out=rs_t[:], in_=lnv[:], func=AF.Exp, scale=-0.5),
                 reads=[r_ln], writes=[r_rs])
            for c in range(NCH):
                P.op("dve", lambda h, c=c, ts=ts, rs_t=rs_t: h.scalar_tensor_tensor(
                    out=xn[:, c, ts], in0=xT[:, c, ts], scalar=g_sb[:, c:c + 1], in1=rs_t[:], op0=ALU.mult, op1=ALU.mult),
                    reads=[self.r_xT[c], r_rs, r_g], writes=[r_xn[c]])

    def mlp(self, g_dram_row, w1_ap, w2_ap):
        P = self.P
        with ExitStack() as st:
            g_sb = P.sbuf("mlp_g", [128, NCH], F32, st)
            r_g = Reg()
            P.dma("sp", lambda h: h.dma_start(out=g_sb[:], in_=g_dram_row),
                  writes=[r_g])
            xn = P.sbuf("mlp_xn", [128, NCH, TOK], BF16, st)
            r_xn = [Reg() for _ in range(NCH)]
            self.rmsnorm(g_sb, r_g, xn, r_xn, st)
            h1 = P.sbuf("mlp_h1", [128, 16, TOK], BF16, st)
            r_h1 = [Reg() for _ in range(16)]
            rl = [P.sbuf(f"mlp_r{i}", [128, 512], F32, st) for i in range(2)]
            r_rl = [Reg() for _ in range(2)]
            xT = self.xT
            nb = 0
            nr = 0
            for fq in range(4):
                for fg in range(4):
                    f0 = fq * 2048 + fg * 512
                    wb, rw = self.load_w(w1_ap[:, f0:f0 + 512])
                    for fc in range(4):
                        for th in range(2):
                            ts = slice(th * 512, (th + 1) * 512)
                            bank, rb = self.banks[2 + nb % 4], self.r_banks[2 + nb % 4]
                            nb += 1
                            for c in range(NCH):
                                P.op("pe", lambda h, wb=wb, c=c, fc=fc, ts=ts, bank=bank: h.matmul(
                                    bank[:], wb[:, c, fc * 128:(fc + 1) * 128], xn[:, c, ts], start=(c == 0), stop=(c == NCH - 1)),
                                    reads=[rw[c // 8], r_xn[c]], writes=[rb], signal=(c == NCH - 1))
                            r_t, r_r = rl[nr % 2], r_rl[nr % 2]
                            nr += 1
                            P.op("act", lambda h, r_t=r_t, bank=bank: h.activation(out=r_t[:], in_=bank[:], func=AF.Relu),
                                 reads=[rb], writes=[r_r])
                            hc = fg * 4 + fc
                            P.op("dve", lambda h, r_t=r_t, hc=hc, ts=ts: h.tensor_tensor(
                                out=h1[:, hc, ts], in0=r_t[:], in1=r_t[:], op=ALU.mult),
                                reads=[r_r], writes=[r_h1[hc]])
                for dg in range(4):
                    wb, rw = self.load_w(w2_ap[fq * 2048:(fq + 1) * 2048, dg * 512:(dg + 1) * 512])
                    for dc in range(4):
                        for th in range(2):
                            ts = slice(th * 512, (th + 1) * 512)
                            bank, rb = self.banks[2 + nb % 4], self.r_banks[2 + nb % 4]
                            nb += 1
                            for c in range(16):
                                P.op("pe", lambda h, wb=wb, c=c, dc=dc, ts=ts, bank=bank: h.matmul(
                                    bank[:], wb[:, c, dc * 128:(dc + 1) * 128], h1[:, c, ts], start=(c == 0), stop=(c == 15)),
                                    reads=[rw[c // 8], r_h1[c]], writes=[rb], signal=(c == 15))
                            xc = dg * 4 + dc
                            P.op("dve", lambda h, xc=xc, ts=ts, bank=bank: h.tensor_tensor(
                                out=xT[:, xc, ts], in0=xT[:, xc, ts], in1=bank[:], op=ALU.add),
                                reads=[rb], writes=[self.r_xT[xc]])
            P.barrier()
            P.flush()

    def load_small(self, name, dram_ap, shape, dtype, st, q="sp"):
        P = self.P
        t = P.sbuf("s_" + name, shape, dtype, st)
        r = Reg(name)
        P.dma(q, lambda h: h.dma_start(out=t[:], in_=dram_ap), writes=[r])
        return t, r

    def proj_fm(self, w_ap, xn, r_xn, ncols, evac):
        P = self.P
        nb = 0
        for g0 in range(0, ncols, 512):
            wb, rw = self.load_w(w_ap[:, g0:g0 + 512])
            for jc in range(4):
                for th in range(2):
                    ts = slice(th * 512, (th + 1) * 512)
                    bank, rb = self.banks[nb % 3], self.r_banks[nb % 3]
                    nb += 1
                    for c in range(NCH):
                        P.op("pe", lambda h, wb=wb, c=c, jc=jc, ts=ts, bank=bank: h.matmul(
                            bank[:], wb[:, c, jc * 128:(jc + 1) * 128], xn[:, c, ts], start=(c == 0), stop=(c == NCH - 1)),
                            reads=[rw[c // 8], r_xn[c]], writes=[rb], signal=(c == NCH - 1))
                    evac(g0 // 128 + jc, th, bank, rb)

    def proj_tm(self, w_ap, xn, r_xn, ncols, evac):
        P = self.P
        nb = 0
        for g0 in range(0, ncols, 512):
            wb, rw = self.load_w(w_ap[:, g0:g0 + 512])
            for tt in range(NQB):
                bank, rb = self.banks[nb % 3], self.r_banks[nb % 3]
                nb += 1
                for c in range(NCH):
                    P.op("pe", lambda h, wb=wb, c=c, tt=tt, bank=bank: h.matmul(
                        bank[:], xn[:, c, tt * 128:(tt + 1) * 128], wb[:, c, :], start=(c == 0), stop=(c == NCH - 1)),
                        reads=[rw[c // 8], r_xn[c]], writes=[rb], signal=(c == NCH - 1))
                evac(g0 // 512, tt, bank, rb)

    def out_proj(self, w_ap, oT, r_oT):
        P = self.P
        xT = self.xT
        nb = 0
        for dg in range(4):
            wb, rw = self.load_w(w_ap[:, dg * 512:(dg + 1) * 512])
            for dc in range(4):
                for th in range(2):
                    ts = slice(th * 512, (th + 1) * 512)
                    bank, rb = self.banks[nb % 3], self.r_banks[nb % 3]
                    nb += 1
                    for c in range(16):
                        P.op("pe", lambda h, wb=wb, c=c, dc=dc, ts=ts, bank=bank: h.matmul(
                            bank[:], wb[:, c, dc * 128:(dc + 1) * 128], oT[:, c, ts], start=(c == 0), stop=(c == 15)),
                            reads=[rw[c // 8], r_oT[c]], writes=[rb], signal=(c == 15))
                    xc = dg * 4 + dc
                    P.op("dve", lambda h, xc=xc, ts=ts, bank=bank: h.tensor_tensor(
                        out=xT[:, xc, ts], in0=xT[:, xc, ts], in1=bank[:], op=ALU.add),
                        reads=[rb], writes=[self.r_xT[xc]])

    def sb_proj(self, g_ap, wqkv_ap, qT_out, kT_out, v_out):
        P = self.P
        with ExitStack() as st:
            g_sb, r_g = self.load_small("sb_g", g_ap, [128, NCH], F32, st)
            xn = P.sbuf("sb_xn", [128, NCH, TOK], BF16, st)
            r_xn = [Reg() for _ in range(NCH)]
            self.rmsnorm(g_sb, r_g, xn, r_xn, st)
            stg = [P.sbuf(f"sb_stg{i}", [128, 512], BF16, st) for i in range(4)]
            r_stg = [Reg() for _ in range(4)]
            cnt = [0]
            outs = []

            def evac_fm(dst, scale):
                def f(j, th, bank, rb):
                    i = cnt[0] % 4
                    cnt[0] += 1
                    s, rs = stg[i], r_stg[i]
                    P.op("act", lambda h: h.activation(out=s[:], in_=bank[:], func=AF.Copy, scale=scale),
                         reads=[rb], writes=[rs])
                    outs.append(P.dma("sp", lambda h: h.dma_start(out=dst.ap()[j, :, th * 512:(th + 1) * 512], in_=s[:]),
                                      reads=[rs]))
                return f

            def evac_tm(g, tt, bank, rb):
                i = cnt[0] % 4
                cnt[0] += 1
                s, rs = stg[i], r_stg[i]
                P.op("dve", lambda h: h.tensor_copy(out=s[:], in_=bank[:]), reads=[rb], writes=[rs])
                outs.append(P.dma("sp", lambda h: h.dma_start(
                    out=v_out.ap()[tt * 128:(tt + 1) * 128, g * 512:(g + 1) * 512], in_=s[:]), reads=[rs]))

            self.proj_fm(wqkv_ap[:, 0:2048], xn, r_xn, 2048, evac_fm(qT_out, SCALE))
            self.proj_fm(wqkv_ap[:, 2048:4096], xn, r_xn, 2048, evac_fm(kT_out, 1.0))
            self.proj_tm(wqkv_ap[:, 4096:6144], xn, r_xn, 2048, evac_tm)
            P.barrier()
            P.flush()
        return outs

    def sb_attn(self, ck, qT_in, KT_all, V_all, consts, wout_ap):
        P = self.P
        with ExitStack() as st:
            negtri, r_negtri = self.load_small("negtri", consts["negtri"].ap(), [128, 128], BF16, st)
            negones, r_negones = self.load_small("negones", consts["negones"].ap(), [128, 128], BF16, st)
            m01, r_m01 = self.load_small("m01", consts["m01"].ap(), [128, 128], BF16, st)
            nm, r_nm = self.load_small("nm", consts["nm"].ap(), [128, 128], BF16, st)
            nk = 8 * ck + 8
            ident, r_ident = self.ident, self.r_ident
            qq = [P.sbuf(f"sb_q{i}", [128, TOK], BF16, st) for i in range(2)]
            r_qq = [Reg() for _ in range(2)]
            oT = P.sbuf("sb_oT", [128, HEADS, TOK], BF16, st)
            r_oT = [Reg() for _ in range(HEADS)]
            kt = [P.sbuf(f"sb_kt{i}", [128, SEQ], BF16, st) for i in range(2)]
            r_kt = [Reg() for _ in range(2)]
            vv = [P.sbuf(f"sb_v{i}", [128, 32, 128], BF16, st) for i in range(2)]
            r_vv = [Reg() for _ in range(2)]
            E = [P.sbuf(f"sb_E{i}", [128, 512], F32, st) for i in range(1)]
            r_E = [Reg() for _ in range(1)]
            SP = [P.sbuf(f"sb_SP{i}", [128, 512], BF16, st) for i in range(3)]
            r_SP = [Reg() for _ in range(3)]
            PP = [P.sbuf(f"sb_P{i}", [128, 512], BF16, st) for i in range(3)]
            r_PP = [Reg() for _ in range(3)]
            LS = [P.sbuf(f"sb_LS{i}", [128, 512], BF16, st) for i in range(2)]
            r_LS = [Reg() for _ in range(2)]
            E0, r_E0 = E[0], r_E[0]

            def load_head(hd):
                ktb, r_ktb = kt[hd % 2], r_kt[hd % 2]
                vb, r_vb = vv[hd % 2], r_vv[hd % 2]
                qb, r_qb = qq[hd % 2], r_qq[hd % 2]
                P.dma("sp", lambda h: h.dma_start(out=ktb[:, 0:nk * 128], in_=KT_all.ap()[hd][:, 0:nk * 128]), writes=[r_ktb])
                P.dma("sp", lambda h: h.dma_start(out=qb[:], in_=qT_in.ap()[hd]), writes=[r_qb])
                hk = nk // 2
                for half in range(2):
                    P.dma("sp", lambda h, half=half: h.dma_start(
                        out=vb[:, half * hk:(half + 1) * hk, :],
                        in_=V_all.ap()[half * hk * 128:(half + 1) * hk * 128, hd * 128:(hd + 1) * 128].rearrange("(i k) d -> k i d", k=128)),
                        writes=[r_vb])

            def mk_step(hd, grp, i, b0, n, first_av, last, obank, r_ob, ls, r_ls, first_of_head):
                ktb, r_ktb = kt[hd % 2], r_kt[hd % 2]
                vb, r_vb = vv[hd % 2], r_vv[hd % 2]
                qb, r_qb = qq[hd % 2], r_qq[hd % 2]
                p0 = grp * 4
                smin = max(0, i - b0)
                c_lo = smin * 128
                cols = slice(c_lo, 512)
                gcols = slice(p0 * 128 + c_lo, p0 * 128 + 512)
                has_top = i >= b0
                tcols = slice(c_lo, c_lo + 128)
                cc_lo = c_lo + (128 if has_top else 0)
                ccols = slice(cc_lo, 512)
                abank, r_ab = self.banks[n % 3], self.r_banks[n % 3]
                sp, r_sp = SP[n % 3], r_SP[n % 3]
                pp, r_pp = PP[n % 3], r_PP[n % 3]

                def A():
                    if first_of_head == 0 and hd == 0:
                        load_head(0)
                    if first_of_head == 3 and hd + 1 < HEADS:
                        load_head(hd + 1)
                    P.op("pe", lambda h: h.matmul(abank[:, cols], ktb[:, i * 128:(i + 1) * 128], qb[:, gcols], start=True, stop=True),
                         reads=[r_ktb, r_qb], writes=[r_ab])
                    P.op("act", lambda h: h.activation(out=E0[:, cols], in_=abank[:, cols], func=AF.Exp), reads=[r_ab], writes=[r_E0])
                    P.op("act", lambda h: h.activation(out=sp[:, cols], in_=E0[:, cols], func=AF.Ln, bias=1.0), reads=[r_E0], writes=[r_sp])
                    if has_top:
                        P.op("pool", lambda h: h.tensor_tensor(out=sp[:, tcols], in0=sp[:, tcols], in1=m01[:], op=ALU.mult),
                             reads=[r_m01], writes=[r_sp])

                def B():
                    P.op("pe", lambda h: h.matmul(abank[:, cols], negtri[:], sp[:, cols], start=False, stop=True, skip_group_check=True),
                         reads=[r_sp, r_negtri], writes=[r_ab])
                    if cc_lo < 512:
                        P.op("pe", lambda h: h.matmul(abank[:, ccols], negones[:], ls[:, ccols], start=False, stop=True, skip_group_check=True),
                             reads=[r_ls, r_negones], writes=[r_ab])
                    if has_top:
                        P.op("pe", lambda h: h.matmul(abank[:, tcols], ident[:], nm[:], start=False, stop=True, skip_group_check=True),
                             reads=[r_nm, r_ident], writes=[r_ab])
                    P.op("act", lambda h: h.activation(out=pp[:, cols], in_=abank[:, cols], func=AF.Exp), reads=[r_ab], writes=[r_pp])
                    if has_top:
                        P.op("dve", lambda h: h.tensor_copy(out=ls[:, tcols], in_=sp[:, tcols]), reads=[r_sp], writes=[r_ls])
                    if cc_lo < 512 and i > 0:
                        P.op("dve", lambda h: h.tensor_tensor(out=ls[:, ccols], in0=ls[:, ccols], in1=sp[:, ccols], op=ALU.add),
                             reads=[r_sp], writes=[r_ls])

                def C():
                    P.op("pe", lambda h: h.matmul(obank[:, cols], vb[:, i, :], pp[:, cols], start=first_av, stop=True, skip_group_check=True),
                         reads=[r_vb, r_pp], writes=[r_ob])
                    if last:
                        P.op("dve", lambda h: h.tensor_copy(out=oT[:, hd, p0 * 128:p0 * 128 + 512], in_=obank[:]),
                             reads=[r_ob], writes=[r_oT[hd]])
                return A, B, C

            steps = []
            nO = 0
            for hd in range(HEADS):
                kh = 0
                for grp in range(2):
                    b0 = 8 * ck + 4 * grp
                    obank, r_ob = self.banks[3 + nO % 2], self.r_banks[3 + nO % 2]
                    ls, r_ls = LS[nO % 2], r_LS[nO % 2]
                    nO += 1
                    for i in range(b0 + 3, -1, -1):
                        steps.append(mk_step(hd, grp, i, b0, len(steps), i == b0 + 3, i == 0, obank, r_ob, ls, r_ls, kh))
                        kh += 1
            for n in range(len(steps) + 2):
                if n < len(steps):
                    steps[n][0]()
                if 0 <= n - 1 < len(steps):
                    steps[n - 1][1]()
                if 0 <= n - 2 < len(steps):
                    steps[n - 2][2]()
            self.out_proj(wout_ap, oT, r_oT)
            P.barrier()
            P.flush()

def _nsa_proj(self, g_ap, win_ap, qg_ap, kg_ap, gb_ap, outs):
    P = self.P
    ones = self.ones
    with ExitStack() as st:
        g_sb, r_g = self.load_small("na_g", g_ap, [128, NCH], F32, st)
        qg, r_qg = self.load_small("na_qg", qg_ap, [128, 1], F32, st)
        kg, r_kg = self.load_small("na_kg", kg_ap, [128, 3], F32, st)
        gb, r_gb = self.load_small("na_gb", gb_ap, [48, 1], F32, st)
        qgs = P.sbuf("na_qgs", [128, 1], F32, st)
        r_qgs = Reg()
        P.op("dve", lambda h: h.tensor_scalar(out=qgs[:], in0=qg[:], scalar1=SCALE, scalar2=None, op0=ALU.mult),
             reads=[r_qg], writes=[r_qgs])
        ngb = P.sbuf("na_ngb", [48, 1], F32, st)
        r_ngb = Reg()
        P.op("dve", lambda h: h.tensor_scalar(out=ngb[:], in0=gb[:], scalar1=-1.0, scalar2=None, op0=ALU.mult),
             reads=[r_gb], writes=[r_ngb])
        xn = P.sbuf("na_xn", [128, NCH, TOK], BF16, st)
        r_xn = [Reg() for _ in range(NCH)]
        self.rmsnorm(g_sb, r_g, xn, r_xn, st)
        stg = [P.sbuf(f"na_stg{i}", [128, 512], BF16, st) for i in range(4)]
        r_stg = [Reg() for _ in range(4)]
        sqb = [P.sbuf(f"na_sq{i}", [128, 512], BF16, st) for i in range(2)]
        r_sqb = [Reg() for _ in range(2)]
        lnv = P.sbuf("na_ln", [128, 512], F32, st)
        r_ln = Reg()
        rstd = [P.sbuf(f"na_rstd{i}", [128, 512], F32, st) for i in range(2)]
        r_rstd = [Reg() for _ in range(2)]
        cnt = [0]
        cn = [0]

        def evac_copy(dst):
            def f(j, th, bank, rb):
                i = cnt[0] % 4
                cnt[0] += 1
                s, rs = stg[i], r_stg[i]
                P.op("act", lambda h: h.activation(out=s[:], in_=bank[:], func=AF.Copy), reads=[rb], writes=[rs])
                P.dma("sp", lambda h: h.dma_start(out=dst.ap()[j, :, th * 512:(th + 1) * 512], in_=s[:]), reads=[rs])
            return f

        def evac_norm(dst, gvec, r_gvec, extra_scale):
            lnscale = float(np.log(extra_scale))

            def f(j, th, bank, rb):
                i = cnt[0] % 4
                cnt[0] += 1
                s, rs = stg[i], r_stg[i]
                k = cn[0] % 2
                cn[0] += 1
                sq, r_sq = sqb[k], r_sqb[k]
                rs_t, r_rs = rstd[k], r_rstd[k]
                sbank, r_sb = self.banks[3 + k], self.r_banks[3 + k]
                P.op("act", lambda h: h.activation(out=sq[:], in_=bank[:], func=AF.Square), reads=[rb], writes=[r_sq])
                P.op("pe", lambda h: h.matmul(sbank[:], ones[:], sq[:], start=True, stop=True),
                     reads=[r_sq, self.r_ones], writes=[r_sb])
                P.op("act", lambda h: h.activation(out=lnv[:], in_=sbank[:], func=AF.Ln, bias=EPS, scale=1.0 / DH),
                     reads=[r_sb], writes=[r_ln])
                P.op("act", lambda h: h.activation(out=rs_t[:], in_=lnv[:], func=AF.Exp, scale=-0.5),
                     reads=[r_ln], writes=[r_rs])
                P.op("dve", lambda h: h.scalar_tensor_tensor(out=s[:], in0=bank[:], scalar=gvec, in1=rs_t[:],
                                                             op0=ALU.mult, op1=ALU.mult),
                     reads=[rb, r_rs, r_gvec], writes=[rs])
                P.dma("sp", lambda h: h.dma_start(out=dst.ap()[j, :, th * 512:(th + 1) * 512], in_=s[:]), reads=[rs])
            return f

        def evac_tm(dst):
            def f(g, tt, bank, rb):
                i = cnt[0] % 4
                cnt[0] += 1
                s, rs = stg[i], r_stg[i]
                P.op("dve", lambda h: h.tensor_copy(out=s[:], in_=bank[:]), reads=[rb], writes=[rs])
                P.dma("sp", lambda h: h.dma_start(out=dst.ap()[tt * 128:(tt + 1) * 128, :], in_=s[:]), reads=[rs])
            return f

        parts = _DBG_PARTS
        if parts is None or "q" in parts:
            self.proj_fm(win_ap[:, 0:2048], xn, r_xn, 2048, evac_norm(outs["qn"], qgs[:, 0:1], r_qgs, 1.0))
        if parts is None or "kc" in parts:
            self.proj_fm(win_ap[:, 2048:2560], xn, r_xn, 512, evac_copy(outs["kc"]))
            self.proj_fm(win_ap[:, 2560:3072], xn, r_xn, 512, evac_copy(outs["vc"]))
        if parts is None or "ks" in parts:
            self.proj_fm(win_ap[:, 3072:3584], xn, r_xn, 512, evac_norm(outs["ks"], kg[:, 1:2], r_kg, 1.0))
            self.proj_fm(win_ap[:, 4096:4608], xn, r_xn, 512, evac_norm(outs["kw"], kg[:, 2:3], r_kg, 1.0))
        if parts is None or "vs" in parts:
            self.proj_tm(win_ap[:, 3584:4096], xn, r_xn, 512, evac_tm(outs["vs"]))
            self.proj_tm(win_ap[:, 4608:5120], xn, r_xn, 512, evac_tm(outs["vw"]))
        if parts is not None and "gates" not in parts:
            P.barrier()
            P.flush()
            return
        wb, rw = self.load_w(win_ap[:, 5120:5168], ncols=48)
        ge = P.sbuf("na_ge", [48, 512], F32, st)
        r_ge = Reg()
        for th in range(2):
            ts = slice(th * 512, (th + 1) * 512)
            bank, rb = self.banks[5 + th], self.r_banks[5 + th]
            for c in range(NCH):
                P.op("pe", lambda h, c=c, ts=ts, bank=bank: h.matmul(
                    bank[0:48, :], wb[:, c, 0:48], xn[:, c, ts], start=(c == 0), stop=(c == NCH - 1)),
                    reads=[rw[c // 8], r_xn[c]], writes=[rb], signal=(c == NCH - 1))
            P.op("act", lambda h, bank=bank: h.activation(out=ge[:], in_=bank[0:48, :], func=AF.Exp, scale=-1.0, bias=ngb[:, 0:1]),
                 reads=[rb, r_ngb], writes=[r_ge])
            P.op("dve", lambda h: h.tensor_scalar(out=ge[:], in0=ge[:], scalar1=1.0, scalar2=None, op0=ALU.add),
                 reads=[r_ge], writes=[r_ge])
            P.op("dve", lambda h: h.reciprocal(out=ge[:], in_=ge[:]), reads=[r_ge], writes=[r_ge])
            P.dma("sp", lambda h, ts=ts: h.dma_start(out=outs["gates"].ap()[:, ts], in_=ge[:]), reads=[r_ge])
        P.barrier()
        P.flush()


Core.nsa_proj = _nsa_proj


def _nsa_compress(self, D):
    P = self.P
    ones, r_ones = self.ones, self.r_ones
    banks, r_banks = self.banks, self.r_banks
    if not hasattr(self, "kcn"):
        self.kcn = P.sbuf("nb_kcn", [128, 4, 256], BF16)
        self.r_kcn = Reg()
        self.vcb = P.sbuf("nb_vcb", [128, 2, 4, 128], BF16)
        self.r_vcb = Reg()
    kcn, r_kcn, vcb, r_vcb = self.kcn, self.r_kcn, self.vcb, self.r_vcb
    with ExitStack() as st:
        kg0, r_kg0 = self.load_small("nb_kg0", D["kg0"].ap(), [128, 1], F32, st)
        P.op("pool", lambda h: h.memset(kcn[:], 0.0), writes=[r_kcn])
        P.op("pool", lambda h: h.memset(vcb[:], 0.0), writes=[r_vcb])
        with ExitStack() as st2:
            xc = P.sbuf("nb_xc", [128, 4, 256, 16], BF16, st2)
            r_xc = Reg()
            w1c = P.sbuf("nb_w1c", [128, 32, 128], BF16, st2)
            r_w1c = Reg()
            w2c = P.sbuf("nb_w2c", [128, 128], BF16, st2)
            r_w2c = Reg()
            peT = P.sbuf("nb_peT", [128, 32], BF16, st2)
            r_peT = Reg()
            bias_sb = P.sbuf("nb_cb", [128, 1], F32, st2)
            r_bias = Reg()
            fa = [P.sbuf(f"nb_f{i}", [128, 256], F32, st2) for i in range(4)]
            r_fa = [Reg() for _ in range(4)]
            H2 = P.sbuf("nb_H2", [128, 256], BF16, st2)
            r_H2 = Reg()
            sq = P.sbuf("nb_csq", [128, 256], BF16, st2)
            r_sq = Reg()
            P.op("pool", lambda h: h.memset(H2[:], 0.0), writes=[r_H2])
            for kv in range(2):
                src = D["KC"] if kv == 0 else D["VC"]
                for g in range(4):
                    P.dma("sp", lambda h, g=g, src=src: h.dma_start(
                        out=xc[:, g, :, :], in_=src.ap()[g].rearrange("d (n r) -> d n r", r=16)), writes=[r_xc])
                P.dma("pool", lambda h, kv=kv: h.dma_start(
                    out=w1c[:], in_=D["cw1"].ap()[kv].rearrange("(l d) h -> d l h", d=128)), writes=[r_w1c])
                P.dma("pool", lambda h, kv=kv: h.dma_start(out=w2c[:], in_=D["cw2"].ap()[kv]), writes=[r_w2c])
                P.dma("pool", lambda h, kv=kv: h.dma_start(out=peT[:], in_=D["peT"].ap()[kv]), writes=[r_peT])
                bb, r_bb = banks[3], r_banks[3]
                for l in range(32):
                    P.op("pe", lambda h, l=l: h.matmul(bb[:, 0:1], w1c[:, l, :], peT[:, l:l + 1], start=(l == 0), stop=(l == 31)),
                         reads=[r_w1c, r_peT], writes=[r_bb], signal=(l == 31))
                P.op("dve", lambda h: h.tensor_copy(out=bias_sb[:], in_=bb[:, 0:1]), reads=[r_bb], writes=[r_bias])
                for g in range(4):
                    hb, r_hb = banks[5 + g % 2], r_banks[5 + g % 2]
                    for l in range(32):
                        n0, rr = (0, l) if l < 16 else (1, l - 16)
                        P.op("pe", lambda h, l=l, g=g, n0=n0, rr=rr, hb=hb: h.matmul(
                            hb[:, 0:255], w1c[:, l, :], xc[:, g, n0:n0 + 255, rr], start=(l == 0), stop=(l == 31)),
                            reads=[r_w1c, r_xc], writes=[r_hb], signal=(l == 31))
                    a, a2, u, th = fa
                    r_a, r_a2, r_u, r_th = r_fa
                    P.op("act", lambda h, hb=hb: h.activation(out=a[:, 0:255], in_=hb[:, 0:255], func=AF.Identity, bias=bias_sb[:, 0:1]),
                         reads=[r_hb, r_bias], writes=[r_a])
                    P.op("dve", lambda h: h.tensor_tensor(out=a2[:, 0:255], in0=a[:, 0:255], in1=a[:, 0:255], op=ALU.mult),
                         reads=[r_a], writes=[r_a2])
                    P.op("dve", lambda h: h.tensor_scalar(out=a2[:, 0:255], in0=a2[:, 0:255], scalar1=0.044715, scalar2=1.0,
                                                          op0=ALU.mult, op1=ALU.add), reads=[r_a2], writes=[r_a2])
                    P.op("dve", lambda h: h.tensor_tensor(out=u[:, 0:255], in0=a2[:, 0:255], in1=a[:, 0:255], op=ALU.mult),
                         reads=[r_a2, r_a], writes=[r_u])
                    P.op("act", lambda h: h.activation(out=th[:, 0:255], in_=u[:, 0:255], func=AF.Tanh, scale=0.7978845608028654),
                         reads=[r_u], writes=[r_th])
                    P.op("dve", lambda h: h.scalar_tensor_tensor(out=H2[:, 0:255], in0=th[:, 0:255], scalar=1.0, in1=a[:, 0:255],
                                                                 op0=ALU.add, op1=ALU.mult), reads=[r_th, r_a], writes=[r_H2])
                    if kv == 0:
                        kb, r_kb = banks[3], r_banks[3]
                        sb_, r_sb = banks[4], r_banks[4]
                        P.op("pe", lambda h: h.matmul(kb[:, 0:255], w2c[:], H2[:, 0:255], start=True, stop=True),
                             reads=[r_w2c, r_H2], writes=[r_kb])
                        P.op("act", lambda h: h.activation(out=a2[:, 0:255], in_=kb[:, 0:255], func=AF.Copy, scale=0.5),
                             reads=[r_kb], writes=[r_a2])
                        P.op("act", lambda h: h.activation(out=sq[:, 0:255], in_=kb[:, 0:255], func=AF.Square, scale=0.5),
                             reads=[r_kb], writes=[r_sq])
                        P.op("pe", lambda h: h.matmul(sb_[:, 0:255], ones[:], sq[:, 0:255], start=True, stop=True),
                             reads=[r_sq, r_ones], writes=[r_sb])
                        P.op("act", lambda h: h.activation(out=u[:, 0:255], in_=sb_[:, 0:255], func=AF.Ln, bias=EPS, scale=1.0 / DH),
                             reads=[r_sb], writes=[r_u])
                        P.op("act", lambda h: h.activation(out=th[:, 0:255], in_=u[:, 0:255], func=AF.Exp, scale=-0.5),
                             reads=[r_u], writes=[r_th])
                        P.op("dve", lambda h, g=g: h.scalar_tensor_tensor(out=kcn[:, g, 0:255], in0=a2[:, 0:255], scalar=kg0[:, 0:1],
                                                                          in1=th[:, 0:255], op0=ALU.mult, op1=ALU.mult),
                             reads=[r_a2, r_th, r_kg0], writes=[r_kcn])
                    else:
                        for nc_ in range(2):
                            M = 128 if nc_ == 0 else 127
                            vbk, r_vbk = banks[3 + nc_], r_banks[3 + nc_]
                            P.op("pe", lambda h, nc_=nc_, M=M, vbk=vbk: h.matmul(
                                vbk[0:M, 0:128], H2[:, nc_ * 128:nc_ * 128 + M], w2c[:], start=True, stop=True),
                                reads=[r_w2c, r_H2], writes=[r_vbk])
                            P.op("act", lambda h, nc_=nc_, M=M, g=g, vbk=vbk: h.activation(
                                out=vcb[0:M, nc_, g, :], in_=vbk[0:M, 0:128], func=AF.Copy, scale=0.5),
                                reads=[r_vbk], writes=[r_vcb])
            P.barrier()
            P.flush()


Core.nsa_compress = _nsa_compress


def _nsa_attn(self, ck, D, wout_ap):
    P = self.P
    ones, ident = self.ones, self.ident
    r_ones, r_ident = self.r_ones, self.r_ident
    banks, r_banks = self.banks, self.r_banks
    xT = self.xT
    with ExitStack() as st:
        L = lambda n, ap, sh, dt: self.load_small("nb_" + n, ap, sh, dt, st)
        aL, r_aL = L("aL", D["alibiL"].ap(), [6, HEADS, 128], BF16)
        aR, r_aR = L("aR", D["alibiR"].ap()[ck], [6, TOK], BF16)
        bKI, r_bKI = L("bKI", D["biasKI"].ap(), [128, HEADS, 32], F32)
        bC, r_bC = L("bC", D["biasC"].ap(), [128, HEADS, 2], F32)
        nmc, r_nmc = L("nmc", D["negmask_c"].ap()[ck], [128, 2, TOK], BF16)
        ovl, r_ovl = L("ovl", D["overlap"].ap(), [128, 2, 64], BF16)
        keep, r_keep = L("keep", D["keep01"].ap()[ck], [128, NQB, 64], F32)
        addc, r_addc = L("addc", D["addc"].ap()[ck], [128, NQB, 64], F32)
        lcomb = [P.sbuf(f"nb_lcomb{i}", [70, 32, 128], BF16, st) for i in range(1)]
        r_lcomb = [Reg() for _ in range(1)]
        for i_ in range(1):
            P.dma("sp", lambda h, i_=i_: h.dma_start(out=lcomb[i_][0:64], in_=D["eexp"].ap()), writes=[r_lcomb[i_]])
        nmi, r_nmi = L("nmi", D["nm_incl"].ap(), [128, 128], BF16)
        nma, r_nma = L("nma", D["nm_after"].ap(), [128, 128], BF16)
        nk = 8 * ck + 8
        sel3, r_sel3 = L("sel3", D["sel3"].ap(), [3, 3, 128], F32)
        gat = P.sbuf("nb_gat", [3, 512], F32, st)
        r_gat = Reg()
        gates_v = D["gates"].ap().rearrange("(r h) t -> r h t", r=3)

        ghi = P.sbuf("nb_ghi", [3, 512], BF16, st)
        glo = P.sbuf("nb_glo", [3, 512], BF16, st)
        r_gh = Reg()
        sel3b = P.sbuf("nb_sel3b", [3, 3, 128], BF16, st)
        r_sel3b = Reg()
        P.op("dve", lambda h: h.tensor_copy(out=sel3b[:], in_=sel3[:]), reads=[r_sel3], writes=[r_sel3b])

        def load_gate(h_, half):
            P.dma("sp", lambda h: h.dma_start(out=gat[:], in_=gates_v[:, h_, half * 512:(half + 1) * 512]), writes=[r_gat])
            P.op("dve", lambda h: h.tensor_copy(out=ghi[:], in_=gat[:]), reads=[r_gat], writes=[r_gh])
            P.op("dve", lambda h: h.tensor_tensor(out=glo[:], in0=gat[:], in1=ghi[:], op=ALU.subtract), reads=[r_gat, r_gh], writes=[r_gh])

        kcn, r_kcn, vcb, r_vcb = self.kcn, self.r_kcn, self.vcb, self.r_vcb

        ksb = P.sbuf("nb_ksb", [128, SEQ], BF16, st)
        r_ksb = Reg()
        vsb = P.sbuf("nb_vsb", [128, 32, 128], BF16, st)
        r_vsb = Reg()
        qg4 = P.sbuf("nb_q4", [128, 4, TOK], BF16, st)
        r_qg4 = Reg()
        acc = P.sbuf("nb_acc", [128, 4, TOK], F32, st)
        r_acc = [Reg() for _ in range(4)]
        ob16 = [P.sbuf(f"nb_ob{i}", [128, TOK], BF16, st) for i in range(4)]
        r_ob16 = [Reg() for _ in range(4)]
        Pc = [P.sbuf(f"nb_Pc{i}", [128, 512], BF16, st) for i in range(4)]
        r_Pc = [Reg() for _ in range(4)]
        Pn = [P.sbuf(f"nb_Pn{i}", [128, 512], BF16, st) for i in range(2)]
        r_Pn = [Reg() for _ in range(2)]
        PS = [P.sbuf(f"nb_PS{i}", [128, 512], BF16, st) for i in range(4)]
        r_PS = [Reg() for _ in range(4)]
        rden = P.sbuf("nb_rden", [128, 512], F32, st)
        r_rden = Reg()
        Gs = P.sbuf("nb_Gs", [128, 512], F32, st)
        r_Gs = Reg()
        Osb = [P.sbuf(f"nb_osb{i}", [128, 512], F32, st) for i in range(2)]
        r_Osb = [Reg() for _ in range(2)]
        Pacc = [P.sbuf(f"nb_pacc{i}", [128, 512], F32, st) for i in range(2)]
        r_Pacc = [Reg() for _ in range(2)]
        ones32 = P.sbuf("nb_ones32", [128, 128], F32, st)
        r_ones32 = Reg()
        P.op("pool", lambda h: h.memset(ones32[:], 1.0), writes=[r_ones32])
        impf = P.sbuf("nb_impf", [128, NQB, 64], F32, st)
        r_impf = Reg()
        wrk = P.sbuf("nb_wrk", [128, 64], F32, st)
        r_wrk = Reg()
        m8 = P.sbuf("nb_m8", [128, 8], F32, st)
        r_m8 = Reg()
        s01 = P.sbuf("nb_s01", [128, 64], F32, st)
        r_s01 = Reg()
        nsel = P.sbuf("nb_nsel", [128, 64], BF16, st)
        r_nsel = Reg()
        nselT = P.sbuf("nb_nselT", [70, TOK], BF16, st)
        r_nselT = Reg()
        P.dma("sp", lambda h: h.dma_start(out=nselT[64:70, :], in_=D["alibiR"].ap()[ck]), writes=[r_nselT])
        tpsum = self.tbank[0:64, 0:128]
        r_tps = self.r_tbank
        nA = [0]
        nPS = [0]

        def next_A():
            k = nA[0] % 3
            nA[0] += 1
            return banks[k], r_banks[k]

        nEp = [0]

        def gate_epilogue(hl, h_, branch, cols, obank, r_ob, dbank, r_db, first_branch):
            gb, r_gb = banks[5], r_banks[5]
            osb, r_osb = Osb[nEp[0] % 2], r_Osb[nEp[0] % 2]
            nEp[0] += 1
            P.op("act", lambda h: h.activation(out=osb[:], in_=obank[:], func=AF.Copy), reads=[r_ob], writes=[r_osb])
            if dbank is not None:
                P.op("dve", lambda h: h.tensor_scalar(out=rden[:], in0=dbank[:], scalar1=1e-30, scalar2=None, op0=ALU.max),
                     reads=[r_db], writes=[r_rden])
            P.op("pe", lambda h: h.matmul(gb[:], sel3b[:, branch, :], ghi[:], start=True, stop=False),
                 reads=[r_sel3b, r_gh], writes=[r_gb])
            P.op("pe", lambda h: h.matmul(gb[:], sel3b[:, branch, :], glo[:], start=False, stop=True),
                 reads=[r_sel3b, r_gh], writes=[r_gb])
            P.op("act", lambda h: h.activation(out=Gs[:], in_=gb[:], func=AF.Copy), reads=[r_gb], writes=[r_Gs])
            if dbank is not None:
                P.op("act", lambda h: h.activation(out=rden[:], in_=rden[:], func=AF.Ln), reads=[r_rden], writes=[r_rden])
                P.op("act", lambda h: h.activation(out=rden[:], in_=rden[:], func=AF.Exp, scale=-1.0), reads=[r_rden], writes=[r_rden])
                P.op("dve", lambda h: h.tensor_tensor(out=rden[:], in0=rden[:], in1=Gs[:], op=ALU.mult),
                     reads=[r_rden, r_Gs], writes=[r_rden])
                mul = rden
                r_mul = r_rden
            else:
                mul = Gs
                r_mul = r_Gs
            if first_branch:
                P.op("dve", lambda h: h.tensor_tensor(out=acc[:, hl, cols], in0=osb[:], in1=mul[:], op=ALU.mult),
                     reads=[r_osb, r_mul], writes=[r_acc[hl]])
            else:
                P.op("dve", lambda h: h.tensor_tensor(out=osb[:], in0=osb[:], in1=mul[:], op=ALU.mult),
                     reads=[r_osb, r_mul], writes=[r_osb])
                P.op("dve", lambda h: h.tensor_tensor(out=acc[:, hl, cols], in0=acc[:, hl, cols], in1=osb[:], op=ALU.add),
                     reads=[r_osb], writes=[r_acc[hl]])

        db, r_db = banks[3], r_banks[3]
        ob, r_ob = banks[4], r_banks[4]

        def mk_step(branch, hl, h_, grp, p0, b0, i, first, last, n):
            smin = max(0, i - b0)
            if branch == 1:
                smax = 3
                masks = [(smin, nmi, r_nmi)] if i >= b0 else []
            else:
                smax = min(3, i - b0 + 4)
                masks = []
                for s_ in range(smin, smax + 1):
                    rr_ = b0 + s_ - i
                    if rr_ == 0:
                        masks.append((s_, nmi, r_nmi))
                    elif rr_ == 4:
                        masks.append((s_, nma, r_nma))
            c_lo = smin * 128
            c_hi = (smax + 1) * 128
            cols = slice(c_lo, c_hi)
            gcols = slice(p0 * 128 + c_lo, p0 * 128 + c_hi)
            gcols_all = slice(p0 * 128, p0 * 128 + 512)
            ABK = (0, 1, 2, 6)
            ab, r_ab = banks[ABK[n % 4]], r_banks[ABK[n % 4]]
            ps, r_ps = PS[n % 4], r_PS[n % 4]
            lc, r_lc = lcomb[0], r_lcomb[0]
            pacc, r_pacc = Pacc[(2 * hl + grp) % 2], r_Pacc[(2 * hl + grp) % 2]

            def A():
                if first:
                    P.op("pool", lambda h: h.memset(pacc[:], 0.0), writes=[r_pacc])
                if branch == 1 and first and grp == 0:
                    P.dma("sp", lambda h: h.dma_start(out=lc[64:70], in_=D["alibiLx"].ap()[h_]), writes=[r_lc])
                P.op("pe", lambda h: h.matmul(ab[:, cols], ksb[:, i * 128:(i + 1) * 128], qg4[:, hl, gcols], start=True, stop=True),
                     reads=[r_ksb, r_qg4], writes=[r_ab])
                if branch == 1:
                    P.op("pe", lambda h: h.matmul(ab[:, cols], lc[:, i, :], nselT[:, gcols], start=False, stop=True, skip_group_check=True),
                         reads=[r_lc, r_nselT], writes=[r_ab])
                else:
                    P.op("pe", lambda h: h.matmul(ab[:, cols], aL[:, h_, :], aR[:, gcols], start=False, stop=True, skip_group_check=True),
                         reads=[r_aL, r_aR], writes=[r_ab])
                for (s_, mt, r_mt) in masks:
                    tcols = slice(s_ * 128, (s_ + 1) * 128)
                    P.op("pe", lambda h, tcols=tcols, mt=mt: h.matmul(ab[:, tcols], ident[:], mt[:], start=False, stop=True, skip_group_check=True),
                         reads=[r_mt, r_ident], writes=[r_ab])
                P.op("act", lambda h: h.activation(out=ps[:, cols], in_=ab[:, cols], func=AF.Exp, bias=bKI[:, h_, i:i + 1]),
                     reads=[r_ab, r_bKI], writes=[r_ps])

            def B():
                P.op("dve", lambda h: h.tensor_tensor(out=pacc[:, cols], in0=pacc[:, cols], in1=ps[:, cols], op=ALU.add),
                     reads=[r_ps], writes=[r_pacc])
                P.op("pe", lambda h: h.matmul(ob[:, cols], vsb[:, i, :], ps[:, cols], start=first, stop=True, skip_group_check=True),
                     reads=[r_ps, r_vsb], writes=[r_ob])
                if last:
                    P.op("pe", lambda h: h.matmul(db[:], ones32[:], pacc[:], start=True, stop=True),
                         reads=[r_pacc, r_ones32], writes=[r_db])
                    load_gate(h_, grp)
                    gate_epilogue(hl, h_, branch, gcols_all, ob, r_ob, db, r_db, False)
            return A, B

        for g in range(4):
            hk = nk // 2
            P.dma("sp", lambda h, g=g: h.dma_start(out=ksb[:, 0:nk * 128], in_=D["KS"].ap()[g][:, 0:nk * 128]), writes=[r_ksb])
            for half in range(2):
                P.dma("sp", lambda h, g=g, half=half: h.dma_start(
                    out=vsb[:, half * hk:(half + 1) * hk, :],
                    in_=D["VS"].ap()[half * hk * 128:(half + 1) * hk * 128, g * 128:(g + 1) * 128].rearrange("(i k) d -> k i d", k=128)),
                    writes=[r_vsb])
            P.dma("sp", lambda h, g=g: h.dma_start(out=qg4[:], in_=D["qn"].ap()[4 * g:4 * g + 4].rearrange("h d t -> d h t")),
                  writes=[r_qg4])
            ib, r_ib = banks[6], r_banks[6]

            def mk_cstep(hl, h_, qg, n, g=g):
                cols = slice(qg * 512, (qg + 1) * 512)
                cdb, r_cdb = banks[2 + n % 2], r_banks[2 + n % 2]
                cob, r_cob = banks[4], r_banks[4]
                pcs = [(Pc[2 * (n % 2) + k_], r_Pc[2 * (n % 2) + k_]) for k_ in range(2)]

                def S1():
                    for nc_ in range(2):
                        M = 128 if nc_ == 0 else 127
                        ab, r_ab = banks[nc_], r_banks[nc_]
                        pc, r_pc = pcs[nc_]
                        P.op("pe", lambda h, ab=ab, nc_=nc_, M=M: h.matmul(
                            ab[0:M, :], kcn[:, g, nc_ * 128:nc_ * 128 + M], qg4[:, hl, cols], start=True, stop=True),
                            reads=[r_kcn, r_qg4], writes=[r_ab])
                        P.op("pe", lambda h, ab=ab, M=M: h.matmul(
                            ab[0:M, :], aL[:, h_, 0:M], aR[:, cols], start=False, stop=True, skip_group_check=True),
                            reads=[r_aL, r_aR], writes=[r_ab])
                        P.op("pe", lambda h, ab=ab, M=M, nc_=nc_: h.matmul(
                            ab[0:M, :], ident[0:M, 0:M], nmc[0:M, nc_, cols], start=False, stop=True, skip_group_check=True),
                            reads=[r_nmc, r_ident], writes=[r_ab])
                        P.op("act", lambda h, ab=ab, M=M, pc=pc, nc_=nc_: h.activation(
                            out=pc[0:M, :], in_=ab[0:M, :], func=AF.Exp, bias=bC[0:M, h_, nc_:nc_ + 1]),
                            reads=[r_ab, r_bC], writes=[r_pc])
                        P.op("pe", lambda h, M=M, pc=pc, nc_=nc_: h.matmul(
                            cdb[:], ones[0:M, :], pc[0:M, :], start=(nc_ == 0), stop=(nc_ == 1)),
                            reads=[r_pc, r_ones], writes=[r_cdb])

                def S2():
                    load_gate(h_, qg)
                    P.op("dve", lambda h: h.tensor_scalar(out=rden[:], in0=cdb[:], scalar1=1e-30, scalar2=None, op0=ALU.max),
                         reads=[r_cdb], writes=[r_rden])
                    P.op("dve", lambda h: h.reciprocal(out=rden[:], in_=rden[:]), reads=[r_rden], writes=[r_rden])
                    for nc_ in range(2):
                        M = 128 if nc_ == 0 else 127
                        pc, r_pc = pcs[nc_]
                        P.op("dve", lambda h, nc_=nc_, M=M, pc=pc: h.tensor_tensor(
                            out=Pn[nc_][0:M, :], in0=pc[0:M, :], in1=rden[0:M, :], op=ALU.mult),
                            reads=[r_pc, r_rden], writes=[r_Pn[nc_]])
                    for nc_ in range(2):
                        M = 128 if nc_ == 0 else 127
                        P.op("pe", lambda h, nc_=nc_, M=M: h.matmul(
                            cob[:], vcb[0:M, nc_, g, :], Pn[nc_][0:M, :], start=(nc_ == 0), stop=(nc_ == 1)),
                            reads=[r_vcb, r_Pn[nc_]], writes=[r_cob])
                    for qb in range(4):
                        for nc_ in range(2):
                            M = 128 if nc_ == 0 else 127
                            fi = (n == 0 and qb == 0 and nc_ == 0)
                            P.op("pe", lambda h, qb=qb, nc_=nc_, M=M, fi=fi: h.matmul(
                                ib[:, (qg * 4 + qb) * 64:(qg * 4 + qb + 1) * 64], Pn[nc_][0:M, qb * 128:(qb + 1) * 128],
                                ovl[0:M, nc_, :], start=fi, stop=True, skip_group_check=True),
                                reads=[r_Pn[nc_], r_ovl], writes=[r_ib])
                    gate_epilogue(hl, h_, 0, cols, cob, r_cob, None, None, True)
                return S1, S2

            csteps = []
            for hl in range(4):
                for qg in range(2):
                    csteps.append(mk_cstep(hl, 4 * g + hl, qg, len(csteps)))
            csteps[0][0]()
            for n in range(len(csteps)):
                if n + 1 < len(csteps):
                    csteps[n + 1][0]()
                csteps[n][1]()
            ibv = ib[:].rearrange("p (q j) -> p q j", j=64)
            P.op("dve", lambda h: h.tensor_tensor(out=impf[:], in0=ibv, in1=keep[:], op=ALU.mult),
                 reads=[r_ib, r_keep], writes=[r_impf])
            P.op("dve", lambda h: h.tensor_tensor(out=impf[:], in0=impf[:], in1=addc[:], op=ALU.add),
                 reads=[r_impf, r_addc], writes=[r_impf])
            for qbi in range(NQB):
                P.op("dve", lambda h, qbi=qbi: h.max(out=m8[:], in_=impf[:, qbi, :]), reads=[r_impf], writes=[r_m8])
                P.op("dve", lambda h, qbi=qbi: h.match_replace(out=wrk[:], in_to_replace=m8[:], in_values=impf[:, qbi, :],
                                                               imm_value=-1e30), reads=[r_m8, r_impf], writes=[r_wrk])
                P.op("dve", lambda h: h.max(out=m8[:], in_=wrk[:]), reads=[r_wrk], writes=[r_m8])
                P.op("dve", lambda h: h.match_replace(out=s01[:], in_to_replace=m8[:], in_values=wrk[:], imm_value=-1e30),
                     reads=[r_m8, r_wrk], writes=[r_s01])
                P.op("dve", lambda h: h.tensor_scalar(out=nsel[:], in0=s01[:], scalar1=-1e29, scalar2=NEG,
                                                      op0=ALU.is_gt, op1=ALU.mult), reads=[r_s01], writes=[r_nsel])
                P.op("pe", lambda h: h.transpose(tpsum, nsel[:], ident[:]), reads=[r_nsel, r_ident], writes=[r_tps])
                P.op("act", lambda h, qbi=qbi: h.activation(out=nselT[0:64, qbi * 128:(qbi + 1) * 128], in_=tpsum, func=AF.Copy),
                     reads=[r_tps], writes=[r_nselT])
            for branch in (1, 2):
                if branch == 2:
                    P.dma("sp", lambda h, g=g: h.dma_start(out=ksb[:, 0:nk * 128], in_=D["KW"].ap()[g][:, 0:nk * 128]), writes=[r_ksb])
                    for half in range(2):
                        P.dma("sp", lambda h, g=g, half=half: h.dma_start(
                            out=vsb[:, half * hk:(half + 1) * hk, :],
                            in_=D["VW"].ap()[half * hk * 128:(half + 1) * hk * 128, g * 128:(g + 1) * 128].rearrange("(i k) d -> k i d", k=128)),
                            writes=[r_vsb])
                steps = []
                for hl in range(4):
                    h_ = 4 * g + hl
                    for grp in range(2):
                        p0 = grp * 4
                        b0 = 8 * ck + 4 * grp
                        itop = b0 + 3
                        ilow = 0 if branch == 1 else max(0, b0 - 4)
                        for i in range(itop, ilow - 1, -1):
                            steps.append(mk_step(branch, hl, h_, grp, p0, b0, i, i == itop, i == ilow, len(steps)))
                for n in range(len(steps) + 2):
                    if n < len(steps):
                        steps[n][0]()
                    if 0 <= n - 2 < len(steps):
                        steps[n - 2][1]()
            for hl in range(4):
                P.op("act", lambda h, hl=hl: h.activation(out=ob16[hl][:], in_=acc[:, hl, :], func=AF.Copy),
                     reads=[r_acc[hl]], writes=[r_ob16[hl]])
            nb = 0
            for dg in range(4):
                wb, rw = self.load_w(wout_ap[g * 512:(g + 1) * 512, dg * 512:(dg + 1) * 512], nch=4)
                for dc in range(4):
                    for th in range(2):
                        ts = slice(th * 512, (th + 1) * 512)
                        bank, rb = banks[5 + nb % 2], r_banks[5 + nb % 2]
                        nb += 1
                        for c in range(4):
                            P.op("pe", lambda h, wb=wb, c=c, dc=dc, ts=ts, bank=bank: h.matmul(
                                bank[:], wb[:, c, dc * 128:(dc + 1) * 128], ob16[c][:, ts], start=(c == 0), stop=(c == 3)),
                                reads=rw + [r_ob16[c]], writes=[rb], signal=(c == 3))
                        xc_ = dg * 4 + dc
                        P.op("dve", lambda h, xc_=xc_, ts=ts, bank=bank: h.tensor_tensor(
                            out=xT[:, xc_, ts], in0=xT[:, xc_, ts], in1=bank[:], op=ALU.add),
                            reads=[rb], writes=[self.r_xT[xc_]])
        P.barrier()
        P.flush()


Core.nsa_attn = _nsa_attn


import ml_dtypes
BF = ml_dtypes.bfloat16
NCK = SEQ // TOK


class View:
    def __init__(self, ap):
        self._ap = ap

    def ap(self):
        return self._ap


def _pc(v):
    return np.ascontiguousarray(np.asarray(v, np.float32).reshape(NCH, 128).T)


def _split3(v):
    v = np.float32(v)
    hi = np.float32(BF(v))
    mid = np.float32(BF(np.float32(v - hi)))
    lo = np.float32(BF(np.float32(v - hi - mid)))
    return hi, mid, lo


def _consts():
    k = np.arange(128)
    kk, tt = k[:, None], k[None, :]
    tri = (kk < tt).astype(np.float32)
    out = {"ident": np.eye(128, dtype=np.float32), "negtri": -(kk >= tt).astype(np.float32),
           "negones": np.full((128, 128), -1.0, np.float32), "m01": tri, "nm": (tri - 1.0) * 30000.0,
           "nm_incl": np.where(kk <= tt, 0.0, NEG), "nm_after": np.where(kk > tt, 0.0, NEG)}
    out = {n: v.astype(BF) for n, v in out.items()}
    slopes = np.exp2(-8.0 * np.arange(1, HEADS + 1, dtype=np.float32) / HEADS).astype(np.float32)
    tok = np.arange(SEQ)
    aR = np.zeros((6, SEQ), np.float32)
    aR[0:3] = -128.0 * (tok // 128)
    aR[3:6] = -1.0 * (tok % 128)
    aL = np.zeros((6, HEADS, 128), np.float32)
    for h in range(HEADS):
        sp = _split3(slopes[h])
        for r in range(6):
            aL[r, h, :] = sp[r % 3]
    s64 = slopes.astype(np.float64)
    bKI = (s64[None, :, None] * (128.0 * np.arange(32)[None, None, :] + k[:, None, None])).astype(np.float32)
    bC = (s64[None, :, None] * (16.0 * (128.0 * np.arange(2)[None, None, :] + k[:, None, None]) + 31.0)).astype(np.float32)
    n = (128 * np.arange(2)[None, :] + k[:, None])
    valid_c = (n[:, :, None] <= 254) & (tok[None, None, :] >= 16 * n[:, :, None] + 31)
    nmc = np.where(valid_c, 0.0, NEG).astype(np.float32)
    nn = np.arange(256)
    jj = np.arange(64)
    ov = np.clip(np.minimum(16 * nn[:, None] + 32, 64 * jj[None, :] + 64) - np.maximum(16 * nn[:, None], 64 * jj[None, :]), 0, None) / 32.0
    ov[255] = 0.0
    ovl = np.ascontiguousarray(ov.reshape(2, 128, 64).transpose(1, 0, 2)).astype(np.float32)
    tq = tok.reshape(NCK, NQB, 128).transpose(0, 2, 1)
    cur = tq // 64
    J = jj[None, None, None, :]
    fut = J > cur[..., None]
    forced = ((J == 0) | (J == cur[..., None]) | (J == cur[..., None] - 1)) & ~fut
    keep = (~(forced | fut)).astype(np.float32)
    addc = np.where(fut, -1.0 - 1e-3 * J, np.where(forced, 1e4 + (64.0 - J), 0.0)).astype(np.float32)
    eexp = ((128 * np.arange(32)[None, :, None] + k[None, None, :]) // 64 == jj[:, None, None]).astype(np.float32)
    sel3 = np.zeros((3, 3, 128), np.float32)
    for r in range(3):
        sel3[r, r, :] = 1.0
    aLx = np.ascontiguousarray(np.broadcast_to(aL.transpose(1, 0, 2)[:, :, None, :], (HEADS, 6, 32, 128)))
    out.update({"alibiL": aL.astype(BF), "alibiLx": aLx.astype(BF),
                "alibiR": np.ascontiguousarray(aR.reshape(6, NCK, TOK).transpose(1, 0, 2)).astype(BF),
                "biasKI": bKI, "biasC": bC,
                "negmask_c": np.ascontiguousarray(nmc.reshape(128, 2, NCK, TOK).transpose(2, 0, 1, 3)).astype(BF),
                "overlap": ovl.astype(BF), "keep01": np.ascontiguousarray(keep), "addc": np.ascontiguousarray(addc),
                "eexp": eexp.astype(BF), "sel3": sel3})
    return out


_CONST_SPECS = (("ident", [128, 128], BF16), ("negtri", [128, 128], BF16), ("negones", [128, 128], BF16),
                ("m01", [128, 128], BF16), ("nm", [128, 128], BF16), ("nm_incl", [128, 128], BF16),
                ("nm_after", [128, 128], BF16), ("alibiL", [6, HEADS, 128], BF16), ("alibiLx", [HEADS, 6, 32, 128], BF16), ("alibiR", [NCK, 6, TOK], BF16),
                ("biasKI", [128, HEADS, 32], F32), ("biasC", [128, HEADS, 2], F32),
                ("negmask_c", [NCK, 128, 2, TOK], BF16), ("overlap", [128, 2, 64], BF16),
                ("keep01", [NCK, 128, NQB, 64], F32), ("addc", [NCK, 128, NQB, 64], F32),
                ("eexp", [64, 32, 128], BF16), ("sel3", [3, 3, 128], F32))
_W_SPECS = (("sb_g", [2, 128, NCH]), ("sb_wqkv", [2, D_MODEL, 3 * D_MODEL]), ("sb_wout", [2, D_MODEL, D_MODEL]),
            ("nsa_g", [2, 128, NCH]), ("nsa_win", [2, D_MODEL, 5168]), ("nsa_qg", [2, 128, 1]), ("nsa_kg", [2, 128, 3]),
            ("nsa_gb", [2, 48, 1]), ("cw1", [2, 2, 4096, 128]), ("cw2", [2, 2, 128, 128]), ("peT", [2, 2, 128, 32]),
            ("kg0", [2, 128, 1]), ("nsa_wout", [2, D_MODEL, D_MODEL]), ("mlp_g", [4, 128, NCH]),
            ("mlp_w1", [4, D_MODEL, D_FF]), ("mlp_w2", [4, D_FF, D_MODEL]))

_DEBUG_LAYERS = None
_DBG_PARTS = None


def _build_fused(C, depth):
    P = C.P
    xin = C.din("xin", [NCK, D_MODEL, TOK])
    K = {n: C.din(n, sh, dt) for n, sh, dt in _CONST_SPECS}
    W = {n: C.din(n, sh, F32) for n, sh in _W_SPECS}
    yout = C.dout("yout", [NCK, D_MODEL, TOK])
    xs = C.dscratch("xs", [NCK, D_MODEL, TOK])
    qs = C.dscratch("qs", [NCK, HEADS, 128, TOK], BF16)
    KT = C.dscratch("KT", [HEADS, 128, SEQ], BF16)
    Vv = C.dscratch("Vv", [SEQ, D_MODEL], BF16)
    kv = {n: C.dscratch(n, [4, 128, SEQ], BF16) for n in ("KC", "VC", "KS", "KW")}
    kv.update({n: C.dscratch(n, [SEQ, 512], BF16) for n in ("VS", "VW")})
    gts = C.dscratch("gts", [NCK, 48, TOK], F32)
    C.init_consts(K["ident"])
    for layer in range(depth):
        slot = layer // 2
        src = xin if layer == 0 else xs
        dst = yout if layer == depth - 1 else xs
        if layer % 2 == 0:
            for ck in range(NCK):
                tsl = slice(ck * TOK, (ck + 1) * TOK)
                C.load_x(src.ap()[ck])
                C.sb_proj(W["sb_g"].ap()[slot], W["sb_wqkv"].ap()[slot], View(qs.ap()[ck]), View(KT.ap()[:, :, tsl]),
                          View(Vv.ap()[tsl, :]))
            for ck in range(NCK):
                C.load_x(src.ap()[ck])
                C.sb_attn(ck, View(qs.ap()[ck]), KT, Vv, K, W["sb_wout"].ap()[slot])
                C.mlp(W["mlp_g"].ap()[layer], W["mlp_w1"].ap()[layer], W["mlp_w2"].ap()[layer])
                C.store_x(dst.ap()[ck])
                P.barrier()
                P.flush()
        else:
            for ck in range(NCK):
                tsl = slice(ck * TOK, (ck + 1) * TOK)
                C.load_x(src.ap()[ck])
                outs = {"qn": View(qs.ap()[ck]), "gates": View(gts.ap()[ck]),
                        "kc": View(kv["KC"].ap()[:, :, tsl]), "vc": View(kv["VC"].ap()[:, :, tsl]),
                        "ks": View(kv["KS"].ap()[:, :, tsl]), "kw": View(kv["KW"].ap()[:, :, tsl]),
                        "vs": View(kv["VS"].ap()[tsl, :]), "vw": View(kv["VW"].ap()[tsl, :])}
                C.nsa_proj(W["nsa_g"].ap()[slot], W["nsa_win"].ap()[slot], W["nsa_qg"].ap()[slot], W["nsa_kg"].ap()[slot],
                           W["nsa_gb"].ap()[slot], outs)
            D = dict(K)
            D.update(kv)
            D["cw1"] = View(W["cw1"].ap()[slot])
            D["cw2"] = View(W["cw2"].ap()[slot])
            D["peT"] = View(W["peT"].ap()[slot])
            D["kg0"] = View(W["kg0"].ap()[slot])
            C.nsa_compress(D)
            for ck in range(NCK):
                C.load_x(src.ap()[ck])
                D["qn"] = View(qs.ap()[ck])
                D["gates"] = View(gts.ap()[ck])
                C.nsa_attn(ck, D, W["nsa_wout"].ap()[slot])
                C.mlp(W["mlp_g"].ap()[layer], W["mlp_w1"].ap()[layer], W["mlp_w2"].ap()[layer])
                C.store_x(dst.ap()[ck])
                P.barrier()
                P.flush()


_PROGS = {}


def _prog(depth):
    if depth not in _PROGS:
        nc = bass.Bass("TRN2", target_bir_lowering=False)
        C = Core(nc)
        _build_fused(C, depth)
        C.P.barrier()
        C.P.flush()
        C.P.close()
        _PROGS[depth] = nc
    return _PROGS[depth]


def kernel(x, sb_norm_g, sb_w_qkv, sb_w_out, nsa_norm_g, nsa_w_in, nsa_gate_b, nsa_q_norm_g, nsa_k_norm_g,
           nsa_cmp_pe, nsa_cmp_w1, nsa_cmp_w2, nsa_w_out, mlp_norm_g, mlp_w1, mlp_w2):
    f32 = lambda a: np.ascontiguousarray(np.asarray(a, np.float32))
    x = f32(x)
    depth = 4 if _DEBUG_LAYERS is None else _DEBUG_LAYERS
    shared = dict(_consts())
    kgT = f32(np.asarray(nsa_k_norm_g).transpose(0, 2, 1))
    shared.update({
        "sb_g": np.stack([_pc(g) for g in np.asarray(sb_norm_g)]), "sb_wqkv": f32(sb_w_qkv), "sb_wout": f32(sb_w_out),
        "nsa_g": np.stack([_pc(g) for g in np.asarray(nsa_norm_g)]), "nsa_win": f32(nsa_w_in),
        "nsa_qg": f32(nsa_q_norm_g).reshape(2, 128, 1), "nsa_kg": kgT, "nsa_gb": f32(nsa_gate_b).reshape(2, 48, 1),
        "cw1": f32(nsa_cmp_w1), "cw2": f32(nsa_cmp_w2), "peT": f32(np.asarray(nsa_cmp_pe).transpose(0, 1, 3, 2)),
        "kg0": f32(kgT[:, :, 0:1]), "nsa_wout": f32(nsa_w_out),
        "mlp_g": np.stack([_pc(g) for g in np.asarray(mlp_norm_g)]), "mlp_w1": f32(mlp_w1), "mlp_w2": f32(mlp_w2)})
    xin = [np.ascontiguousarray(x[b].reshape(NCK, TOK, D_MODEL).transpose(0, 2, 1)) for b in range(BATCH)]
    in_maps = []
    for c in range(NCORES):
        m = dict(shared)
        m["xin"] = xin[c // 4]
        in_maps.append(m)
    nc = _prog(depth)
    res = run_bass_kernel_spmd(nc, in_maps, core_ids=list(range(NCORES))).results
    out = np.empty((BATCH, SEQ, D_MODEL), np.float32)
    for b in range(BATCH):
        out[b] = res[4 * b]["yout"].transpose(0, 2, 1).reshape(SEQ, D_MODEL)
    return out
```
